# Optimizing a Trainium2 kernel written in Bass

```python
import functools
import jax, jax.numpy as jnp
from jax import lax
import numpy as np

D_MODEL = 2048
BATCH = 4
SEQ = 2048
DEPTH = 1
DEC_BATCH = 32
DEC_SEQ = 1
PAST_LEN = 16384
PAGE_SIZE = 128

C_CONV = D_MODEL // 2
CONV_WIDTH = 31
HEAD_DIM = D_MODEL // 16
N_KV = 4
DIL_GROUPS = ((128, 1), (512, 4), (2048, 16))
N_GROUPS = len(DIL_GROUPS)
N_QH = N_GROUPS * N_KV
WINDOW_MAX = 2048
N_MEM = 256
N_XHEADS = 4
X_HEAD_DIM = D_MODEL // 16
D_FF = -(-8 * D_MODEL // (3 * 256)) * 256
EPS = 1e-6
NEG = -1e30
SCALE = HEAD_DIM ** -0.5
X_SCALE = X_HEAD_DIM ** -0.5

W_CONV_IN = 2 * C_CONV
W_QD = N_QH * HEAD_DIM
W_KV = N_KV * HEAD_DIM
W_QX = N_XHEADS * X_HEAD_DIM
W_GATE = 3 * D_MODEL
N_IN = W_CONV_IN + W_QD + 2 * W_KV + W_QX + W_GATE
SPLITS = (W_CONV_IN, W_CONV_IN + W_QD, W_CONV_IN + W_QD + W_KV,
          W_CONV_IN + W_QD + 2 * W_KV, W_CONV_IN + W_QD + 2 * W_KV + W_QX)

kernel_name = 'hybrid_conv_dilated_mem_decoder_step'


def _rmsnorm(x, g):
    xf = x.astype(jnp.float32)
    y = xf * lax.rsqrt(jnp.mean(xf * xf, axis=-1, keepdims=True) + EPS)
    return (y * g.astype(jnp.float32)).astype(x.dtype)


def _layernorm(x, g, b):
    xf = x.astype(jnp.float32)
    mu = jnp.mean(xf, axis=-1, keepdims=True)
    var = jnp.mean(jnp.square(xf - mu), axis=-1, keepdims=True)
    y = (xf - mu) * lax.rsqrt(var + EPS)
    return (y * g.astype(jnp.float32) + b.astype(jnp.float32)).astype(x.dtype)


def _alibi_slopes():
    i = jnp.arange(1, N_QH + 1, dtype=jnp.float32)
    return jnp.exp2(-8.0 * i / N_QH).reshape(N_GROUPS, N_KV)


def _conformer_conv(u_pre, prefix, conv_w, conv_b, ln_g, ln_b, w_conv_out):
    a, b = jnp.split(u_pre, 2, axis=-1)
    u = a * jax.nn.sigmoid(b)
    full = jnp.concatenate([prefix.astype(u.dtype), u], axis=1)
    y = lax.conv_general_dilated(full, conv_w[:, None, :].astype(u.dtype), window_strides=(1,),
                                 padding='VALID', dimension_numbers=('NWC', 'WIO', 'NWC'),
                                 feature_group_count=C_CONV)
    y = _layernorm(y + conv_b, ln_g, ln_b)
    return jax.nn.silu(y) @ w_conv_out, full[:, -(CONV_WIDTH - 1):]


def _mix_by_denominator(outs, lses):
    wts = jax.nn.softmax(jnp.stack(lses), axis=0)
    o = jnp.stack(outs)
    return jnp.sum(wts[..., None].astype(o.dtype) * o, axis=0)


def _dilated_group_prompt(q, k, v, slopes, window, dilation):
    B, S, H, Dh = q.shape
    L = S // dilation
    span = window // dilation
    nb = -(-L // span)
    Lp = nb * span

    def to_blocks(t):
        t = jnp.moveaxis(t.reshape(B, L, dilation, H, Dh), 2, 1)
        t = jnp.pad(t, ((0, 0), (0, 0), (0, Lp - L), (0, 0), (0, 0)))
        return t.reshape(B, dilation, nb, span, H, Dh)

    def with_prev(t):
        prev = jnp.pad(t, ((0, 0), (0, 0), (1, 0), (0, 0), (0, 0), (0, 0)))[:, :, :-1]
        return jnp.concatenate([prev, t], axis=3)

    qb = to_blocks(q)
    kk = with_prev(to_blocks(k))
    vv = with_prev(to_blocks(v))
    s = jnp.einsum('brnqhe,brnkhe->brnhqk', qb, kk).astype(jnp.float32) * SCALE
    qi = jnp.arange(span)[:, None] + span
    ki = jnp.arange(2 * span)[None, :]
    dist = qi - ki
    blk_start = jnp.arange(nb)[:, None, None] * span - span
    valid = (dist >= 0) & (dist <= span) & (blk_start + ki >= 0)
    bias = -slopes.astype(jnp.float32)[:, None, None] * (dist * dilation).astype(jnp.float32)
    s = jnp.where(valid[:, None], s + bias, NEG)
    m = jnp.max(s, axis=-1, keepdims=True)
    p = jnp.exp(s - m)
    l = jnp.sum(p, axis=-1, keepdims=True)
    o = jnp.einsum('brnhqk,brnkhe->brnqhe', (p / l).astype(v.dtype), vv)
    lse = jnp.swapaxes((m + jnp.log(l))[..., 0], -1, -2)

    def from_blocks(t):
        t = t.reshape((B, dilation, Lp) + t.shape[4:])[:, :, :L]
        return jnp.moveaxis(t, 1, 2).reshape((B, S) + t.shape[3:])

    return from_blocks(o), from_blocks(lse)


def _dilated_prompt(q, k, v, slopes):
    outs, lses = [], []
    for g, (window, dilation) in enumerate(DIL_GROUPS):
        o, lse = _dilated_group_prompt(q[:, :, g], k, v, slopes[g], window, dilation)
        outs.append(o)
        lses.append(lse)
    return _mix_by_denominator(outs, lses)


def _dilated_sample(q, k, v, slopes, cache_k, cache_v):
    N, T = q.shape[:2]
    win = cache_k.shape[1]
    kk = jnp.concatenate([cache_k.astype(k.dtype), k], axis=1)
    vv = jnp.concatenate([cache_v.astype(v.dtype), v], axis=1)
    t_idx = win + jnp.arange(T)
    outs, lses = [], []
    for g, (window, dilation) in enumerate(DIL_GROUPS):
        J = window // dilation + 1
        j = jnp.arange(J)
        idx = t_idx[:, None] - j[None, :] * dilation
        valid = idx >= 0
        flat = jnp.clip(idx, 0).reshape(-1)
        kg = jnp.take(kk, flat, axis=1).reshape(N, T, J, N_KV, HEAD_DIM)
        vg = jnp.take(vv, flat, axis=1).reshape(N, T, J, N_KV, HEAD_DIM)
        s = jnp.einsum('nthe,ntjhe->nthj', q[:, :, g], kg).astype(jnp.float32) * SCALE
        s = s - slopes[g].astype(jnp.float32)[:, None] * (j * dilation).astype(jnp.float32)
        s = jnp.where(valid[None, :, None, :], s, NEG)
        lse = jax.nn.logsumexp(s, axis=-1)
        p = jnp.exp(s - lse[..., None])
        outs.append(jnp.einsum('nthj,ntjhe->nthe', p.astype(vg.dtype), vg))
        lses.append(lse)
    return _mix_by_denominator(outs, lses)


def _mem_kv(mem, g_mem, w_mem_kv):
    N, M, _ = mem.shape
    kv = _rmsnorm(mem, g_mem) @ w_mem_kv
    mk, mv = jnp.split(kv, 2, axis=-1)
    return mk.reshape(N, M, N_XHEADS, X_HEAD_DIM), mv.reshape(N, M, N_XHEADS, X_HEAD_DIM)


def _mem_attend(qx, mk, mv):
    s = jnp.einsum('nthe,nmhe->nhtm', qx, mk.astype(qx.dtype)).astype(jnp.float32) * X_SCALE
    p = jax.nn.softmax(s, axis=-1)
    return jnp.einsum('nhtm,nmhe->nthe', p.astype(mv.dtype), mv)


def _layer(x, conv_prefix, dil_attend, mem_k, mem_v, slopes,
           g_pre_mix, w_in, b_gate, conv_w, conv_b, conv_ln_g, conv_ln_b, w_conv_out,
           w_dil_o, w_x_o, w_out, g_post_mix, g_pre_ffn, w_ffn_gate, w_ffn_up, w_ffn_down, g_post_ffn):
    N, T, _ = x.shape
    h = _rmsnorm(x, g_pre_mix)
    proj = h @ w_in
    u_pre, q_d, k_d, v_d, q_x, gate_logits = jnp.split(proj, SPLITS, axis=-1)
    y_conv, conv_state = _conformer_conv(u_pre, conv_prefix, conv_w, conv_b, conv_ln_g, conv_ln_b, w_conv_out)
    q_d = q_d.reshape(N, T, N_GROUPS, N_KV, HEAD_DIM)
    k_d = k_d.reshape(N, T, N_KV, HEAD_DIM)
    v_d = v_d.reshape(N, T, N_KV, HEAD_DIM)
    y_dil = dil_attend(q_d, k_d, v_d, slopes).reshape(N, T, W_KV) @ w_dil_o
    y_mem = _mem_attend(q_x.reshape(N, T, N_XHEADS, X_HEAD_DIM), mem_k, mem_v).reshape(N, T, W_QX) @ w_x_o
    gates = jax.nn.sigmoid(gate_logits + b_gate).reshape(N, T, 3, D_MODEL)
    merged = gates[:, :, 0] * y_conv + gates[:, :, 1] * y_dil + gates[:, :, 2] * y_mem
    x = x + _rmsnorm(merged @ w_out, g_post_mix)
    h2 = _rmsnorm(x, g_pre_ffn)
    f = (jax.nn.silu(h2 @ w_ffn_gate) * (h2 @ w_ffn_up)) @ w_ffn_down
    x = x + _rmsnorm(f, g_post_ffn)
    return x, conv_state, k_d, v_d


def setup_inputs(seed: int = 0) -> dict:
    key = jax.random.key(seed)
    ks = jax.random.split(key, 32)
    win = min(WINDOW_MAX, PAST_LEN)

    def nrm(k, shape, scale):
        return jax.random.normal(k, shape, jnp.float32) * scale

    def gain(k, n):
        return 1.0 + 0.05 * jax.random.normal(k, (DEPTH, n), jnp.float32)

    return {
        'x_prompt': nrm(ks[0], (BATCH, SEQ, D_MODEL), 1.0),
        'x_sample': nrm(ks[1], (DEC_BATCH, DEC_SEQ, D_MODEL), 1.0),
        'cache_win_k': nrm(ks[2], (DEPTH, DEC_BATCH, win, N_KV, HEAD_DIM), 1.0),
        'cache_win_v': nrm(ks[3], (DEPTH, DEC_BATCH, win, N_KV, HEAD_DIM), 1.0),
        'state_conv': nrm(ks[4], (DEPTH, DEC_BATCH, CONV_WIDTH - 1, C_CONV), 0.5),
        'cache_mem_k': nrm(ks[5], (DEPTH, DEC_BATCH, N_MEM, N_XHEADS, X_HEAD_DIM), 1.0),
        'cache_mem_v': nrm(ks[6], (DEPTH, DEC_BATCH, N_MEM, N_XHEADS, X_HEAD_DIM), 1.0),
        'mem_prompt': nrm(ks[7], (BATCH, N_MEM, D_MODEL), 1.0),
        'g_pre_mix': gain(ks[8], D_MODEL),
        'w_in': nrm(ks[9], (DEPTH, D_MODEL, N_IN), D_MODEL ** -0.5),
        'b_gate': nrm(ks[10], (DEPTH, W_GATE), 0.1),
        'conv_w': nrm(ks[11], (DEPTH, CONV_WIDTH, C_CONV), CONV_WIDTH ** -0.5),
        'conv_b': nrm(ks[12], (DEPTH, C_CONV), 0.01),
        'conv_ln_g': gain(ks[13], C_CONV),
        'conv_ln_b': nrm(ks[14], (DEPTH, C_CONV), 0.01),
        'w_conv_out': nrm(ks[15], (DEPTH, C_CONV, D_MODEL), C_CONV ** -0.5),
        'w_dil_o': nrm(ks[16], (DEPTH, W_KV, D_MODEL), W_KV ** -0.5),
        'g_mem': gain(ks[17], D_MODEL),
        'w_mem_kv': nrm(ks[18], (DEPTH, D_MODEL, 2 * W_QX), D_MODEL ** -0.5),
        'w_x_o': nrm(ks[19], (DEPTH, W_QX, D_MODEL), W_QX ** -0.5),
        'w_out': nrm(ks[20], (DEPTH, D_MODEL, D_MODEL), D_MODEL ** -0.5),
        'g_post_mix': gain(ks[21], D_MODEL),
        'g_pre_ffn': gain(ks[22], D_MODEL),
        'w_ffn_gate': nrm(ks[23], (DEPTH, D_MODEL, D_FF), D_MODEL ** -0.5),
        'w_ffn_up': nrm(ks[24], (DEPTH, D_MODEL, D_FF), D_MODEL ** -0.5),
        'w_ffn_down': nrm(ks[25], (DEPTH, D_FF, D_MODEL), D_FF ** -0.5),
        'g_post_ffn': gain(ks[26], D_MODEL),
    }


def reference(x_prompt, x_sample, cache_win_k, cache_win_v, state_conv, cache_mem_k, cache_mem_v, mem_prompt,
              g_pre_mix, w_in, b_gate, conv_w, conv_b, conv_ln_g, conv_ln_b, w_conv_out, w_dil_o,
              g_mem, w_mem_kv, w_x_o, w_out, g_post_mix, g_pre_ffn, w_ffn_gate, w_ffn_up, w_ffn_down, g_post_ffn):
    slopes = _alibi_slopes()
    B, S, _ = x_prompt.shape
    win_p = min(WINDOW_MAX, S)
    xp, xs = x_prompt, x_sample
    wkp, wvp, cvp, mkp_l, mvp_l, wks, wvs, cvs = [], [], [], [], [], [], [], []
    for l in range(DEPTH):
        weights = (g_pre_mix[l], w_in[l], b_gate[l], conv_w[l], conv_b[l], conv_ln_g[l], conv_ln_b[l],
                   w_conv_out[l], w_dil_o[l], w_x_o[l], w_out[l], g_post_mix[l], g_pre_ffn[l],
                   w_ffn_gate[l], w_ffn_up[l], w_ffn_down[l], g_post_ffn[l])
        mkp, mvp = _mem_kv(mem_prompt, g_mem[l], w_mem_kv[l])
        conv_zero = jnp.zeros((B, CONV_WIDTH - 1, C_CONV), xp.dtype)
        xp, conv_p, kp, vp = _layer(xp, conv_zero, _dilated_prompt, mkp, mvp, slopes, *weights)
        dil_s = functools.partial(_dilated_sample, cache_k=cache_win_k[l], cache_v=cache_win_v[l])
        xs, conv_s, ks_new, vs_new = _layer(xs, state_conv[l], dil_s, cache_mem_k[l], cache_mem_v[l], slopes, *weights)
        wkp.append(kp[:, S - win_p:])
        wvp.append(vp[:, S - win_p:])
        cvp.append(conv_p)
        mkp_l.append(mkp)
        mvp_l.append(mvp)
        wks.append(ks_new)
        wvs.append(vs_new)
        cvs.append(conv_s)
    return (xp, xs, jnp.stack(wkp), jnp.stack(wvp), jnp.stack(cvp), jnp.stack(mkp_l), jnp.stack(mvp_l),
            jnp.stack(wks), jnp.stack(wvs), jnp.stack(cvs))
```

```python
import os
import math
from contextlib import ExitStack
import numpy as np
import concourse.bass as bass
import concourse.mybir as mybir
from concourse.bass_utils import run_bass_kernel_spmd

F32 = mybir.dt.float32
BF16 = mybir.dt.bfloat16
AF = mybir.ActivationFunctionType
ALU = mybir.AluOpType
AX = mybir.AxisListType

D = 2048
NCH = 16
SEQ = 2048
HALF = 1024
C_CONV = 1024
N_IN = 11264
D_FF = 5632
NFF = 44
EPS = 1e-6
SCALE = 128 ** -0.5
NEGB = -30000.0
T_EXT = 1060
OWN0 = 32
SMP0 = 1056
NT = 1028
CH3 = [(32, 343), (375, 343), (718, 342)]
CH3U = [(0, 354), (354, 353), (707, 353)]
COL_A, COL_B, COL_Q, COL_K, COL_V, COL_QX, COL_G = 0, 1024, 2048, 3584, 4096, 4608, 5120
SB_BASE = 16512
SB_TOP = 229312
DIL = (1, 4, 16)
DEBUG = bool(os.environ.get("MK_DEBUG"))


class Buf:
    __slots__ = ("name", "w", "r", "lsem", "lcnt", "dj")

    def __init__(self, name="", dj=False):
        self.name = name
        self.w = {}
        self.r = []
        self.lsem = None
        self.lcnt = 0
        self.dj = dj


class Sched:
    ENG = ("pe", "dve", "act", "pool", "sp")

    def __init__(self, nc, stack):
        self.nc = nc
        self.stack = stack
        self.e = {"pe": nc.tensor, "dve": nc.vector, "act": nc.scalar, "pool": nc.gpsimd, "sp": nc.sync}
        self.sems = {}
        self.cnt = {}
        for k in self.ENG:
            self.sems[k] = stack.enter_context(nc.semaphore("s_" + k))
            self.cnt[k] = 0
        self.seen = {k: {} for k in self.ENG}
        self.nsem = 5
        self.dsem_pool = []
        self.n_wait = 0
        self.n_ins = {k: 0 for k in self.ENG}

    def _dsem(self, buf):
        if buf.lsem is None:
            key = "d%d" % self.nsem
            self.sems[key] = self.stack.enter_context(self.nc.semaphore(key))
            self.nsem += 1
            buf.lsem = key
        return buf.lsem

    def share_dsem(self, src, dst):
        dst.lsem = src.lsem
        dst.lcnt = src.lcnt

    def _wait(self, e, deps):
        best = {}
        for (k, v) in deps:
            if k == "pe" and e == "pe":
                continue
            if v > best.get(k, 0):
                best[k] = v
        for k, v in best.items():
            if self.seen[e].get(k, 0) >= v:
                continue
            self.e[e].wait_ge(self.sems[k], v)
            self.seen[e][k] = v
            self.n_wait += 1

    @staticmethod
    def _deps(reads, writes):
        deps = set()
        for b in reads:
            deps.update(b.w.items())
        for b in writes:
            if not b.dj:
                deps.update(b.w.items())
            deps.update(b.r)
        return deps

    @staticmethod
    def _mark(reads, writes, tick):
        for b in reads:
            if len(b.r) > 48:
                b.r = Sched._compress(b.r)
            b.r.append(tick)
        for b in writes:
            if b.dj:
                b.w[tick[0]] = max(b.w.get(tick[0], 0), tick[1])
            else:
                b.w = {tick[0]: tick[1]}
                b.r = []

    def op(self, e, fn, reads=(), writes=(), inc=True):
        self._wait(e, self._deps(reads, writes))
        ins = fn(self.e[e])
        self.n_ins[e] += 1
        if inc:
            self.cnt[e] += 1
            ins.then_inc(self.sems[e], 1)
            tick = (e, self.cnt[e])
        else:
            tick = (e, self.cnt[e] + 1)
        self._mark(reads, writes, tick)
        return ins

    @staticmethod
    def _compress(ticks):
        best = {}
        for (k, v) in ticks:
            if v > best.get(k, 0):
                best[k] = v
        return list(best.items())

    def dma(self, q, out, in_, sb, reads=(), writes=(), **kw):
        self._wait(q, self._deps(reads, writes))
        key = self._dsem(sb)
        ins = self.e[q].dma_start(out=out, in_=in_, **kw)
        self.n_ins[q] += 1
        sb.lcnt += 16
        ins.then_inc(self.sems[key], 16)
        tick = (key, sb.lcnt)
        self._mark(reads, writes, tick)
        return ins

    def finish(self, bufs, e="sp"):
        deps = set()
        for b in bufs:
            deps.update(b.w.items())
            deps.update(b.r)
        self._wait(e, deps)


class Tn:
    def __init__(self, ap, off, end, inherit, name):
        self.ap = ap
        self.off = off
        self.end = end
        self.inherit = inherit
        self.bufs = []
        self.name = name
        self.b = self.buf(name)

    def buf(self, name=None, dj=False):
        b = Buf(name or self.name, dj)
        b.r = list(self.inherit)
        self.bufs.append(b)
        return b

    def __getitem__(self, k):
        return self.ap[k]


class Mem:
    def __init__(self, arena, base, top):
        self.arena = arena
        self.base = base
        self.free_list = [(base, top)]
        self.retired = []
        self.peak = 0

    def view(self, off, shape, dt):
        n = int(np.prod(shape[1:]))
        sz = 4 if dt == F32 else 2
        e0 = (off - self.base) // 2
        ap = self.arena[0:shape[0], e0:e0 + n * sz // 2]
        if dt == F32:
            ap = ap.bitcast(F32)
        if len(shape) == 3:
            ap = ap.rearrange("p (a b) -> p a b", a=shape[1])
        elif len(shape) == 4:
            ap = ap.rearrange("p (a b c) -> p a b c", a=shape[1], b=shape[2])
        return ap

    def alloc(self, name, shape, dt, top=False):
        n = int(np.prod(shape[1:])) * (4 if dt == F32 else 2)
        n = (n + 63) // 64 * 64
        order = list(enumerate(self.free_list))
        if top:
            order = order[::-1]
        for i, (s, e) in order:
            if e - s >= n:
                if top:
                    off = e - n
                    if e - s == n:
                        self.free_list.pop(i)
                    else:
                        self.free_list[i] = (s, e - n)
                else:
                    off = s
                    if e - s == n:
                        self.free_list.pop(i)
                    else:
                        self.free_list[i] = (s + n, e)
                break
        else:
            raise RuntimeError("SBUF arena OOM for %s (%d bytes); free=%s" % (name, n, self.free_list))
        end = off + n
        self.peak = max(self.peak, end)
        ticks = set()
        keep = []
        for (rs, re, rt) in self.retired:
            if rs < end and re > off:
                ticks.update(rt)
                if rs >= off and re <= end:
                    continue
            keep.append((rs, re, rt))
        self.retired = keep
        return Tn(self.view(off, shape, dt), off, end, list(Sched._compress(ticks)), name)

    def free(self, *tns):
        for tn in tns:
            ticks = set(tn.inherit)
            for b in tn.bufs:
                ticks.update(b.w.items())
                ticks.update(b.r)
            self.retired.append((tn.off, tn.end, Sched._compress(ticks)))
            self.free_list.append((tn.off, tn.end))
            self.free_list.sort()
            merged = []
            for s, e in self.free_list:
                if merged and merged[-1][1] == s:
                    merged[-1] = (merged[-1][0], e)
                else:
                    merged.append((s, e))
            self.free_list = merged


class Ring:
    def __init__(self, tns):
        self.tns = tns
        self.i = 0

    def next(self):
        t = self.tns[self.i % len(self.tns)]
        self.i += 1
        return t


class WStream:
    def __init__(self, S, mem, ns):
        self.S = S
        self.slots = [mem.alloc("wslot%d" % i, [128, 16, 128], BF16) for i in range(ns)]
        self.plan = []
        self.issued = 0
        self.consumed = 0

    def add(self, tag, w, r0, nk, c0):
        self.plan.append((tag, w, r0, nk, c0))

    def _issue(self, i):
        tag, w, r0, nk, c0 = self.plan[i]
        slot = self.slots[i % len(self.slots)]
        src = w[r0:r0 + nk * 128, c0:c0 + 128].rearrange("(k p) n -> p k n", p=128)
        self.S.dma("pool", slot[:, 0:nk, :], src, slot.b, writes=[slot.b])

    def next(self, tag):
        i = self.consumed
        assert self.plan[i][0] == tag, (self.plan[i][0], tag)
        lim = min(len(self.plan), i + len(self.slots) - 1)
        while self.issued < lim:
            self._issue(self.issued)
            self.issued += 1
        self.consumed += 1
        return self.slots[i % len(self.slots)]


def _sl(start, count, step=1):
    return slice(start, start + step * (count - 1) + 1, step)


def build_program(dbg=()):
    nc = bass.Bass("TRN2", target_bir_lowering=False)

    def din(name, shape, dt=F32):
        return nc.dram_tensor(name, list(shape), dt, kind="ExternalInput").ap()

    def dout(name, shape, dt=F32):
        return nc.dram_tensor(name, list(shape), dt, kind="ExternalOutput").ap()

    x_own = din("x_own", [HALF, D]); x_ctx = din("x_ctx", [HALF, D]); x_smp = din("x_smp", [4, D])
    ck = din("ck", [4, 2048, 512]); cv = din("cv", [4, 2048, 512])
    sconv = din("sconv", [4, 30, C_CONV])
    cmk = din("cmk", [4, 256, 512]); cmv = din("cmv", [4, 256, 512])
    mem = din("mem", [256, D])
    w_in = din("w_in", [D, N_IN]); w_conv_out = din("w_conv_out", [C_CONV, D]); w_dil_o = din("w_dil_o", [512, D])
    w_mem_kv = din("w_mem_kv", [D, 1024]); w_x_o = din("w_x_o", [512, D]); w_out = din("w_out", [D, D])
    w_gate = din("w_gate", [D, D_FF]); w_up = din("w_up", [D, D_FF]); w_down = din("w_down", [D_FF, D])
    g_pre_mix = din("g_pre_mix", [1, D]); g_mem = din("g_mem", [1, D]); g_post_mix = din("g_post_mix", [1, D])
    g_pre_ffn = din("g_pre_ffn", [1, D]); g_post_ffn = din("g_post_ffn", [1, D])
    gpmT_d = din("gpmT", [128, 16]); gpfT_d = din("gpfT", [128, 16])
    bgT_d = din("bgT", [128, 48]); cwT_d = din("cwT", [128, 8 * 31]); cbT_d = din("cbT", [128, 8])
    lngT_d = din("lngT", [128, 8]); lnbT_d = din("lnbT", [128, 8])
    ident_d = din("ident", [128, 128]); sel_d = din("sel", [4, 512])
    tb0_d = din("tb0", [128, 1024]); tb0f_d = din("tb0f", [128, 1024])
    tb1_d = din("tb1", [128, 1024]); tb1f_d = din("tb1f", [128, 1024])
    tb2_d = din("tb2", [128, 1024]); tbs_d = din("tbs", [128, 12])

    y_own = dout("y_own", [HALF, D]); y_smp = dout("y_smp", [4, D])
    wk_own = dout("wk_own", [HALF, 512]); wv_own = dout("wv_own", [HALF, 512])
    conv_p = dout("conv_p", [30, C_CONV])
    memk = dout("memk", [256, 512]); memv = dout("memv", [256, 512])
    wk_s = dout("wk_s", [4, 512]); wv_s = dout("wv_s", [4, 512])
    conv_s = dout("conv_s", [4, 30, C_CONV])
    x1s = nc.dram_tensor("x1s", [HALF + 4, D], F32).ap()

    st = ExitStack()
    S = Sched(nc, st)
    arena = nc.alloc_sbuf_tensor_at("arena", [128, (SB_TOP - SB_BASE) // 2], BF16, offset=SB_BASE)
    M = Mem(arena, SB_BASE, SB_TOP)
    PSF = st.enter_context(nc.psum_tensor("PSF", [128, 3072], F32))
    PSB = st.enter_context(nc.psum_tensor("PSB", [128, 2048], BF16))
    pf_bufs = [Buf("pf%d" % i) for i in range(6)]
    pb_bufs = [Buf("pb%d" % i) for i in range(2)]
    pstate = {"f": 0, "b": 0, "rot": [0, 1, 2, 3, 4, 5], "pair": 0}
    out_bufs = []
    dbg_outs = {}

    def pf():
        rot = pstate["rot"]
        i = rot[pstate["f"] % len(rot)]
        pstate["f"] += 1
        return PSF[:, i * 512:(i + 1) * 512], pf_bufs[i]

    def pf_fixed(i):
        return PSF[:, i * 512:(i + 1) * 512], pf_bufs[i]

    def pb():
        i = pstate["b"] % 2
        pstate["b"] += 1
        return PSB[:, i * 1024:(i + 1) * 1024], pb_bufs[i]

    def OP(e, fn, R=(), W=(), inc=True):
        return S.op(e, fn, reads=R, writes=W, inc=inc)

    def act(out, in_, func, R, W, **kw):
        return OP("act", lambda e: e.activation(out, in_, func, **kw), R, W)

    def load(dst_tn, dst_ap, src, q="sp"):
        S.dma(q, dst_ap, src, dst_tn.b, writes=[dst_tn.b])

    x1s_buf = Buf("x1s", dj=True)

    def store(dst, src_ap, src_buf, final=True, dram_buf=None):
        S.dma("sp", dst, src_ap, src_buf, reads=[src_buf], writes=([dram_buf] if dram_buf is not None else []))
        if final and src_buf not in out_bufs:
            out_bufs.append(src_buf)

    def dump(name, tn, ap=None, dt=F32, shape=None):
        if name not in dbg:
            return
        ap = tn.ap if ap is None else ap
        shape = list(ap.shape) if shape is None else shape
        d = dout("dbg_" + name, shape, dt)
        S.dma("sp", d, ap, tn.b, reads=[tn.b])
        out_bufs.append(tn.b)

    ws = WStream(S, M, 5)
    for j in range(8):
        ws.add("ctxkv", w_in, 0, 16, COL_K + j * 128)
    for c in range(8):
        ws.add("a", w_in, 0, 16, COL_A + c * 128)
        ws.add("b", w_in, 0, 16, COL_B + c * 128)
    for h in range(4):
        ws.add("k", w_in, 0, 16, COL_K + h * 128)
        ws.add("v", w_in, 0, 16, COL_V + h * 128)
        for g in range(3):
            ws.add("q", w_in, 0, 16, COL_Q + (g * 4 + h) * 128)
    for j in range(8):
        ws.add("memkv", w_mem_kv, 0, 16, j * 128)
    for h in range(4):
        ws.add("qx", w_in, 0, 16, COL_QX + h * 128)
    for c in range(16):
        for br in range(3):
            ws.add("gate", w_in, 0, 16, COL_G + br * D + c * 128)
            ws.add("bw", (w_conv_out, w_dil_o, w_x_o)[br], 0, (8, 4, 4)[br], c * 128)
    for c in range(16):
        ws.add("wout", w_out, 0, 16, c * 128)
    for p in range(4):
        for f in range(11 * p, 11 * p + 11):
            ws.add("fg", w_gate, 0, 16, f * 128)
            ws.add("fu", w_up, 0, 16, f * 128)
        for c in range(16):
            ws.add("fd", w_down, p * 11 * 128, 11, c * 128)

    identF = M.alloc("identF", [128, 128], F32); load(identF, identF.ap, ident_d)
    identB = M.alloc("identB", [128, 128], BF16)
    OP("dve", lambda e: e.tensor_copy(identB.ap, identF.ap), [identF.b], [identB.b])
    onesB = M.alloc("onesB", [128, 128], BF16); OP("dve", lambda e: e.memset(onesB.ap, 1.0), [], [onesB.b])
    onesF = M.alloc("onesF", [128, 128], F32); OP("dve", lambda e: e.memset(onesF.ap, 1.0), [], [onesF.b])
    bg = M.alloc("bg", [128, 48], F32); load(bg, bg.ap, bgT_d)
    cw = M.alloc("cw", [128, 8, 31], F32); load(cw, cw.ap, cwT_d.rearrange("p (c j) -> p c j", c=8))
    cb = M.alloc("cb", [128, 8], F32); load(cb, cb.ap, cbT_d)
    lng = M.alloc("lng", [128, 8], F32); load(lng, lng.ap, lngT_d)
    lnb = M.alloc("lnb", [128, 8], F32); load(lnb, lnb.ap, lnbT_d)
    gpm = M.alloc("gpm", [128, 16], F32); load(gpm, gpm.ap, gpmT_d)
    gpf = M.alloc("gpf", [128, 16], F32); load(gpf, gpf.ap, gpfT_d)
    sel = M.alloc("sel", [4, 4, 128], F32); load(sel, sel.ap, sel_d.rearrange("p (s m) -> p s m", s=4))
    smalls = M.alloc("smalls", [128, 64], F32)
    sm_bufs = [smalls.buf("sm%d" % i) for i in range(8)]
    sm_state = {"i": 0}

    def small2():
        i = sm_state["i"] % 8
        sm_state["i"] += 1
        return smalls[:, 2 * i:2 * i + 1], smalls[:, 2 * i + 1:2 * i + 2], sm_bufs[i]

    io = {}

    def io_alloc(with_z=False):
        io["x"] = Ring([M.alloc("xr%d" % i, [128, D], F32) for i in range(3 if with_z else 5)])
        if with_z:
            io["z"] = Ring([M.alloc("zr%d" % i, [128, D], F32) for i in range(4)])
        io["hb"] = Ring([M.alloc("hb%d" % i, [128, D], BF16) for i in range(4)])
        if not with_z:
            io["gA"] = M.alloc("gA", [128, D], F32)
        io["gB"] = M.alloc("gB", [128, D], F32)

    def io_free():
        M.free(*io["x"].tns, *io["hb"].tns, io["gB"])
        if "gA" in io:
            M.free(io["gA"])
        if "z" in io:
            M.free(*io["z"].tns)
        io.clear()

    def rstd_chain(ssap, rsap, sb, n):
        OP("dve", lambda e: e.tensor_scalar(rsap[0:n], ssap[0:n], 1.0 / D, EPS, ALU.mult, ALU.add), [sb], [sb])
        OP("act", lambda e: e.sqrt(rsap[0:n], rsap[0:n]), [sb], [sb])
        OP("dve", lambda e: e.reciprocal(rsap[0:n], rsap[0:n]), [sb], [sb])

    def norm_a(src, n, xt=None):
        if xt is None:
            xt = io["x"].next()
            S.dma("sp", xt[0:n, :], src, xt.b, writes=[xt.b])
        hb = io["hb"].next()
        ss, rs, sb = small2()
        OP("dve", lambda e: e.memset(ss[0:n], 0.0), [], [sb])
        act(hb[0:n, :], xt[0:n, :], AF.Square, [xt.b], [hb.b, sb], accum_out=ss[0:n])
        return (xt, hb, ss, rs, sb, n)

    def norm_r(ctx):
        xt, hb, ss, rs, sb, n = ctx
        rstd_chain(ss, rs, sb, n)

    def norm_b1(ctx, gtn):
        xt, hb, ss, rs, sb, n = ctx
        OP("dve", lambda e: e.scalar_tensor_tensor(hb[0:n, :], xt[0:n, :], rs[0:n], gtn[0:n, :], ALU.mult, ALU.mult),
           [xt.b, sb, gtn.b], [hb.b])

    def norm_b2(ctx, dst_fn, dst_buf):
        xt, hb, ss, rs, sb, n = ctx
        for hlf in range(2):
            bank, bb = pb()
            for c in range(8):
                cc = hlf * 8 + c
                OP("pe", lambda e: e.transpose(bank[:, c * 128:c * 128 + n], hb[0:n, cc * 128:(cc + 1) * 128], identB[0:n, 0:n]),
                   [hb.b, identB.b], [bb], inc=(c == 7))
            bv = bank.rearrange("p (c t) -> p c t", c=8)
            if hlf == 0:
                act(dst_fn(hlf), bv[:, :, 0:n], AF.Copy, [bb], [dst_buf])
            else:
                OP("dve", lambda e: e.tensor_copy(dst_fn(hlf), bv[:, :, 0:n]), [bb], [dst_buf])

    def sw_pipeline(n, stages, order=None):
        if order is None:
            order = list(reversed(range(len(stages))))
        for step in range(n + len(stages) - 1):
            for si in order:
                t = step - si
                if 0 <= t < n:
                    stages[si](t)

    def proj(slab, nk, rhs_fn, rhs_bufs, chunks, evac):
        for ci, (c0, n) in enumerate(chunks):
            bank, bb = pf()
            for k in range(nk):
                OP("pe", lambda e: e.matmul(bank[:, 0:n], slab[:, k, :], rhs_fn(k, c0, n), start=(k == 0), stop=(k == nk - 1)),
                   [slab.b] + rhs_bufs, [bb], inc=(k == nk - 1))
            evac(ci, c0, n, bank, bb)

    def transpose_out(src_ap, src_buf, ntok_tiles, n_last, dst_fn):
        for t in range(ntok_tiles):
            n = 128 if t < ntok_tiles - 1 or n_last == 128 else n_last
            bank, bb = pf()
            OP("pe", lambda e: e.transpose(bank[0:n, 0:128], src_ap[:, t * 128:t * 128 + n], identF.ap), [src_buf, identF.b], [bb])
            dst_fn(t, n, bank, bb)

    io_alloc()
    load(io["gA"], io["gA"].ap, g_pre_mix.partition_broadcast(128))
    kvT = M.alloc("kvT", [128, 8, SEQ], BF16, top=True)
    hT = M.alloc("hT", [128, NCH, T_EXT], BF16, top=True)
    hTc = M.alloc("hTc", [128, NCH, HALF], BF16)
    memT = M.alloc("memT", [128, NCH, 256], BF16, top=True)
    load(io["gB"], io["gB"].ap, g_mem.partition_broadcast(128))
    hT.b.dj = True
    memT.b.dj = True
    hTc.b.dj = True
    kvT.b.dj = True
    tiles = []

    bgq = []

    def ctx_after():
        def tail():
            act(hT[:, :, 0:32], hTc[:, :, HALF - 32:HALF], AF.Copy, [hTc.b], [hT.b])
        bgq.append(tail)
        pend = {}
        for j in range(8):
            def mm(j=j):
                slab = ws.next("ctxkv")
                got = []

                def ev(ci, c0, n, bank, bb):
                    got.append((c0, n, bank, bb))
                proj(slab, 16, lambda k, c0, n: hTc[:, k, c0:c0 + n], [hTc.b], [(0, 512), (512, 512)], ev)
                pend[j] = got

            def evac(j=j):
                for (c0, n, bank, bb) in pend.pop(j):
                    act(kvT[:, j, c0:c0 + n], bank[:, 0:n], AF.Copy, [bb], [kvT.b])
            if j == 0:
                bgq.append(mm)
            else:
                bgq.append(lambda mm=mm, evp=prev_ev: (evp(), mm()))
            prev_ev = evac
        bgq.append(prev_ev)
    for t in range(8):
        tiles.append((x_ctx[t * 128:(t + 1) * 128, :], 128, "gA",
                      (lambda hlf, t=t: hTc[:, 8 * hlf:8 * hlf + 8, t * 128:(t + 1) * 128]), hTc.b,
                      ctx_after if t == 7 else None))
    for t in range(8):
        tiles.append((x_own[t * 128:(t + 1) * 128, :], 128, "gA",
                      (lambda hlf, t=t: hT[:, 8 * hlf:8 * hlf + 8, OWN0 + t * 128:OWN0 + (t + 1) * 128]), hT.b, None))
    tiles.append((x_smp, 4, "gA", (lambda hlf: hT[:, 8 * hlf:8 * hlf + 8, SMP0:SMP0 + 4]), hT.b, None))
    for t in range(2):
        tiles.append((mem[t * 128:(t + 1) * 128, :], 128, "gB",
                      (lambda hlf, t=t: memT[:, 8 * hlf:8 * hlf + 8, t * 128:(t + 1) * 128]), memT.b, None))
    nctx = {}

    def st_a(i):
        nctx[i] = norm_a(tiles[i][0], tiles[i][1])

    def st_r(i):
        norm_r(nctx[i])

    def st_b1(i):
        norm_b1(nctx[i], io[tiles[i][2]])

    def st_b2(i):
        src, n, gk, dst_fn, dst_buf, hook = tiles[i]
        norm_b2(nctx.pop(i), dst_fn, dst_buf)
        if hook is not None:
            hook()
        elif bgq:
            bgq.pop(0)()
    sw_pipeline(len(tiles), [st_a, st_r, st_b1, st_b2], order=[1, 0, 2, 3])
    while bgq:
        bgq.pop(0)()
    hT.b.dj = False
    memT.b.dj = False
    kvT.b.dj = False
    M.free(hTc)
    io_free()
    dump("hT", hT, dt=BF16)
    dump("kvT_ctx", kvT, dt=BF16)

    branchT = M.alloc("branchT", [128, NCH, T_EXT], BF16, top=True)
    b_conv = branchT.b
    b_conv.dj = True
    b_dil = branchT.buf("dil", dj=True)
    b_xat = branchT.buf("xat", dj=True)
    yconv = M.alloc("yconv", [128, 8, NT], F32)
    yconv.b.dj = True
    utail = M.alloc("utail", [128, 8, 32], F32); utail.b.dj = True
    usmp = M.alloc("usmp", [128, 8, 4], F32); usmp.b.dj = True
    fulls = M.alloc("fulls", [128, 8, 4, 31], F32); fulls.b.dj = True
    ucr = Ring([M.alloc("uc%d" % i, [128, T_EXT], F32) for i in range(2)])
    sgr = Ring([M.alloc("sgd%d" % i, [128, 354], F32) for i in range(3)])
    scr = Ring([M.alloc("sc%d" % i, [32, C_CONV], F32) for i in range(2)])

    for s in range(4):
        sc = scr.next()
        S.dma("sp", sc[0:30, :], sconv[s], sc.b, writes=[sc.b])
        bank, bb = pf()
        for c in range(8):
            OP("pe", lambda e: e.transpose(bank[:, c * 32:c * 32 + 30], sc[0:30, c * 128:(c + 1) * 128], identF[0:30, 0:30]),
               [sc.b, identF.b], [bb], inc=(c == 7))
        act(fulls[:, :, s, 0:30], bank[:, 0:256].rearrange("p (c t) -> p c t", c=8)[:, :, 0:30], AF.Copy, [bb], [fulls.b])
        store(conv_s[s, 0:29, :], sc[1:30, :], sc.b)

    M.free(*scr.tns)
    ubr = Ring([M.alloc("ub%d" % i, [128, T_EXT], BF16) for i in range(2)])
    dgr = Ring([M.alloc("dg%d" % i, [128, 31, 128], BF16) for i in range(2)])
    for c in range(8):
        slab_a = ws.next("a")
        slab_b = ws.next("b")
        uc = ucr.next()
        ub = ubr.next()
        dg = dgr.next()
        OP("dve", lambda e: e.tensor_tensor(dg.ap, identB.ap.unsqueeze(1).to_broadcast([128, 31, 128]),
                                            cw[:, c, :].unsqueeze(2).to_broadcast([128, 31, 128]), ALU.mult),
           [identB.b, cw.b], [dg.b])
        for ci, (c0, n) in enumerate(CH3U):
            bka, bba = pf()
            for k in range(16):
                OP("pe", lambda e: e.matmul(bka[:, 0:n], slab_a[:, k, :], hT[:, k, c0:c0 + n], start=(k == 0), stop=(k == 15)),
                   [slab_a.b, hT.b], [bba], inc=(k == 15))
            bkb, bbb = pf()
            for k in range(16):
                OP("pe", lambda e: e.matmul(bkb[:, 0:n], slab_b[:, k, :], hT[:, k, c0:c0 + n], start=(k == 0), stop=(k == 15)),
                   [slab_b.b, hT.b], [bbb], inc=(k == 15))
            sg = sgr.next()
            act(sg[:, 0:n], bkb[:, 0:n], AF.Sigmoid, [bbb], [sg.b])
            OP("dve", lambda e: e.tensor_tensor(uc[:, c0:c0 + n], bka[:, 0:n], sg[:, 0:n], ALU.mult), [bba, sg.b], [uc.b])
        act(ub.ap, uc.ap, AF.Copy, [uc.b], [ub.b])
        OP("dve", lambda e: e.tensor_copy(utail[:, c, 0:30], uc[:, 1026:1056]), [uc.b], [utail.b])
        OP("dve", lambda e: e.tensor_copy(usmp[:, c, :], uc[:, SMP0:SMP0 + 4]), [uc.b], [usmp.b])
        OP("dve", lambda e: e.tensor_copy(fulls[:, c, :, 30], uc[:, SMP0:SMP0 + 4]), [uc.b], [fulls.b])
        for hv in range(2):
            bank, bb = pf()
            for j in range(31):
                o_ = 2 + j + 512 * hv
                OP("pe", lambda e: e.matmul(bank[:, 0:512], dg[:, j, :], ub[:, o_:o_ + 512], start=(j == 0), stop=(j == 30)),
                   [dg.b, ub.b], [bb], inc=(j == 30))
            act(yconv[:, c, 512 * hv:512 * (hv + 1)], bank[:, 0:512], AF.Identity, [bb, cb.b], [yconv.b], bias=cb[:, c:c + 1])
    M.free(*ubr.tns, *dgr.tns)
    urow = M.alloc("urow", [32, C_CONV], F32)
    dump("yconv", yconv)
    for (src, n, dst) in ((utail, 30, conv_p), (usmp, 4, None)):
        for hlf in range(2):
            bank, bb = pf()
            for c4 in range(4):
                c = hlf * 4 + c4
                OP("pe", lambda e: e.transpose(bank[0:n, c4 * 128:(c4 + 1) * 128], src[:, c, 0:n], identF.ap),
                   [src.b, identF.b], [bb], inc=(c4 == 3))
            act(urow[0:n, hlf * 512:(hlf + 1) * 512], bank[0:n, 0:512], AF.Copy, [bb], [urow.b])
        if dst is not None:
            store(dst, urow[0:30, :], urow.b)
        else:
            store(conv_s[:, 29, :], urow[0:4, :], urow.b)
    prods = M.alloc("prods", [128, 8, 4, 31], F32)
    ysm = M.alloc("ysm", [128, 8, 4], F32)
    for s in range(4):
        OP("dve", lambda e: e.tensor_tensor(prods[:, :, s, :], fulls[:, :, s, :], cw.ap, ALU.mult), [fulls.b, cw.b], [prods.b])
    OP("dve", lambda e: e.reduce_sum(ysm.ap.rearrange("p c s -> p (c s)"), prods.ap.rearrange("p c s j -> p (c s) j"), AX.X),
       [prods.b], [ysm.b])
    for s in range(4):
        OP("dve", lambda e: e.tensor_tensor(yconv[:, :, HALF + s], ysm[:, :, s], cb.ap, ALU.add), [ysm.b, cb.b], [yconv.b])
    M.free(prods, ysm, urow, *ucr.tns, *sgr.tns)
    LNCH = [(0, 343), (343, 343), (686, 342)]
    lnt = Ring([M.alloc("lnt%d" % i, [128, 343], F32) for i in range(3)])
    mus = [M.alloc("mu%d" % i, [128, 343], F32) for i in range(3)]
    rsds = [M.alloc("rsd%d" % i, [128, 343], F32) for i in range(3)]
    for li, (c0, n) in enumerate(LNCH):
        mu, rsd = mus[li], rsds[li]
        b1, bb1 = pf()
        b2, bb2 = pf()
        for c in range(8):
            OP("pe", lambda e: e.matmul(b1[:, 0:n], onesF.ap, yconv[:, c, c0:c0 + n], start=(c == 0), stop=(c == 7)),
               [onesF.b, yconv.b], [bb1], inc=(c == 7))
        for c in range(8):
            ysq = lnt.next()
            act(ysq[:, 0:n], yconv[:, c, c0:c0 + n], AF.Square, [yconv.b], [ysq.b])
            OP("pe", lambda e: e.matmul(b2[:, 0:n], onesF.ap, ysq[:, 0:n], start=(c == 0), stop=(c == 7)),
               [onesF.b, ysq.b], [bb2], inc=True)
        act(mu[:, 0:n], b1[:, 0:n], AF.Copy, [bb1], [mu.b], scale=1.0 / C_CONV)
        OP("dve", lambda e: e.tensor_tensor(rsd[:, 0:n], mu[:, 0:n], mu[:, 0:n], ALU.mult), [mu.b], [rsd.b])
        OP("dve", lambda e: e.scalar_tensor_tensor(rsd[:, 0:n], b2[:, 0:n], 1.0 / C_CONV, rsd[:, 0:n], ALU.mult, ALU.subtract),
           [bb2, rsd.b], [rsd.b])
        OP("dve", lambda e: e.tensor_scalar(rsd[:, 0:n], rsd[:, 0:n], 1.0, EPS, ALU.mult, ALU.add), [rsd.b], [rsd.b])
        OP("act", lambda e: e.sqrt(rsd[:, 0:n], rsd[:, 0:n]), [rsd.b], [rsd.b])
        OP("dve", lambda e: e.reciprocal(rsd[:, 0:n], rsd[:, 0:n]), [rsd.b], [rsd.b])
    items = [(li, c) for li in range(3) for c in range(8)]
    ynr = Ring([M.alloc("yn%d" % i, [128, 343], F32) for i in range(3)])
    t1r = Ring([M.alloc("t1_%d" % i, [128, 343], F32) for i in range(3)])
    sgl = Ring([M.alloc("sgl%d" % i, [128, 343], F32) for i in range(3)])
    lnst = {}

    def ln1(i):
        li, c = items[i]
        c0, n = LNCH[li]
        t1 = t1r.next()
        OP("dve", lambda e: e.tensor_tensor(t1[:, 0:n], yconv[:, c, c0:c0 + n], mus[li][:, 0:n], ALU.subtract), [yconv.b, mus[li].b], [t1.b])
        OP("dve", lambda e: e.tensor_tensor(t1[:, 0:n], t1[:, 0:n], rsds[li][:, 0:n], ALU.mult), [t1.b, rsds[li].b], [t1.b])
        lnst[i] = t1

    def ln2(i):
        li, c = items[i]
        c0, n = LNCH[li]
        t1 = lnst[i]
        yn = ynr.next()
        sg = sgl.next()
        act(yn[:, 0:n], t1[:, 0:n], AF.Identity, [t1.b, lng.b, lnb.b], [yn.b], scale=lng[:, c:c + 1], bias=lnb[:, c:c + 1])
        act(sg[:, 0:n], t1[:, 0:n], AF.Sigmoid, [t1.b, lng.b, lnb.b], [sg.b], scale=lng[:, c:c + 1], bias=lnb[:, c:c + 1])
        lnst[i] = (yn, sg)

    def ln3(i):
        li, c = items[i]
        c0, n = LNCH[li]
        yn, sg = lnst.pop(i)
        OP("dve", lambda e: e.tensor_tensor(branchT[:, c, OWN0 + c0:OWN0 + c0 + n], yn[:, 0:n], sg[:, 0:n], ALU.mult),
           [yn.b, sg.b], [b_conv])
    sw_pipeline(len(items), [ln1, ln2, ln3], order=[2, 0, 1])
    M.free(*ynr.tns)
    mu, rsd = mus[0], rsds[0]
    M.free(yconv, utail, usmp, fulls, *mus, *rsds, *lnt.tns, *t1r.tns, *sgl.tns)
    dump("uconvT", branchT, ap=branchT[:, 0:8, :], dt=BF16)

    kvT.b.dj = True
    tb = {}
    for nm, d_ in (("tb0", tb0_d), ("tb0f", tb0f_d), ("tb1", tb1_d), ("tb1f", tb1f_d), ("tb2", tb2_d)):
        tb[nm] = M.alloc(nm, [128, 4, 256], F32)
        load(tb[nm], tb[nm].ap, d_.rearrange("p (h q) -> p h q", h=4))
    tbs = M.alloc("tbs", [128, 12], F32); load(tbs, tbs.ap, tbs_d)
    qsF = M.alloc("qsF", [128, 3, 4, 4], F32); qsF.b.dj = True
    ksF = M.alloc("ksF", [128, 4, 4], F32); ksF.b.dj = True
    vsF = M.alloc("vsF", [128, 4, 4], F32); vsF.b.dj = True
    qtm = M.alloc("qtm", [4, 1536], F32); qtm.b.dj = True
    qT = M.alloc("qT", [128, 3, T_EXT], BF16)
    kf = M.alloc("kf", [128, NT], F32)
    stg = Ring([M.alloc("stg%d" % i, [128, 8, 128], F32) for i in range(2)])
    srow = Ring([M.alloc("srow%d" % i, [4, 128], F32) for i in range(2)])
    Vtok = M.alloc("Vtok", [128, 37, 128], BF16)
    accOD = M.alloc("accOD", [128, 2, HALF], F32)
    str_ = Ring([M.alloc("st%d" % i, [128, 256], F32) for i in range(3)])
    Pr = Ring([M.alloc("P%d" % i, [128, 256], BF16) for i in range(3)])

    def kv_slab(tag, h, j, out_own, out_smp, smpF):
        slab = ws.next(tag)

        def ev(ci, c0, n, bank, bb):
            act(kf[:, c0 - OWN0:c0 - OWN0 + n], bank[:, 0:n], AF.Copy, [bb], [kf.b])
        kf.b.dj = True
        proj(slab, 16, lambda k, c0, n: hT[:, k, c0:c0 + n], [hT.b], CH3, ev)
        kf.b.dj = False
        act(kvT[:, j, HALF:SEQ], kf[:, 0:HALF], AF.Copy, [kf.b], [kvT.b])
        OP("dve", lambda e: e.tensor_copy(smpF[:, h, :], kf[:, HALF:HALF + 4]), [kf.b], [smpF.b])
        sg_ = stg.next()
        for hlf in range(2):
            bank, bb = pf()
            for t4 in range(4):
                t = hlf * 4 + t4
                OP("pe", lambda e: e.transpose(bank[:, t4 * 128:(t4 + 1) * 128], kf[:, t * 128:(t + 1) * 128], identF.ap),
                   [kf.b, identF.b], [bb], inc=(t4 == 3))
            act(sg_[:, hlf * 4:hlf * 4 + 4, :], bank[:, 0:512].rearrange("p (t d) -> p t d", t=4), AF.Copy, [bb], [sg_.b])
        store(out_own[:, h * 128:(h + 1) * 128].rearrange("(t p) d -> p t d", p=128), sg_.ap, sg_.b)
        bank, bb = pf()
        OP("pe", lambda e: e.transpose(bank[0:4, 0:128], kf[:, HALF:HALF + 4], identF.ap), [kf.b, identF.b], [bb])
        sr = srow.next()
        act(sr.ap, bank[0:4, 0:128], AF.Copy, [bb], [sr.b])
        store(out_smp[:, h * 128:(h + 1) * 128], sr.ap, sr.b)

    for h in range(4):
        kv_slab("k", h, h, wk_own, wk_s, ksF)
        kv_slab("v", h, 4 + h, wv_own, wv_s, vsF)
        for g in range(3):
            slab = ws.next("q")

            def evq(ci, c0, n, bank, bb, g=g, h=h):
                act(qT[:, g, c0:c0 + n], bank[:, 0:n], AF.Copy, [bb], [qT.b])
                if ci == 2:
                    OP("dve", lambda e: e.tensor_copy(qsF[:, g, h, :], bank[:, SMP0 - c0:SMP0 - c0 + 4]), [bb], [qsF.b])
            qT.b.dj = True
            proj(slab, 16, lambda k, c0, n: hT[:, k, c0:c0 + n], [hT.b], CH3, evq)
            qT.b.dj = False
            bank, bb = pf()
            for k in range(16):
                OP("pe", lambda e: e.matmul(bank[0:4, 0:128], hT[:, k, SMP0:SMP0 + 4], slab[:, k, :], start=(k == 0), stop=(k == 15)),
                   [hT.b, slab.b], [bb], inc=(k == 15))
            act(qtm[0:4, (g * 4 + h) * 128:(g * 4 + h + 1) * 128], bank[0:4, 0:128], AF.Copy, [bb], [qtm.b])
        sels = [slice(128 * (7 + i), 128 * (8 + i)) for i in range(9)]
        for r in range(4):
            for nl in (1, 2, 3):
                sels.append(_sl(512 * nl + r, 128, 4))
        for r in range(16):
            sels.append(_sl(r, 128, 16))
        Vtok.b.dj = True
        for i0 in range(0, 37, 8):
            bank, bb = pb()
            cnt = min(8, 37 - i0)
            for ii in range(cnt):
                OP("pe", lambda e: e.transpose(bank[:, ii * 128:(ii + 1) * 128], kvT[:, 4 + h, sels[i0 + ii]], identB.ap),
                   [kvT.b, identB.b], [bb], inc=(ii == cnt - 1))
            act(Vtok[:, i0:i0 + cnt, :], bank[:, 0:cnt * 128].rearrange("p (i d) -> p i d", i=cnt), AF.Copy, [bb], [Vtok.b])
        Vtok.b.dj = False

        def unit(ksel, vidx, qsel, tbl_ap, tbl_buf, nq):
            nk_ = len(ksel)
            bS, bbS = pf()
            for i_, ks_ in enumerate(ksel):
                OP("pe", lambda e: e.matmul(bS[:, i_ * nq:(i_ + 1) * nq], kvT[:, h, ks_], qT[:, qsel[0], qsel[1]], start=True, stop=True),
                   [kvT.b, qT.b], [bbS], inc=(i_ == nk_ - 1))
            stt = str_.next()
            P = Pr.next()
            w_ = nk_ * nq
            OP("dve", lambda e: e.scalar_tensor_tensor(stt[:, 0:w_], bS[:, 0:w_], SCALE, tbl_ap, ALU.mult, ALU.add),
               [bbS, tbl_buf], [stt.b])
            act(P[:, 0:w_], stt[:, 0:w_], AF.Exp, [stt.b], [P.b])
            return P, w_

        units = []

        def mk_band(g_, tbl, ip, ic, qs_, accv, first):
            def s1():
                return unit([sels[ip], sels[ic]], None, (g_, qs_), tbl[:, h, :], tbl.b, 128)

            def s2(P):
                b2, bb2 = pf()
                OP("pe", lambda e: e.matmul(b2[:, 0:128], Vtok[:, ip, :], P[:, 0:128], start=True, stop=False), [Vtok.b, P.b], [bb2], inc=False)
                OP("pe", lambda e: e.matmul(b2[:, 0:128], Vtok[:, ic, :], P[:, 128:256], start=False, stop=True), [Vtok.b, P.b], [bb2], inc=False)
                OP("pe", lambda e: e.matmul(b2[:, 128:256], onesB.ap, P[:, 0:128], start=True, stop=False), [onesB.b, P.b], [bb2], inc=False)
                OP("pe", lambda e: e.matmul(b2[:, 128:256], onesB.ap, P[:, 128:256], start=False, stop=True), [onesB.b, P.b], [bb2])
                src = b2[:, 0:256].rearrange("p (o q) -> p o q", o=2)
                if first:
                    OP("dve", lambda e: e.tensor_copy(accv, src), [bb2], [accOD.b])
                else:
                    OP("dve", lambda e: e.tensor_tensor(accv, accv, src, ALU.add), [bb2, accOD.b], [accOD.b])
            return (s1, s2)
        for i in range(8):
            tbl = tb["tb0f"] if i == 0 else tb["tb0"]
            units.append(mk_band(0, tbl, i, i + 1, slice(OWN0 + 128 * i, OWN0 + 128 * (i + 1)),
                                 accOD[:, :, 128 * i:128 * (i + 1)], True))
        for r in range(4):
            for nl in (2, 3):
                tbl = tb["tb1f"] if nl == 2 else tb["tb1"]
                ip = 9 + r * 3 + (nl - 2)
                units.append(mk_band(1, tbl, ip, ip + 1, _sl(OWN0 + 512 * (nl - 2) + r, 128, 4),
                                     accOD[:, :, _sl(512 * (nl - 2) + r, 128, 4)], False))

        def mk_g2(r0):
            def s1():
                bS, bbS = pf()
                for rr in range(4):
                    r = r0 + rr
                    OP("pe", lambda e: e.matmul(bS[:, rr * 64:(rr + 1) * 64], kvT[:, h, sels[21 + r]], qT[:, 2, _sl(OWN0 + r, 64, 16)],
                                                start=True, stop=True), [kvT.b, qT.b], [bbS], inc=(rr == 3))
                stt = str_.next()
                P = Pr.next()
                OP("dve", lambda e: e.scalar_tensor_tensor(stt.ap, bS[:, 0:256], SCALE, tb["tb2"][:, h, :], ALU.mult, ALU.add),
                   [bbS, tb["tb2"].b], [stt.b])
                act(P.ap, stt.ap, AF.Exp, [stt.b], [P.b])
                return P, 256

            def s2(P):
                b2, bb2 = pf()
                for rr in range(4):
                    OP("pe", lambda e: e.matmul(b2[:, rr * 64:(rr + 1) * 64], Vtok[:, 21 + r0 + rr, :], P[:, rr * 64:(rr + 1) * 64],
                                                start=True, stop=True), [Vtok.b, P.b], [bb2], inc=False)
                OP("pe", lambda e: e.matmul(b2[:, 256:512], onesB.ap, P.ap, start=True, stop=True), [onesB.b, P.b], [bb2])
                for o in range(2):
                    av = accOD[:, o, :].rearrange("p (i r) -> p r i", r=16)[:, r0:r0 + 4, :]
                    OP("dve", lambda e: e.tensor_tensor(av, av, b2[:, o * 256:(o + 1) * 256].rearrange("p (r i) -> p r i", r=4), ALU.add),
                       [bb2, accOD.b], [accOD.b])
            return (s1, s2)
        for r0 in range(0, 16, 4):
            units.append(mk_g2(r0))
        prev = None
        for (s1, s2) in units:
            P, _w = s1()
            if prev is not None:
                prev[0](prev[1])
            prev = (s2, P)
        prev[0](prev[1])
        act(accOD[:, 1, :], accOD[:, 1, :], AF.Ln, [accOD.b], [accOD.b])
        act(accOD[:, 1, :], accOD[:, 1, :], AF.Exp, [accOD.b], [accOD.b], scale=-1.0)
        OP("dve", lambda e: e.tensor_tensor(branchT[:, 8 + h, OWN0:OWN0 + HALF], accOD[:, 0, :], accOD[:, 1, :], ALU.mult),
           [accOD.b], [b_dil])
    M.free(qT, kf, Vtok, accOD, *stg.tns, *srow.tns, *str_.tns, *Pr.tns)
    for nm in ("tb0", "tb0f", "tb1", "tb1f", "tb2"):
        M.free(tb[nm])
    dump("attnT", branchT, ap=branchT[:, 8:12, :], dt=BF16)

    def sample_attend(q_tm, qcols, ngrp, key_rows, val_rows, tbl, out_chunk0, extra):
        ncol = ngrp * 4
        bO, bbO = pf_fixed(4)
        bD, bbD = pf_fixed(5)
        qb = M.alloc("qb", [128, qcols], F32)
        Kr = Ring([M.alloc("Kr%d" % i, [128, 512], F32) for i in range(2)])
        Vr = Ring([M.alloc("Vr%d" % i, [128, 512], F32) for i in range(2)])
        Vb = M.alloc("Vb", [128, ngrp, 512], BF16)
        prod = M.alloc("prod", [128, 512], F32)
        scs = M.alloc("scs", [128, ncol], F32)
        Ps = M.alloc("Ps", [128, ncol], BF16)
        for s in range(4):
            for g in range(qcols // 512):
                bank, bb = pf()
                OP("pe", lambda e: e.matmul(bank[:, 0:512], sel[0:4, s, :], q_tm[0:4, g * 512:(g + 1) * 512], start=True, stop=True),
                   [sel.b, q_tm.b], [bb])
                act(qb[:, g * 512:(g + 1) * 512], bank[:, 0:512], AF.Copy, [bb], [qb.b])
            yield
            for g in range(ngrp):
                K_ = Kr.next()
                S.dma("sp", K_.ap, key_rows(s, g), K_.b, writes=[K_.b])
                V_ = Vr.next()
                S.dma("sp", V_.ap, val_rows(s, g), V_.b, writes=[V_.b])
                qoff = (g * 512) % qcols
                OP("dve", lambda e: e.tensor_tensor(prod.ap, K_.ap, qb[:, qoff:qoff + 512], ALU.mult), [K_.b, qb.b], [prod.b])
                OP("dve", lambda e: e.reduce_sum(scs[:, g * 4:(g + 1) * 4], prod.ap.rearrange("p (h d) -> p h d", h=4), AX.X),
                   [prod.b], [scs.b])
                act(Vb[:, g, :], V_.ap, AF.Copy, [V_.b], [Vb.b])
                yield
            if tbl is not None:
                OP("dve", lambda e: e.scalar_tensor_tensor(scs.ap, scs.ap, SCALE, tbl.ap, ALU.mult, ALU.add), [scs.b, tbl.b], [scs.b])
                act(Ps.ap, scs.ap, AF.Exp, [scs.b], [Ps.b])
            else:
                act(Ps.ap, scs.ap, AF.Exp, [scs.b], [Ps.b], scale=SCALE)
            for h in range(4):
                for g in range(ngrp):
                    OP("pe", lambda e: e.matmul(bO[:, h * 4 + s:h * 4 + s + 1], Vb[:, g, h * 128:(h + 1) * 128], Ps[:, g * 4 + h:g * 4 + h + 1],
                                                start=(g == 0), stop=(g == ngrp - 1)), [Vb.b, Ps.b], [bbO], inc=(g == ngrp - 1))
            OP("pe", lambda e: e.matmul(bD[:, s * ncol:(s + 1) * ncol], onesB.ap, Ps.ap, start=True, stop=True), [onesB.b, Ps.b], [bbD])
            yield
        dsb = M.alloc("dsb", [128, 4 * ncol], F32)
        num = M.alloc("num", [128, 4, 4], F32)
        den = M.alloc("den", [128, 4, 4], F32)
        act(dsb.ap, bD[:, 0:4 * ncol], AF.Copy, [bbD], [dsb.b])
        dv = dsb.ap.rearrange("p (s g h) -> p g h s", s=4, g=ngrp)
        OP("dve", lambda e: e.tensor_tensor(den.ap, dv[:, 0], dv[:, 1], ALU.add), [dsb.b], [den.b])
        for g in range(2, ngrp):
            OP("dve", lambda e: e.tensor_tensor(den.ap, den.ap, dv[:, g], ALU.add), [dsb.b, den.b], [den.b])
        pso = bO[:, 0:16].rearrange("p (h s) -> p h s", h=4)
        if extra is not None:
            e0s, e0b, vs_ = extra
            OP("dve", lambda e: e.tensor_tensor(den.ap, den.ap, e0s, ALU.add), [den.b, e0b], [den.b])
            OP("dve", lambda e: e.tensor_tensor(num.ap, e0s, vs_.ap, ALU.mult), [e0b, vs_.b], [num.b])
            OP("dve", lambda e: e.tensor_tensor(num.ap, num.ap, pso, ALU.add), [num.b, bbO], [num.b])
        else:
            OP("dve", lambda e: e.tensor_copy(num.ap, pso), [bbO], [num.b])
        OP("dve", lambda e: e.reciprocal(den.ap, den.ap), [den.b], [den.b])
        OP("dve", lambda e: e.tensor_tensor(branchT[:, out_chunk0:out_chunk0 + 4, SMP0:SMP0 + 4], num.ap, den.ap, ALU.mult),
           [num.b, den.b], [b_dil if out_chunk0 == 8 else b_xat])
        M.free(qb, Vb, prod, scs, Ps, dsb, num, den, *Kr.tns, *Vr.tns)

    prod0 = M.alloc("prod0", [128, 3, 4, 4], F32)
    E0 = M.alloc("E0", [128, 3, 4, 4], F32)
    E0s = M.alloc("E0s", [128, 4, 4], F32)
    for g in range(3):
        OP("dve", lambda e: e.tensor_tensor(prod0[:, g], qsF[:, g], ksF.ap, ALU.mult), [qsF.b, ksF.b], [prod0.b])
    bank, bb = pf()
    OP("pe", lambda e: e.matmul(bank[:, 0:48], onesF.ap, prod0.ap.rearrange("p g h s -> p (g h s)"), start=True, stop=True),
       [onesF.b, prod0.b], [bb])
    act(E0.ap.rearrange("p g h s -> p (g h s)"), bank[:, 0:48], AF.Exp, [bb], [E0.b], scale=SCALE)
    OP("dve", lambda e: e.tensor_tensor(E0s.ap, E0[:, 0], E0[:, 1], ALU.add), [E0.b], [E0s.b])
    OP("dve", lambda e: e.tensor_tensor(E0s.ap, E0s.ap, E0[:, 2], ALU.add), [E0.b, E0s.b], [E0s.b])

    def ck_rows(src):
        def f(s, g):
            d_ = DIL[g]
            return src[s, _sl(2048 - 128 * d_, 128, d_), :]
        return f
    M.free(kvT)
    pstate["rot"] = [0, 1, 2, 3]
    sgen = sample_attend(qtm, 1536, 3, ck_rows(ck), ck_rows(cv), tbs, 8, (E0s.ap, E0s.b, vsF))

    def tick(k=1):
        for _ in range(k):
            next(sgen, None)

    mkT = M.alloc("mkT", [128, 4, 256], BF16); mkT.b.dj = True
    mvtok = M.alloc("mvtok", [128, 2, 4, 128], BF16); mvtok.b.dj = True
    mkvF = Ring([M.alloc("mkvF%d" % i, [128, 256], F32) for i in range(2)])
    mstg = Ring([M.alloc("mstg%d" % i, [128, 2, 128], F32) for i in range(2)])
    for j in range(8):
        slab = ws.next("memkv")
        mf = mkvF.next()

        def evm(ci, c0, n, bank, bb):
            act(mf.ap, bank[:, 0:256], AF.Copy, [bb], [mf.b])
        proj(slab, 16, lambda k, c0, n: memT[:, k, 0:256], [memT.b], [(0, 256)], evm)
        if j < 4:
            act(mkT[:, j, :], mf.ap, AF.Copy, [mf.b], [mkT.b])
        bank, bb = pf()
        for mt in range(2):
            OP("pe", lambda e: e.transpose(bank[:, mt * 128:(mt + 1) * 128], mf[:, mt * 128:(mt + 1) * 128], identF.ap),
               [mf.b, identF.b], [bb], inc=(mt == 1))
        ms = mstg.next()
        act(ms.ap, bank[:, 0:256].rearrange("p (t d) -> p t d", t=2), AF.Copy, [bb], [ms.b])
        dst = memk if j < 4 else memv
        jj = j % 4
        store(dst[:, jj * 128:(jj + 1) * 128].rearrange("(t p) d -> p t d", p=128), ms.ap, ms.b)
        if j >= 4:
            OP("dve", lambda e: e.tensor_copy(mvtok[:, :, jj, :], ms.ap), [ms.b], [mvtok.b])
        tick(2)
    M.free(memT, *mkvF.tns)
    qxT = M.alloc("qxT", [128, 4, T_EXT], BF16); qxT.b.dj = True
    qxtm = M.alloc("qxtm", [4, 512], F32); qxtm.b.dj = True
    for h in range(4):
        slab = ws.next("qx")

        def evx(ci, c0, n, bank, bb, h=h):
            act(qxT[:, h, c0:c0 + n], bank[:, 0:n], AF.Copy, [bb], [qxT.b])
        proj(slab, 16, lambda k, c0, n: hT[:, k, c0:c0 + n], [hT.b], CH3, evx)
        bank, bb = pf()
        for k in range(16):
            OP("pe", lambda e: e.matmul(bank[0:4, 0:128], hT[:, k, SMP0:SMP0 + 4], slab[:, k, :], start=(k == 0), stop=(k == 15)),
               [hT.b, slab.b], [bb], inc=(k == 15))
        act(qxtm[0:4, h * 128:(h + 1) * 128], bank[0:4, 0:128], AF.Copy, [bb], [qxtm.b])
        tick(2)
    Pm = Ring([M.alloc("Pm%d" % i, [128, 2, 512], BF16) for i in range(3)])
    rD = Ring([M.alloc("rD%d" % i, [128, 512], F32) for i in range(2)])
    def xs1(h, cc):
        cs = slice(OWN0 + 512 * cc, OWN0 + 512 * (cc + 1))
        P = Pm.next()
        for mt in range(2):
            bS, bbS = pf()
            OP("pe", lambda e: e.matmul(bS[:, 0:512], mkT[:, h, mt * 128:(mt + 1) * 128], qxT[:, h, cs], start=True, stop=True),
               [mkT.b, qxT.b], [bbS])
            act(P[:, mt, :], bS[:, 0:512], AF.Exp, [bbS], [P.b], scale=SCALE)
        return P

    def xs2(h, cc, P):
        cs = slice(OWN0 + 512 * cc, OWN0 + 512 * (cc + 1))
        bO, bbO = pf()
        bD, bbD = pf()
        for mt in range(2):
            OP("pe", lambda e: e.matmul(bO[:, 0:512], mvtok[:, mt, h, :], P[:, mt, :], start=(mt == 0), stop=(mt == 1)),
               [mvtok.b, P.b], [bbO], inc=(mt == 1))
        for mt in range(2):
            OP("pe", lambda e: e.matmul(bD[:, 0:512], onesB.ap, P[:, mt, :], start=(mt == 0), stop=(mt == 1)),
               [onesB.b, P.b], [bbD], inc=(mt == 1))
        r_ = rD.next()
        act(r_.ap, bD[:, 0:512], AF.Ln, [bbD], [r_.b])
        act(r_.ap, r_.ap, AF.Exp, [r_.b], [r_.b], scale=-1.0)
        OP("dve", lambda e: e.tensor_tensor(branchT[:, 12 + h, cs], bO[:, 0:512], r_.ap, ALU.mult), [bbO, r_.b], [b_xat])
    xprev = None
    for h in range(4):
        for cc in range(2):
            P = xs1(h, cc)
            if xprev is not None:
                xs2(*xprev)
            xprev = (h, cc, P)
            tick(1)
    xs2(*xprev)
    for _ in sgen:
        pass
    M.free(prod0, E0, E0s, qsF, ksF, vsF, qtm, tbs)
    M.free(*Pm.tns, *rD.tns, mkT, mvtok, *mstg.tns)
    xgen = sample_attend(qxtm, 512, 2, lambda s, g: cmk[s, g * 128:(g + 1) * 128, :], lambda s, g: cmv[s, g * 128:(g + 1) * 128, :],
                         None, 12, None)
    xstate = {"live": True}

    def xtick(k=1, drain=False):
        if not xstate["live"]:
            return
        for _ in range(10 ** 6 if drain else k):
            try:
                next(xgen)
            except StopIteration:
                xstate["live"] = False
                pstate["rot"] = [0, 1, 2, 3, 4, 5]
                M.free(qxT, qxtm)
                return
    dump("branchT", branchT, dt=BF16)

    mergedT = M.alloc("mergedT", [128, NCH, NT], BF16, top=True); mergedT.b.dj = True
    sgH = Ring([M.alloc("sgH%d" % i, [128, 343], F32) for i in range(6)])
    mH = Ring([M.alloc("mH%d" % i, [128, 343], F32) for i in range(6)])
    tH = Ring([M.alloc("tH%d" % i, [128, 343], F32) for i in range(3)])
    br_off = (0, 8, 12)
    br_nk = (8, 4, 4)
    br_buf = (b_conv, b_dil, b_xat)
    for c in range(16):
        mcur = [mH.next() for _ in range(3)]
        for br in range(3):
            slab_g = ws.next("gate")
            sgs = []

            def evg(ci, c0, n, bank, bb, br=br, c=c):
                sg = sgH.next()
                sgs.append(sg)
                act(sg[:, 0:n], bank[:, 0:n], AF.Sigmoid, [bb, bg.b], [sg.b], bias=bg[:, br * 16 + c:br * 16 + c + 1])
                xtick(2)
            proj(slab_g, 16, lambda k, c0, n: hT[:, k, c0:c0 + n], [hT.b], CH3, evg)
            if br == 2:
                xtick(drain=True)
            slab_w = ws.next("bw")

            def evw(ci, c0, n, bank, bb, br=br, c=c):
                sg = sgs[ci]
                m_ = mcur[ci]
                if br == 0:
                    OP("dve", lambda e: e.tensor_tensor(m_[:, 0:n], bank[:, 0:n], sg[:, 0:n], ALU.mult), [bb, sg.b], [m_.b])
                else:
                    t_ = tH.next()
                    OP("dve", lambda e: e.tensor_tensor(t_[:, 0:n], bank[:, 0:n], sg[:, 0:n], ALU.mult), [bb, sg.b], [t_.b])
                    if br == 1:
                        OP("dve", lambda e: e.tensor_tensor(m_[:, 0:n], m_[:, 0:n], t_[:, 0:n], ALU.add), [m_.b, t_.b], [m_.b])
                    else:
                        OP("dve", lambda e: e.tensor_tensor(mergedT[:, c, c0 - OWN0:c0 - OWN0 + n], m_[:, 0:n], t_[:, 0:n], ALU.add),
                           [m_.b, t_.b], [mergedT.b])
            o_ = br_off[br]
            proj(slab_w, br_nk[br], lambda k, c0, n: branchT[:, o_ + k, c0:c0 + n], [br_buf[br]], CH3, evw)
    M.free(hT, branchT, *sgH.tns, *mH.tns, *tH.tns)
    dump("mergedT", mergedT, dt=BF16)

    zT = M.alloc("zT", [128, NCH, NT], F32, top=True)
    z_bufs = [[zT.buf("z%d_%d" % (c, ci)) for ci in range(3)] for c in range(16)]
    all_z = [b for row in z_bufs for b in row]
    ssb_ap, ssb_buf = pf_fixed(5)

    def ss_matmuls(zsq):
        for t in range(9):
            n = 128 if t < 8 else 4
            for c in range(16):
                OP("pe", lambda e: e.matmul(ssb_ap[0:n, t:t + 1], zsq[:, c, t * 128:t * 128 + n], onesB[:, 0:1],
                                            start=(c == 0), stop=(c == 15)), [zsq.b, onesB.b], [ssb_buf], inc=(c == 15))

    zsq = M.alloc("zsq", [128, NCH, NT], BF16); zsq.b.dj = True
    pstate["rot"] = [0, 1, 2, 3, 4]
    for c in range(16):
        slab = ws.next("wout")

        def evz(ci, c0, n, bank, bb, c=c):
            act(zsq[:, c, c0 - OWN0:c0 - OWN0 + n], bank[:, 0:n], AF.Square, [bb], [zsq.b])
            act(zT[:, c, c0 - OWN0:c0 - OWN0 + n], bank[:, 0:n], AF.Copy, [bb, gpm.b], [z_bufs[c][ci]], scale=gpm[:, c:c + 1])
        proj(slab, 16, lambda k, c0, n: mergedT[:, k, c0 - OWN0:c0 - OWN0 + n], [mergedT.b], CH3, evz)
    zsq.b.dj = False
    ss_matmuls(zsq)
    M.free(mergedT, zsq)
    dump("zT1", zT)
    h2T = M.alloc("h2T", [128, NCH, NT], BF16, top=True); h2T.b.dj = True
    io_alloc(with_z=True)

    def post(res_src, res_is_dram_in, out_dst, g2_d, make_h2):
        if make_h2:
            load(io["gB"], io["gB"].ap, g2_d.partition_broadcast(128))
        rs1 = M.alloc("rs1", [128, 16], F32)
        for (pp, c0_, c1_) in ((128, 0, 8), (4, 8, 9)):
            OP("dve", lambda e: e.tensor_scalar(rs1[0:pp, c0_:c1_], ssb_ap[0:pp, c0_:c1_], 1.0 / D, EPS, ALU.mult, ALU.add),
               [ssb_buf], [rs1.b])
            OP("act", lambda e: e.sqrt(rs1[0:pp, c0_:c1_], rs1[0:pp, c0_:c1_]), [rs1.b], [rs1.b])
            OP("dve", lambda e: e.reciprocal(rs1[0:pp, c0_:c1_], rs1[0:pp, c0_:c1_]), [rs1.b], [rs1.b])
        st8 = {}

        def stL(t):
            n = 128 if t < 8 else 4
            xt = io["x"].next()
            S.dma("act", xt[0:n, :], res_src(t, n), xt.b, reads=([] if res_is_dram_in else [x1s_buf]), writes=[xt.b])
            st8[("x", t)] = xt

        def stP(t):
            n = 128 if t < 8 else 4
            xt = st8.pop(("x", t))
            zt = io["z"].next()
            for hlf in range(2):
                pbufs = [pf_bufs[2 * hlf], pf_bufs[2 * hlf + 1]]
                for c8 in range(8):
                    c = hlf * 8 + c8
                    OP("pe", lambda e: e.transpose(PSF[0:n, hlf * 1024 + c8 * 128:hlf * 1024 + (c8 + 1) * 128],
                                                   zT[:, c, t * 128:t * 128 + n], identF.ap),
                       all_z + [identF.b], [pbufs[c8 // 4]], inc=(c8 % 4 == 3))
                act(zt[0:n, hlf * 1024:(hlf + 1) * 1024], PSF[0:n, hlf * 1024:(hlf + 1) * 1024], AF.Copy, pbufs, [zt.b])
            st8[t] = (xt, zt, n)

        def stQ(t):
            xt, zt, n = st8.pop(t)
            OP("dve", lambda e: e.scalar_tensor_tensor(zt[0:n, :], zt[0:n, :], rs1[0:n, t:t + 1], xt[0:n, :], ALU.mult, ALU.add),
               [zt.b, rs1.b, xt.b], [zt.b])
            store(out_dst(t, n), zt[0:n, :], zt.b, final=not make_h2, dram_buf=(x1s_buf if make_h2 else None))
            if make_h2:
                st8[("n", t)] = norm_a(None, n, xt=zt)

        def stR(t):
            norm_r(st8[("n", t)])

        def stS1(t):
            norm_b1(st8[("n", t)], io["gB"])

        def stS2(t):
            n = 128 if t < 8 else 4
            norm_b2(st8.pop(("n", t)), lambda hlf: h2T[:, 8 * hlf:8 * hlf + 8, t * 128:t * 128 + n], h2T.b)
        pstate["rot"] = [4]
        if make_h2:
            sw_pipeline(9, [stL, stP, stQ, stR, stS1, stS2], order=[0, 3, 2, 1, 4, 5])
        else:
            sw_pipeline(9, [stL, stP, stQ], order=[0, 2, 1])
        pstate["rot"] = [0, 1, 2, 3, 4, 5]
        M.free(rs1)

    def res1(t, n):
        return x_own[t * 128:(t + 1) * 128, :] if t < 8 else x_smp

    def x1_ap(t, n):
        return x1s[t * 128:t * 128 + n, :]
    post(res1, True, x1_ap, g_pre_ffn, True)
    io_free()
    dump("h2T", h2T, dt=BF16)

    x1_bufs = []
    actT = M.alloc("actT", [128, 11, NT], BF16)
    sgF = Ring([M.alloc("sgF%d" % i, [128, 343], F32) for i in range(3)])
    tF = Ring([M.alloc("tF%d" % i, [128, 343], F32) for i in range(3)])
    zsq2 = M.alloc("zsq2", [128, NCH, NT], BF16); zsq2.b.dj = True
    pstate["rot"] = [0, 1, 2, 3, 4]
    for p in range(4):
        actT.b.dj = True
        for f in range(11):
            slab_g = ws.next("fg")
            slab_u = ws.next("fu")
            ts_ = []

            def evfg(ci, c0, n, bank, bb):
                sg = sgF.next()
                t_ = tF.next()
                ts_.append(t_)
                act(sg[:, 0:n], bank[:, 0:n], AF.Sigmoid, [bb], [sg.b])
                OP("dve", lambda e: e.tensor_tensor(t_[:, 0:n], bank[:, 0:n], sg[:, 0:n], ALU.mult), [bb, sg.b], [t_.b])
            proj(slab_g, 16, lambda k, c0, n: h2T[:, k, c0 - OWN0:c0 - OWN0 + n], [h2T.b], CH3, evfg)

            def evfu(ci, c0, n, bank, bb, f=f):
                t_ = ts_[ci]
                OP("dve", lambda e: e.tensor_tensor(actT[:, f, c0 - OWN0:c0 - OWN0 + n], bank[:, 0:n], t_[:, 0:n], ALU.mult),
                   [bb, t_.b], [actT.b])
            proj(slab_u, 16, lambda k, c0, n: h2T[:, k, c0 - OWN0:c0 - OWN0 + n], [h2T.b], CH3, evfu)
        actT.b.dj = False
        for c in range(16):
            slab = ws.next("fd")

            def evd(ci, c0, n, bank, bb, c=c, p=p):
                zb_ = z_bufs[c][ci]
                zv = zT[:, c, c0 - OWN0:c0 - OWN0 + n]
                if p == 0:
                    act(zv, bank[:, 0:n], AF.Copy, [bb], [zb_])
                else:
                    OP("dve", lambda e: e.tensor_tensor(zv, zv, bank[:, 0:n], ALU.add), [bb, zb_], [zb_])
                if p == 3:
                    act(zsq2[:, c, c0 - OWN0:c0 - OWN0 + n], zv, AF.Square, [zb_], [zsq2.b])
                    act(zv, zv, AF.Copy, [zb_, gpf.b], [zb_], scale=gpf[:, c:c + 1])
            proj(slab, 11, lambda k, c0, n: actT[:, k, c0 - OWN0:c0 - OWN0 + n], [actT.b], CH3, evd)
    zsq2.b.dj = False
    ss_matmuls(zsq2)
    M.free(actT, h2T, zsq2, *sgF.tns, *tF.tns)
    io_alloc(with_z=True)
    dump("zT2", zT)

    def y_ap(t, n):
        return y_own[t * 128:(t + 1) * 128, :] if t < 8 else y_smp
    post(lambda t, n: x1s[t * 128:t * 128 + n, :], False, y_ap, None, False)

    S.finish(out_bufs, "sp")
    assert ws.consumed == len(ws.plan), (ws.consumed, len(ws.plan))
    info = dict(n_wait=S.n_wait, n_ins=dict(S.n_ins), nsem=S.nsem, peak=M.peak - SB_BASE)
    return nc, info


def _slopes():
    i = np.arange(1, 13, dtype=np.float32)
    return np.exp2(-8.0 * i / 12.0).astype(np.float32).reshape(3, 4)


def _tables(hf):
    sl = _slopes()
    kp = np.arange(128, dtype=np.float32)[:, None]
    qi = np.arange(128, dtype=np.float32)[None, :]

    def band(g, first):
        d_ = float(DIL[g])
        out = np.empty((128, 4, 256), np.float32)
        for h in range(4):
            dist_p = qi + 128.0 - kp
            prev = np.where(dist_p <= 128.0, -sl[g, h] * dist_p * d_, NEGB)
            if first and hf == 0:
                prev = np.full_like(prev, NEGB)
            dist_c = qi - kp
            cur = np.where(dist_c >= 0.0, -sl[g, h] * dist_c * d_, NEGB)
            out[:, h, 0:128] = prev
            out[:, h, 128:256] = cur
        return np.ascontiguousarray(out.reshape(128, 1024))
    t2 = np.empty((128, 4, 4, 64), np.float32)
    qm = 64.0 + np.arange(64, dtype=np.float32)[None, :]
    dist = qm - kp
    valid = dist >= 0.0
    if hf == 0:
        valid = valid & (kp >= 64.0)
    for h in range(4):
        t2[:, h, :, :] = np.where(valid, -sl[2, h] * 16.0 * dist, NEGB)[:, None, :]
    ts = np.empty((128, 12), np.float32)
    m = np.arange(128, dtype=np.float32)
    for g in range(3):
        for h in range(4):
            ts[:, g * 4 + h] = -sl[g, h] * (128.0 - m) * float(DIL[g])
    return dict(tb0=band(0, False), tb0f=band(0, True), tb1=band(1, False), tb1f=band(1, True),
                tb2=np.ascontiguousarray(t2.reshape(128, 1024)), tbs=ts)


_CACHE = {}


def _fm(v, nch):
    return np.ascontiguousarray(np.asarray(v, np.float32).reshape(nch, 128).T)


def kernel(x_prompt, x_sample, cache_win_k, cache_win_v, state_conv, cache_mem_k, cache_mem_v, mem_prompt,
           g_pre_mix, w_in, b_gate, conv_w, conv_b, conv_ln_g, conv_ln_b, w_conv_out, w_dil_o,
           g_mem, w_mem_kv, w_x_o, w_out, g_post_mix, g_pre_ffn, w_ffn_gate, w_ffn_up, w_ffn_down, g_post_ffn,
           _dbg=()):
    f32 = lambda a: np.ascontiguousarray(np.asarray(a, dtype=np.float32))
    x_prompt = f32(x_prompt); x_sample = f32(x_sample)
    key = tuple(_dbg)
    if key not in _CACHE:
        _CACHE[key] = build_program(dbg=_dbg)
    nc, info = _CACHE[key]
    common = {
        "w_in": f32(w_in[0]), "w_conv_out": f32(w_conv_out[0]), "w_dil_o": f32(w_dil_o[0]), "w_mem_kv": f32(w_mem_kv[0]),
        "w_x_o": f32(w_x_o[0]), "w_out": f32(w_out[0]), "w_gate": f32(w_ffn_gate[0]), "w_up": f32(w_ffn_up[0]),
        "w_down": f32(w_ffn_down[0]),
        "g_pre_mix": f32(g_pre_mix[0:1]), "g_mem": f32(g_mem[0:1]), "g_post_mix": f32(g_post_mix[0:1]),
        "g_pre_ffn": f32(g_pre_ffn[0:1]), "g_post_ffn": f32(g_post_ffn[0:1]),
        "bgT": _fm(b_gate[0], 48), "gpmT": _fm(g_post_mix[0], 16), "gpfT": _fm(g_post_ffn[0], 16),
        "cwT": np.ascontiguousarray(np.asarray(conv_w[0], np.float32).T.reshape(8, 128, 31).transpose(1, 0, 2).reshape(128, 248)),
        "cbT": _fm(conv_b[0], 8), "lngT": _fm(conv_ln_g[0], 8), "lnbT": _fm(conv_ln_b[0], 8),
        "ident": np.eye(128, dtype=np.float32),
        "sel": np.ascontiguousarray(np.repeat(np.eye(4, dtype=np.float32)[:, :, None], 128, axis=2).reshape(4, 512)),
    }
    ck_all = np.asarray(cache_win_k, np.float32)[0].reshape(32, 2048, 512)
    cv_all = np.asarray(cache_win_v, np.float32)[0].reshape(32, 2048, 512)
    cmk_all = np.asarray(cache_mem_k, np.float32)[0].reshape(32, 256, 512)
    cmv_all = np.asarray(cache_mem_v, np.float32)[0].reshape(32, 256, 512)
    sconv_all = np.asarray(state_conv, np.float32)[0]
    mem_all = np.asarray(mem_prompt, np.float32)
    zeros_ctx = np.zeros((HALF, D), np.float32)
    in_maps = []
    for i in range(8):
        b, hf = i // 2, i % 2
        m = dict(common)
        m["x_own"] = f32(x_prompt[b, hf * HALF:(hf + 1) * HALF])
        m["x_ctx"] = f32(x_prompt[b, 0:HALF]) if hf == 1 else zeros_ctx
        m["x_smp"] = f32(x_sample[4 * i:4 * i + 4, 0])
        m["ck"] = f32(ck_all[4 * i:4 * i + 4]); m["cv"] = f32(cv_all[4 * i:4 * i + 4])
        m["sconv"] = f32(sconv_all[4 * i:4 * i + 4])
        m["cmk"] = f32(cmk_all[4 * i:4 * i + 4]); m["cmv"] = f32(cmv_all[4 * i:4 * i + 4])
        m["mem"] = f32(mem_all[b])
        m.update(_tables(hf))
        in_maps.append(m)
    res = run_bass_kernel_spmd(nc, in_maps, core_ids=list(range(8)))
    R = res.results
    kernel.last_results = R
    y_prompt = np.empty((4, SEQ, D), np.float32)
    wk = np.empty((1, 4, SEQ, 4, 128), np.float32)
    wv = np.empty((1, 4, SEQ, 4, 128), np.float32)
    conv_pr = np.empty((1, 4, 30, C_CONV), np.float32)
    mk = np.empty((1, 4, 256, 4, 128), np.float32)
    mv = np.empty((1, 4, 256, 4, 128), np.float32)
    y_sample = np.empty((32, 1, D), np.float32)
    wks = np.empty((1, 32, 1, 4, 128), np.float32)
    wvs = np.empty((1, 32, 1, 4, 128), np.float32)
    conv_sm = np.empty((1, 32, 30, C_CONV), np.float32)
    for i in range(8):
        b, hf = i // 2, i % 2
        r = R[i]
        y_prompt[b, hf * HALF:(hf + 1) * HALF] = r["y_own"]
        wk[0, b, hf * HALF:(hf + 1) * HALF] = r["wk_own"].reshape(HALF, 4, 128)
        wv[0, b, hf * HALF:(hf + 1) * HALF] = r["wv_own"].reshape(HALF, 4, 128)
        if hf == 1:
            conv_pr[0, b] = r["conv_p"]
        else:
            mk[0, b] = r["memk"].reshape(256, 4, 128)
            mv[0, b] = r["memv"].reshape(256, 4, 128)
        y_sample[4 * i:4 * i + 4, 0] = r["y_smp"]
        wks[0, 4 * i:4 * i + 4, 0] = r["wk_s"].reshape(4, 4, 128)
        wvs[0, 4 * i:4 * i + 4, 0] = r["wv_s"].reshape(4, 4, 128)
        conv_sm[0, 4 * i:4 * i + 4] = r["conv_s"]
    return (y_prompt, y_sample, wk, wv, conv_pr, mk, mv, wks, wvs, conv_sm)
```

```python
import os
import math
from contextlib import ExitStack
import numpy as np
import concourse.bass as bass
import concourse.mybir as mybir
from concourse.bass_utils import run_bass_kernel_spmd

F32 = mybir.dt.float32
BF16 = mybir.dt.bfloat16
AF = mybir.ActivationFunctionType
ALU = mybir.AluOpType
AX = mybir.AxisListType

D = 2048
NCH = 16
SEQ = 2048
HALF = 1024
C_CONV = 1024
N_IN = 11264
D_FF = 5632
NFF = 44
EPS = 1e-6
SCALE = 128 ** -0.5
NEGB = -30000.0
T_EXT = 1060
OWN0 = 32
SMP0 = 1056
NT = 1028
CH3 = [(32, 343), (375, 343), (718, 342)]
CH3U = [(0, 354), (354, 353), (707, 353)]
COL_A, COL_B, COL_Q, COL_K, COL_V, COL_QX, COL_G = 0, 1024, 2048, 3584, 4096, 4608, 5120
SB_BASE = 16512
SB_TOP = 229312
DIL = (1, 4, 16)
DEBUG = bool(os.environ.get("MK_DEBUG"))


class Buf:
    __slots__ = ("name", "w", "r", "lsem", "lcnt", "dj")

    def __init__(self, name="", dj=False):
        self.name = name
        self.w = {}
        self.r = []
        self.lsem = None
        self.lcnt = 0
        self.dj = dj


class Sched:
    ENG = ("pe", "dve", "act", "pool", "sp")

    def __init__(self, nc, stack):
        self.nc = nc
        self.stack = stack
        self.e = {"pe": nc.tensor, "dve": nc.vector, "act": nc.scalar, "pool": nc.gpsimd, "sp": nc.sync}
        self.sems = {}
        self.cnt = {}
        for k in self.ENG:
            self.sems[k] = stack.enter_context(nc.semaphore("s_" + k))
            self.cnt[k] = 0
        self.seen = {k: {} for k in self.ENG}
        self.nsem = 5
        self.dsem_pool = []
        self.n_wait = 0
        self.n_ins = {k: 0 for k in self.ENG}

    def _dsem(self, buf):
        if buf.lsem is None:
            key = "d%d" % self.nsem
            self.sems[key] = self.stack.enter_context(self.nc.semaphore(key))
            self.nsem += 1
            buf.lsem = key
        return buf.lsem

    def share_dsem(self, src, dst):
        dst.lsem = src.lsem
        dst.lcnt = src.lcnt

    def _wait(self, e, deps):
        best = {}
        for (k, v) in deps:
            if k == "pe" and e == "pe":
                continue
            if v > best.get(k, 0):
                best[k] = v
        for k, v in best.items():
            if self.seen[e].get(k, 0) >= v:
                continue
            self.e[e].wait_ge(self.sems[k], v)
            self.seen[e][k] = v
            self.n_wait += 1

    @staticmethod
    def _deps(reads, writes):
        deps = set()
        for b in reads:
            deps.update(b.w.items())
        for b in writes:
            if not b.dj:
                deps.update(b.w.items())
            deps.update(b.r)
        return deps

    @staticmethod
    def _mark(reads, writes, tick):
        for b in reads:
            if len(b.r) > 48:
                b.r = Sched._compress(b.r)
            b.r.append(tick)
        for b in writes:
            if b.dj:
                b.w[tick[0]] = max(b.w.get(tick[0], 0), tick[1])
            else:
                b.w = {tick[0]: tick[1]}
                b.r = []

    def op(self, e, fn, reads=(), writes=(), inc=True):
        self._wait(e, self._deps(reads, writes))
        ins = fn(self.e[e])
        self.n_ins[e] += 1
        if inc:
            self.cnt[e] += 1
            ins.then_inc(self.sems[e], 1)
            tick = (e, self.cnt[e])
        else:
            tick = (e, self.cnt[e] + 1)
        self._mark(reads, writes, tick)
        return ins

    @staticmethod
    def _compress(ticks):
        best = {}
        for (k, v) in ticks:
            if v > best.get(k, 0):
                best[k] = v
        return list(best.items())

    def dma(self, q, out, in_, sb, reads=(), writes=(), **kw):
        self._wait(q, self._deps(reads, writes))
        key = self._dsem(sb)
        ins = self.e[q].dma_start(out=out, in_=in_, **kw)
        self.n_ins[q] += 1
        sb.lcnt += 16
        ins.then_inc(self.sems[key], 16)
        tick = (key, sb.lcnt)
        self._mark(reads, writes, tick)
        return ins

    def finish(self, bufs, e="sp"):
        deps = set()
        for b in bufs:
            deps.update(b.w.items())
            deps.update(b.r)
        self._wait(e, deps)


class Tn:
    def __init__(self, ap, off, end, inherit, name):
        self.ap = ap
        self.off = off
        self.end = end
        self.inherit = inherit
        self.bufs = []
        self.name = name
        self.b = self.buf(name)

    def buf(self, name=None, dj=False):
        b = Buf(name or self.name, dj)
        b.r = list(self.inherit)
        self.bufs.append(b)
        return b

    def __getitem__(self, k):
        return self.ap[k]


class Mem:
    def __init__(self, arena, base, top):
        self.arena = arena
        self.base = base
        self.free_list = [(base, top)]
        self.retired = []
        self.peak = 0

    def view(self, off, shape, dt):
        n = int(np.prod(shape[1:]))
        sz = 4 if dt == F32 else 2
        e0 = (off - self.base) // 2
        ap = self.arena[0:shape[0], e0:e0 + n * sz // 2]
        if dt == F32:
            ap = ap.bitcast(F32)
        if len(shape) == 3:
            ap = ap.rearrange("p (a b) -> p a b", a=shape[1])
        elif len(shape) == 4:
            ap = ap.rearrange("p (a b c) -> p a b c", a=shape[1], b=shape[2])
        return ap

    def alloc(self, name, shape, dt, top=False):
        n = int(np.prod(shape[1:])) * (4 if dt == F32 else 2)
        n = (n + 63) // 64 * 64
        order = list(enumerate(self.free_list))
        if top:
            order = order[::-1]
        for i, (s, e) in order:
            if e - s >= n:
                if top:
                    off = e - n
                    if e - s == n:
                        self.free_list.pop(i)
                    else:
                        self.free_list[i] = (s, e - n)
                else:
                    off = s
                    if e - s == n:
                        self.free_list.pop(i)
                    else:
                        self.free_list[i] = (s + n, e)
                break
        else:
            raise RuntimeError("SBUF arena OOM for %s (%d bytes); free=%s" % (name, n, self.free_list))
        end = off + n
        self.peak = max(self.peak, end)
        ticks = set()
        keep = []
        for (rs, re, rt) in self.retired:
            if rs < end and re > off:
                ticks.update(rt)
                if rs >= off and re <= end:
                    continue
            keep.append((rs, re, rt))
        self.retired = keep
        return Tn(self.view(off, shape, dt), off, end, list(Sched._compress(ticks)), name)

    def free(self, *tns):
        for tn in tns:
            ticks = set(tn.inherit)
            for b in tn.bufs:
                ticks.update(b.w.items())
                ticks.update(b.r)
            self.retired.append((tn.off, tn.end, Sched._compress(ticks)))
            self.free_list.append((tn.off, tn.end))
            self.free_list.sort()
            merged = []
            for s, e in self.free_list:
                if merged and merged[-1][1] == s:
                    merged[-1] = (merged[-1][0], e)
                else:
                    merged.append((s, e))
            self.free_list = merged


class Ring:
    def __init__(self, tns):
        self.tns = tns
        self.i = 0

    def next(self):
        t = self.tns[self.i % len(self.tns)]
        self.i += 1
        return t


class WStream:
    def __init__(self, S, mem, ns):
        self.S = S
        self.slots = [mem.alloc("wslot%d" % i, [128, 16, 128], BF16) for i in range(ns)]
        self.plan = []
        self.issued = 0
        self.consumed = 0

    def add(self, tag, w, r0, nk, c0):
        self.plan.append((tag, w, r0, nk, c0))

    def _issue(self, i):
        tag, w, r0, nk, c0 = self.plan[i]
        slot = self.slots[i % len(self.slots)]
        src = w[r0:r0 + nk * 128, c0:c0 + 128].rearrange("(k p) n -> p k n", p=128)
        self.S.dma("pool", slot[:, 0:nk, :], src, slot.b, writes=[slot.b])

    def next(self, tag):
        i = self.consumed
        assert self.plan[i][0] == tag, (self.plan[i][0], tag)
        lim = min(len(self.plan), i + len(self.slots) - 1)
        while self.issued < lim:
            self._issue(self.issued)
            self.issued += 1
        self.consumed += 1
        return self.slots[i % len(self.slots)]


def _sl(start, count, step=1):
    return slice(start, start + step * (count - 1) + 1, step)


def build_program(dbg=()):
    nc = bass.Bass("TRN2", target_bir_lowering=False)

    def din(name, shape, dt=F32):
        return nc.dram_tensor(name, list(shape), dt, kind="ExternalInput").ap()

    def dout(name, shape, dt=F32):
        return nc.dram_tensor(name, list(shape), dt, kind="ExternalOutput").ap()

    x_own = din("x_own", [HALF, D]); x_ctx = din("x_ctx", [HALF, D]); x_smp = din("x_smp", [4, D])
    ck = din("ck", [4, 2048, 512]); cv = din("cv", [4, 2048, 512])
    sconv = din("sconv", [4, 30, C_CONV])
    cmk = din("cmk", [4, 256, 512]); cmv = din("cmv", [4, 256, 512])
    mem = din("mem", [256, D])
    w_in = din("w_in", [D, N_IN]); w_conv_out = din("w_conv_out", [C_CONV, D]); w_dil_o = din("w_dil_o", [512, D])
    w_mem_kv = din("w_mem_kv", [D, 1024]); w_x_o = din("w_x_o", [512, D]); w_out = din("w_out", [D, D])
    w_gate = din("w_gate", [D, D_FF]); w_up = din("w_up", [D, D_FF]); w_down = din("w_down", [D_FF, D])
    g_pre_mix = din("g_pre_mix", [1, D]); g_mem = din("g_mem", [1, D]); g_post_mix = din("g_post_mix", [1, D])
    g_pre_ffn = din("g_pre_ffn", [1, D]); g_post_ffn = din("g_post_ffn", [1, D])
    gpmT_d = din("gpmT", [128, 16]); gpfT_d = din("gpfT", [128, 16])
    bgT_d = din("bgT", [128, 48]); cwT_d = din("cwT", [128, 8 * 31]); cbT_d = din("cbT", [128, 8])
    lngT_d = din("lngT", [128, 8]); lnbT_d = din("lnbT", [128, 8])
    ident_d = din("ident", [128, 128]); sel_d = din("sel", [4, 512])
    tb0_d = din("tb0", [128, 1024]); tb0f_d = din("tb0f", [128, 1024])
    tb1_d = din("tb1", [128, 1024]); tb1f_d = din("tb1f", [128, 1024])
    tb2_d = din("tb2", [128, 1024]); tbs_d = din("tbs", [128, 12])

    y_own = dout("y_own", [HALF, D]); y_smp = dout("y_smp", [4, D])
    wk_own = dout("wk_own", [HALF, 512]); wv_own = dout("wv_own", [HALF, 512])
    conv_p = dout("conv_p", [30, C_CONV])
    memk = dout("memk", [256, 512]); memv = dout("memv", [256, 512])
    wk_s = dout("wk_s", [4, 512]); wv_s = dout("wv_s", [4, 512])
    conv_s = dout("conv_s", [4, 30, C_CONV])
    x1s = nc.dram_tensor("x1s", [HALF + 4, D], F32).ap()

    st = ExitStack()
    S = Sched(nc, st)
    arena = nc.alloc_sbuf_tensor_at("arena", [128, (SB_TOP - SB_BASE) // 2], BF16, offset=SB_BASE)
    M = Mem(arena, SB_BASE, SB_TOP)
    PSF = st.enter_context(nc.psum_tensor("PSF", [128, 3072], F32))
    PSB = st.enter_context(nc.psum_tensor("PSB", [128, 2048], BF16))
    pf_bufs = [Buf("pf%d" % i) for i in range(6)]
    pb_bufs = [Buf("pb%d" % i) for i in range(2)]
    pstate = {"f": 0, "b": 0, "rot": [0, 1, 2, 3, 4, 5], "pair": 0}
    out_bufs = []
    dbg_outs = {}

    def pf():
        rot = pstate["rot"]
        i = rot[pstate["f"] % len(rot)]
        pstate["f"] += 1
        return PSF[:, i * 512:(i + 1) * 512], pf_bufs[i]

    def pf_fixed(i):
        return PSF[:, i * 512:(i + 1) * 512], pf_bufs[i]

    def pb():
        i = pstate["b"] % 2
        pstate["b"] += 1
        return PSB[:, i * 1024:(i + 1) * 1024], pb_bufs[i]

    def OP(e, fn, R=(), W=(), inc=True):
        return S.op(e, fn, reads=R, writes=W, inc=inc)

    def act(out, in_, func, R, W, **kw):
        return OP("act", lambda e: e.activation(out, in_, func, **kw), R, W)

    def load(dst_tn, dst_ap, src, q="sp"):
        S.dma(q, dst_ap, src, dst_tn.b, writes=[dst_tn.b])

    x1s_buf = Buf("x1s", dj=True)

    def store(dst, src_ap, src_buf, final=True, dram_buf=None):
        S.dma("sp", dst, src_ap, src_buf, reads=[src_buf], writes=([dram_buf] if dram_buf is not None else []))
        if final and src_buf not in out_bufs:
            out_bufs.append(src_buf)

    def dump(name, tn, ap=None, dt=F32, shape=None):
        if name not in dbg:
            return
        ap = tn.ap if ap is None else ap
        shape = list(ap.shape) if shape is None else shape
        d = dout("dbg_" + name, shape, dt)
        S.dma("sp", d, ap, tn.b, reads=[tn.b])
        out_bufs.append(tn.b)

    ws = WStream(S, M, 5)
    for j in range(8):
        ws.add("ctxkv", w_in, 0, 16, COL_K + j * 128)
    for c in range(8):
        ws.add("a", w_in, 0, 16, COL_A + c * 128)
        ws.add("b", w_in, 0, 16, COL_B + c * 128)
    for h in range(4):
        ws.add("k", w_in, 0, 16, COL_K + h * 128)
        ws.add("v", w_in, 0, 16, COL_V + h * 128)
        for g in range(3):
            ws.add("q", w_in, 0, 16, COL_Q + (g * 4 + h) * 128)
    for j in range(8):
        ws.add("memkv", w_mem_kv, 0, 16, j * 128)
    for h in range(4):
        ws.add("qx", w_in, 0, 16, COL_QX + h * 128)
    for c in range(16):
        for br in range(3):
            ws.add("gate", w_in, 0, 16, COL_G + br * D + c * 128)
            ws.add("bw", (w_conv_out, w_dil_o, w_x_o)[br], 0, (8, 4, 4)[br], c * 128)
    for c in range(16):
        ws.add("wout", w_out, 0, 16, c * 128)
    for p in range(4):
        for f in range(11 * p, 11 * p + 11):
            ws.add("fg", w_gate, 0, 16, f * 128)
            ws.add("fu", w_up, 0, 16, f * 128)
        for c in range(16):
            ws.add("fd", w_down, p * 11 * 128, 11, c * 128)

    identF = M.alloc("identF", [128, 128], F32); load(identF, identF.ap, ident_d)
    identB = M.alloc("identB", [128, 128], BF16)
    OP("dve", lambda e: e.tensor_copy(identB.ap, identF.ap), [identF.b], [identB.b])
    onesB = M.alloc("onesB", [128, 128], BF16); OP("dve", lambda e: e.memset(onesB.ap, 1.0), [], [onesB.b])
    onesF = M.alloc("onesF", [128, 128], F32); OP("dve", lambda e: e.memset(onesF.ap, 1.0), [], [onesF.b])
    bg = M.alloc("bg", [128, 48], F32); load(bg, bg.ap, bgT_d)
    cw = M.alloc("cw", [128, 8, 31], F32); load(cw, cw.ap, cwT_d.rearrange("p (c j) -> p c j", c=8))
    cb = M.alloc("cb", [128, 8], F32); load(cb, cb.ap, cbT_d)
    lng = M.alloc("lng", [128, 8], F32); load(lng, lng.ap, lngT_d)
    lnb = M.alloc("lnb", [128, 8], F32); load(lnb, lnb.ap, lnbT_d)
    gpm = M.alloc("gpm", [128, 16], F32); load(gpm, gpm.ap, gpmT_d)
    gpf = M.alloc("gpf", [128, 16], F32); load(gpf, gpf.ap, gpfT_d)
    sel = M.alloc("sel", [4, 4, 128], F32); load(sel, sel.ap, sel_d.rearrange("p (s m) -> p s m", s=4))
    smalls = M.alloc("smalls", [128, 64], F32)
    sm_bufs = [smalls.buf("sm%d" % i) for i in range(8)]
    sm_state = {"i": 0}

    def small2():
        i = sm_state["i"] % 8
        sm_state["i"] += 1
        return smalls[:, 2 * i:2 * i + 1], smalls[:, 2 * i + 1:2 * i + 2], sm_bufs[i]

    io = {}

    def io_alloc(with_z=False):
        io["x"] = Ring([M.alloc("xr%d" % i, [128, D], F32) for i in range(3 if with_z else 5)])
        if with_z:
            io["z"] = Ring([M.alloc("zr%d" % i, [128, D], F32) for i in range(4)])
        io["hb"] = Ring([M.alloc("hb%d" % i, [128, D], BF16) for i in range(4)])
        if not with_z:
            io["gA"] = M.alloc("gA", [128, D], F32)
        io["gB"] = M.alloc("gB", [128, D], F32)

    def io_free():
        M.free(*io["x"].tns, *io["hb"].tns, io["gB"])
        if "gA" in io:
            M.free(io["gA"])
        if "z" in io:
            M.free(*io["z"].tns)
        io.clear()

    def rstd_chain(ssap, rsap, sb, n):
        OP("dve", lambda e: e.tensor_scalar(rsap[0:n], ssap[0:n], 1.0 / D, EPS, ALU.mult, ALU.add), [sb], [sb])
        OP("act", lambda e: e.sqrt(rsap[0:n], rsap[0:n]), [sb], [sb])
        OP("dve", lambda e: e.reciprocal(rsap[0:n], rsap[0:n]), [sb], [sb])

    def norm_a(src, n, xt=None):
        if xt is None:
            xt = io["x"].next()
            S.dma("sp", xt[0:n, :], src, xt.b, writes=[xt.b])
        hb = io["hb"].next()
        ss, rs, sb = small2()
        OP("dve", lambda e: e.memset(ss[0:n], 0.0), [], [sb])
        act(hb[0:n, :], xt[0:n, :], AF.Square, [xt.b], [hb.b, sb], accum_out=ss[0:n])
        return (xt, hb, ss, rs, sb, n)

    def norm_r(ctx):
        xt, hb, ss, rs, sb, n = ctx
        rstd_chain(ss, rs, sb, n)

    def norm_b1(ctx, gtn):
        xt, hb, ss, rs, sb, n = ctx
        OP("dve", lambda e: e.scalar_tensor_tensor(hb[0:n, :], xt[0:n, :], rs[0:n], gtn[0:n, :], ALU.mult, ALU.mult),
           [xt.b, sb, gtn.b], [hb.b])

    def norm_b2(ctx, dst_fn, dst_buf):
        xt, hb, ss, rs, sb, n = ctx
        for hlf in range(2):
            bank, bb = pb()
            for c in range(8):
                cc = hlf * 8 + c
                OP("pe", lambda e: e.transpose(bank[:, c * 128:c * 128 + n], hb[0:n, cc * 128:(cc + 1) * 128], identB[0:n, 0:n]),
                   [hb.b, identB.b], [bb], inc=(c == 7))
            bv = bank.rearrange("p (c t) -> p c t", c=8)
            if hlf == 0:
                act(dst_fn(hlf), bv[:, :, 0:n], AF.Copy, [bb], [dst_buf])
            else:
                OP("dve", lambda e: e.tensor_copy(dst_fn(hlf), bv[:, :, 0:n]), [bb], [dst_buf])

    def sw_pipeline(n, stages, order=None):
        if order is None:
            order = list(reversed(range(len(stages))))
        for step in range(n + len(stages) - 1):
            for si in order:
                t = step - si
                if 0 <= t < n:
                    stages[si](t)

    def proj(slab, nk, rhs_fn, rhs_bufs, chunks, evac):
        for ci, (c0, n) in enumerate(chunks):
            bank, bb = pf()
            for k in range(nk):
                OP("pe", lambda e: e.matmul(bank[:, 0:n], slab[:, k, :], rhs_fn(k, c0, n), start=(k == 0), stop=(k == nk - 1)),
                   [slab.b] + rhs_bufs, [bb], inc=(k == nk - 1))
            evac(ci, c0, n, bank, bb)

    def transpose_out(src_ap, src_buf, ntok_tiles, n_last, dst_fn):
        for t in range(ntok_tiles):
            n = 128 if t < ntok_tiles - 1 or n_last == 128 else n_last
            bank, bb = pf()
            OP("pe", lambda e: e.transpose(bank[0:n, 0:128], src_ap[:, t * 128:t * 128 + n], identF.ap), [src_buf, identF.b], [bb])
            dst_fn(t, n, bank, bb)

    io_alloc()
    load(io["gA"], io["gA"].ap, g_pre_mix.partition_broadcast(128))
    kvT = M.alloc("kvT", [128, 8, SEQ], BF16, top=True)
    hT = M.alloc("hT", [128, NCH, T_EXT], BF16, top=True)
    hTc = M.alloc("hTc", [128, NCH, HALF], BF16)
    memT = M.alloc("memT", [128, NCH, 256], BF16, top=True)
    load(io["gB"], io["gB"].ap, g_mem.partition_broadcast(128))
    hT.b.dj = True
    memT.b.dj = True
    hTc.b.dj = True
    kvT.b.dj = True
    tiles = []

    bgq = []

    def ctx_after():
        def tail():
            act(hT[:, :, 0:32], hTc[:, :, HALF - 32:HALF], AF.Copy, [hTc.b], [hT.b])
        bgq.append(tail)
        pend = {}
        for j in range(8):
            def mm(j=j):
                slab = ws.next("ctxkv")
                got = []

                def ev(ci, c0, n, bank, bb):
                    got.append((c0, n, bank, bb))
                proj(slab, 16, lambda k, c0, n: hTc[:, k, c0:c0 + n], [hTc.b], [(0, 512), (512, 512)], ev)
                pend[j] = got

            def evac(j=j):
                for (c0, n, bank, bb) in pend.pop(j):
                    act(kvT[:, j, c0:c0 + n], bank[:, 0:n], AF.Copy, [bb], [kvT.b])
            if j == 0:
                bgq.append(mm)
            else:
                bgq.append(lambda mm=mm, evp=prev_ev: (evp(), mm()))
            prev_ev = evac
        bgq.append(prev_ev)
    for t in range(8):
        tiles.append((x_ctx[t * 128:(t + 1) * 128, :], 128, "gA",
                      (lambda hlf, t=t: hTc[:, 8 * hlf:8 * hlf + 8, t * 128:(t + 1) * 128]), hTc.b,
                      ctx_after if t == 7 else None))
    for t in range(8):
        tiles.append((x_own[t * 128:(t + 1) * 128, :], 128, "gA",
                      (lambda hlf, t=t: hT[:, 8 * hlf:8 * hlf + 8, OWN0 + t * 128:OWN0 + (t + 1) * 128]), hT.b, None))
    tiles.append((x_smp, 4, "gA", (lambda hlf: hT[:, 8 * hlf:8 * hlf + 8, SMP0:SMP0 + 4]), hT.b, None))
    for t in range(2):
        tiles.append((mem[t * 128:(t + 1) * 128, :], 128, "gB",
                      (lambda hlf, t=t: memT[:, 8 * hlf:8 * hlf + 8, t * 128:(t + 1) * 128]), memT.b, None))
    nctx = {}

    def st_a(i):
        nctx[i] = norm_a(tiles[i][0], tiles[i][1])

    def st_r(i):
        norm_r(nctx[i])

    def st_b1(i):
        norm_b1(nctx[i], io[tiles[i][2]])

    def st_b2(i):
        src, n, gk, dst_fn, dst_buf, hook = tiles[i]
        norm_b2(nctx.pop(i), dst_fn, dst_buf)
        if hook is not None:
            hook()
        elif bgq:
            bgq.pop(0)()
    sw_pipeline(len(tiles), [st_a, st_r, st_b1, st_b2], order=[1, 0, 2, 3])
    while bgq:
        bgq.pop(0)()
    hT.b.dj = False
    memT.b.dj = False
    kvT.b.dj = False
    M.free(hTc)
    io_free()
    dump("hT", hT, dt=BF16)
    dump("kvT_ctx", kvT, dt=BF16)

    branchT = M.alloc("branchT", [128, NCH, T_EXT], BF16, top=True)
    b_conv = branchT.b
    b_conv.dj = True
    b_dil = branchT.buf("dil", dj=True)
    b_xat = branchT.buf("xat", dj=True)
    yconv = M.alloc("yconv", [128, 8, NT], F32)
    yconv.b.dj = True
    utail = M.alloc("utail", [128, 8, 32], F32); utail.b.dj = True
    usmp = M.alloc("usmp", [128, 8, 4], F32); usmp.b.dj = True
    fulls = M.alloc("fulls", [128, 8, 4, 31], F32); fulls.b.dj = True
    ucr = Ring([M.alloc("uc%d" % i, [128, T_EXT], F32) for i in range(2)])
    sgr = Ring([M.alloc("sgd%d" % i, [128, 354], F32) for i in range(3)])
    scr = Ring([M.alloc("sc%d" % i, [32, C_CONV], F32) for i in range(2)])

    for s in range(4):
        sc = scr.next()
        S.dma("sp", sc[0:30, :], sconv[s], sc.b, writes=[sc.b])
        bank, bb = pf()
        for c in range(8):
            OP("pe", lambda e: e.transpose(bank[:, c * 32:c * 32 + 30], sc[0:30, c * 128:(c + 1) * 128], identF[0:30, 0:30]),
               [sc.b, identF.b], [bb], inc=(c == 7))
        act(fulls[:, :, s, 0:30], bank[:, 0:256].rearrange("p (c t) -> p c t", c=8)[:, :, 0:30], AF.Copy, [bb], [fulls.b])
        store(conv_s[s, 0:29, :], sc[1:30, :], sc.b)

    M.free(*scr.tns)
    ubr = Ring([M.alloc("ub%d" % i, [128, T_EXT], BF16) for i in range(2)])
    dgr = Ring([M.alloc("dg%d" % i, [128, 31, 128], BF16) for i in range(2)])
    for c in range(8):
        slab_a = ws.next("a")
        slab_b = ws.next("b")
        uc = ucr.next()
        ub = ubr.next()
        dg = dgr.next()
        OP("dve", lambda e: e.tensor_tensor(dg.ap, identB.ap.unsqueeze(1).to_broadcast([128, 31, 128]),
                                            cw[:, c, :].unsqueeze(2).to_broadcast([128, 31, 128]), ALU.mult),
           [identB.b, cw.b], [dg.b])
        for ci, (c0, n) in enumerate(CH3U):
            bka, bba = pf()
            for k in range(16):
                OP("pe", lambda e: e.matmul(bka[:, 0:n], slab_a[:, k, :], hT[:, k, c0:c0 + n], start=(k == 0), stop=(k == 15)),
                   [slab_a.b, hT.b], [bba], inc=(k == 15))
            bkb, bbb = pf()
            for k in range(16):
                OP("pe", lambda e: e.matmul(bkb[:, 0:n], slab_b[:, k, :], hT[:, k, c0:c0 + n], start=(k == 0), stop=(k == 15)),
                   [slab_b.b, hT.b], [bbb], inc=(k == 15))
            sg = sgr.next()
            act(sg[:, 0:n], bkb[:, 0:n], AF.Sigmoid, [bbb], [sg.b])
            OP("dve", lambda e: e.tensor_tensor(uc[:, c0:c0 + n], bka[:, 0:n], sg[:, 0:n], ALU.mult), [bba, sg.b], [uc.b])
        act(ub.ap, uc.ap, AF.Copy, [uc.b], [ub.b])
        OP("dve", lambda e: e.tensor_copy(utail[:, c, 0:30], uc[:, 1026:1056]), [uc.b], [utail.b])
        OP("dve", lambda e: e.tensor_copy(usmp[:, c, :], uc[:, SMP0:SMP0 + 4]), [uc.b], [usmp.b])
        OP("dve", lambda e: e.tensor_copy(fulls[:, c, :, 30], uc[:, SMP0:SMP0 + 4]), [uc.b], [fulls.b])
        for hv in range(2):
            bank, bb = pf()
            for j in range(31):
                o_ = 2 + j + 512 * hv
                OP("pe", lambda e: e.matmul(bank[:, 0:512], dg[:, j, :], ub[:, o_:o_ + 512], start=(j == 0), stop=(j == 30)),
                   [dg.b, ub.b], [bb], inc=(j == 30))
            act(yconv[:, c, 512 * hv:512 * (hv + 1)], bank[:, 0:512], AF.Identity, [bb, cb.b], [yconv.b], bias=cb[:, c:c + 1])
    M.free(*ubr.tns, *dgr.tns)
    urow = M.alloc("urow", [32, C_CONV], F32)
    dump("yconv", yconv)
    for (src, n, dst) in ((utail, 30, conv_p), (usmp, 4, None)):
        for hlf in range(2):
            bank, bb = pf()
            for c4 in range(4):
                c = hlf * 4 + c4
                OP("pe", lambda e: e.transpose(bank[0:n, c4 * 128:(c4 + 1) * 128], src[:, c, 0:n], identF.ap),
                   [src.b, identF.b], [bb], inc=(c4 == 3))
            act(urow[0:n, hlf * 512:(hlf + 1) * 512], bank[0:n, 0:512], AF.Copy, [bb], [urow.b])
        if dst is not None:
            store(dst, urow[0:30, :], urow.b)
        else:
            store(conv_s[:, 29, :], urow[0:4, :], urow.b)
    prods = M.alloc("prods", [128, 8, 4, 31], F32)
    ysm = M.alloc("ysm", [128, 8, 4], F32)
    for s in range(4):
        OP("dve", lambda e: e.tensor_tensor(prods[:, :, s, :], fulls[:, :, s, :], cw.ap, ALU.mult), [fulls.b, cw.b], [prods.b])
    OP("dve", lambda e: e.reduce_sum(ysm.ap.rearrange("p c s -> p (c s)"), prods.ap.rearrange("p c s j -> p (c s) j"), AX.X),
       [prods.b], [ysm.b])
    for s in range(4):
        OP("dve", lambda e: e.tensor_tensor(yconv[:, :, HALF + s], ysm[:, :, s], cb.ap, ALU.add), [ysm.b, cb.b], [yconv.b])
    M.free(prods, ysm, urow, *ucr.tns, *sgr.tns)
    LNCH = [(0, 343), (343, 343), (686, 342)]
    lnt = Ring([M.alloc("lnt%d" % i, [128, 343], F32) for i in range(3)])
    mus = [M.alloc("mu%d" % i, [128, 343], F32) for i in range(3)]
    rsds = [M.alloc("rsd%d" % i, [128, 343], F32) for i in range(3)]
    for li, (c0, n) in enumerate(LNCH):
        mu, rsd = mus[li], rsds[li]
        b1, bb1 = pf()
        b2, bb2 = pf()
        for c in range(8):
            OP("pe", lambda e: e.matmul(b1[:, 0:n], onesF.ap, yconv[:, c, c0:c0 + n], start=(c == 0), stop=(c == 7)),
               [onesF.b, yconv.b], [bb1], inc=(c == 7))
        for c in range(8):
            ysq = lnt.next()
            act(ysq[:, 0:n], yconv[:, c, c0:c0 + n], AF.Square, [yconv.b], [ysq.b])
            OP("pe", lambda e: e.matmul(b2[:, 0:n], onesF.ap, ysq[:, 0:n], start=(c == 0), stop=(c == 7)),
               [onesF.b, ysq.b], [bb2], inc=True)
        act(mu[:, 0:n], b1[:, 0:n], AF.Copy, [bb1], [mu.b], scale=1.0 / C_CONV)
        OP("dve", lambda e: e.tensor_tensor(rsd[:, 0:n], mu[:, 0:n], mu[:, 0:n], ALU.mult), [mu.b], [rsd.b])
        OP("dve", lambda e: e.scalar_tensor_tensor(rsd[:, 0:n], b2[:, 0:n], 1.0 / C_CONV, rsd[:, 0:n], ALU.mult, ALU.subtract),
           [bb2, rsd.b], [rsd.b])
        OP("dve", lambda e: e.tensor_scalar(rsd[:, 0:n], rsd[:, 0:n], 1.0, EPS, ALU.mult, ALU.add), [rsd.b], [rsd.b])
        OP("act", lambda e: e.sqrt(rsd[:, 0:n], rsd[:, 0:n]), [rsd.b], [rsd.b])
        OP("dve", lambda e: e.reciprocal(rsd[:, 0:n], rsd[:, 0:n]), [rsd.b], [rsd.b])
    items = [(li, c) for li in range(3) for c in range(8)]
    ynr = Ring([M.alloc("yn%d" % i, [128, 343], F32) for i in range(3)])
    t1r = Ring([M.alloc("t1_%d" % i, [128, 343], F32) for i in range(3)])
    sgl = Ring([M.alloc("sgl%d" % i, [128, 343], F32) for i in range(3)])
    lnst = {}

    def ln1(i):
        li, c = items[i]
        c0, n = LNCH[li]
        t1 = t1r.next()
        OP("dve", lambda e: e.tensor_tensor(t1[:, 0:n], yconv[:, c, c0:c0 + n], mus[li][:, 0:n], ALU.subtract), [yconv.b, mus[li].b], [t1.b])
        OP("dve", lambda e: e.tensor_tensor(t1[:, 0:n], t1[:, 0:n], rsds[li][:, 0:n], ALU.mult), [t1.b, rsds[li].b], [t1.b])
        lnst[i] = t1

    def ln2(i):
        li, c = items[i]
        c0, n = LNCH[li]
        t1 = lnst[i]
        yn = ynr.next()
        sg = sgl.next()
        act(yn[:, 0:n], t1[:, 0:n], AF.Identity, [t1.b, lng.b, lnb.b], [yn.b], scale=lng[:, c:c + 1], bias=lnb[:, c:c + 1])
        act(sg[:, 0:n], t1[:, 0:n], AF.Sigmoid, [t1.b, lng.b, lnb.b], [sg.b], scale=lng[:, c:c + 1], bias=lnb[:, c:c + 1])
        lnst[i] = (yn, sg)

    def ln3(i):
        li, c = items[i]
        c0, n = LNCH[li]
        yn, sg = lnst.pop(i)
        OP("dve", lambda e: e.tensor_tensor(branchT[:, c, OWN0 + c0:OWN0 + c0 + n], yn[:, 0:n], sg[:, 0:n], ALU.mult),
           [yn.b, sg.b], [b_conv])
    sw_pipeline(len(items), [ln1, ln2, ln3], order=[2, 0, 1])
    M.free(*ynr.tns)
    mu, rsd = mus[0], rsds[0]
    M.free(yconv, utail, usmp, fulls, *mus, *rsds, *lnt.tns, *t1r.tns, *sgl.tns)
    dump("uconvT", branchT, ap=branchT[:, 0:8, :], dt=BF16)

    kvT.b.dj = True
    tb = {}
    for nm, d_ in (("tb0", tb0_d), ("tb0f", tb0f_d), ("tb1", tb1_d), ("tb1f", tb1f_d), ("tb2", tb2_d)):
        tb[nm] = M.alloc(nm, [128, 4, 256], F32)
        load(tb[nm], tb[nm].ap, d_.rearrange("p (h q) -> p h q", h=4))
    tbs = M.alloc("tbs", [128, 12], F32); load(tbs, tbs.ap, tbs_d)
    qsF = M.alloc("qsF", [128, 3, 4, 4], F32); qsF.b.dj = True
    ksF = M.alloc("ksF", [128, 4, 4], F32); ksF.b.dj = True
    vsF = M.alloc("vsF", [128, 4, 4], F32); vsF.b.dj = True
    qtm = M.alloc("qtm", [4, 1536], F32); qtm.b.dj = True
    qT = M.alloc("qT", [128, 3, T_EXT], BF16)
    kf = M.alloc("kf", [128, NT], F32)
    stg = Ring([M.alloc("stg%d" % i, [128, 8, 128], F32) for i in range(2)])
    srow = Ring([M.alloc("srow%d" % i, [4, 128], F32) for i in range(2)])
    Vtok = M.alloc("Vtok", [128, 37, 128], BF16)
    accOD = M.alloc("accOD", [128, 2, HALF], F32)
    str_ = Ring([M.alloc("st%d" % i, [128, 256], F32) for i in range(3)])
    Pr = Ring([M.alloc("P%d" % i, [128, 256], BF16) for i in range(3)])

    def kv_slab(tag, h, j, out_own, out_smp, smpF):
        slab = ws.next(tag)

        def ev(ci, c0, n, bank, bb):
            act(kf[:, c0 - OWN0:c0 - OWN0 + n], bank[:, 0:n], AF.Copy, [bb], [kf.b])
        kf.b.dj = True
        proj(slab, 16, lambda k, c0, n: hT[:, k, c0:c0 + n], [hT.b], CH3, ev)
        kf.b.dj = False
        act(kvT[:, j, HALF:SEQ], kf[:, 0:HALF], AF.Copy, [kf.b], [kvT.b])
        OP("dve", lambda e: e.tensor_copy(smpF[:, h, :], kf[:, HALF:HALF + 4]), [kf.b], [smpF.b])
        sg_ = stg.next()
        for hlf in range(2):
            bank, bb = pf()
            for t4 in range(4):
                t = hlf * 4 + t4
                OP("pe", lambda e: e.transpose(bank[:, t4 * 128:(t4 + 1) * 128], kf[:, t * 128:(t + 1) * 128], identF.ap),
                   [kf.b, identF.b], [bb], inc=(t4 == 3))
            act(sg_[:, hlf * 4:hlf * 4 + 4, :], bank[:, 0:512].rearrange("p (t d) -> p t d", t=4), AF.Copy, [bb], [sg_.b])
        store(out_own[:, h * 128:(h + 1) * 128].rearrange("(t p) d -> p t d", p=128), sg_.ap, sg_.b)
        bank, bb = pf()
        OP("pe", lambda e: e.transpose(bank[0:4, 0:128], kf[:, HALF:HALF + 4], identF.ap), [kf.b, identF.b], [bb])
        sr = srow.next()
        act(sr.ap, bank[0:4, 0:128], AF.Copy, [bb], [sr.b])
        store(out_smp[:, h * 128:(h + 1) * 128], sr.ap, sr.b)

    for h in range(4):
        kv_slab("k", h, h, wk_own, wk_s, ksF)
        kv_slab("v", h, 4 + h, wv_own, wv_s, vsF)
        for g in range(3):
            slab = ws.next("q")

            def evq(ci, c0, n, bank, bb, g=g, h=h):
                act(qT[:, g, c0:c0 + n], bank[:, 0:n], AF.Copy, [bb], [qT.b])
                if ci == 2:
                    OP("dve", lambda e: e.tensor_copy(qsF[:, g, h, :], bank[:, SMP0 - c0:SMP0 - c0 + 4]), [bb], [qsF.b])
            qT.b.dj = True
            proj(slab, 16, lambda k, c0, n: hT[:, k, c0:c0 + n], [hT.b], CH3, evq)
            qT.b.dj = False
            bank, bb = pf()
            for k in range(16):
                OP("pe", lambda e: e.matmul(bank[0:4, 0:128], hT[:, k, SMP0:SMP0 + 4], slab[:, k, :], start=(k == 0), stop=(k == 15)),
                   [hT.b, slab.b], [bb], inc=(k == 15))
            act(qtm[0:4, (g * 4 + h) * 128:(g * 4 + h + 1) * 128], bank[0:4, 0:128], AF.Copy, [bb], [qtm.b])
        sels = [slice(128 * (7 + i), 128 * (8 + i)) for i in range(9)]
        for r in range(4):
            for nl in (1, 2, 3):
                sels.append(_sl(512 * nl + r, 128, 4))
        for r in range(16):
            sels.append(_sl(r, 128, 16))
        Vtok.b.dj = True
        for i0 in range(0, 37, 8):
            bank, bb = pb()
            cnt = min(8, 37 - i0)
            for ii in range(cnt):
                OP("pe", lambda e: e.transpose(bank[:, ii * 128:(ii + 1) * 128], kvT[:, 4 + h, sels[i0 + ii]], identB.ap),
                   [kvT.b, identB.b], [bb], inc=(ii == cnt - 1))
            act(Vtok[:, i0:i0 + cnt, :], bank[:, 0:cnt * 128].rearrange("p (i d) -> p i d", i=cnt), AF.Copy, [bb], [Vtok.b])
        Vtok.b.dj = False

        def unit(ksel, vidx, qsel, tbl_ap, tbl_buf, nq):
            nk_ = len(ksel)
            bS, bbS = pf()
            for i_, ks_ in enumerate(ksel):
                OP("pe", lambda e: e.matmul(bS[:, i_ * nq:(i_ + 1) * nq], kvT[:, h, ks_], qT[:, qsel[0], qsel[1]], start=True, stop=True),
                   [kvT.b, qT.b], [bbS], inc=(i_ == nk_ - 1))
            stt = str_.next()
            P = Pr.next()
            w_ = nk_ * nq
            OP("dve", lambda e: e.scalar_tensor_tensor(stt[:, 0:w_], bS[:, 0:w_], SCALE, tbl_ap, ALU.mult, ALU.add),
               [bbS, tbl_buf], [stt.b])
            act(P[:, 0:w_], stt[:, 0:w_], AF.Exp, [stt.b], [P.b])
            return P, w_

        units = []

        def mk_band(g_, tbl, ip, ic, qs_, accv, first):
            def s1():
                return unit([sels[ip], sels[ic]], None, (g_, qs_), tbl[:, h, :], tbl.b, 128)

            def s2(P):
                b2, bb2 = pf()
                OP("pe", lambda e: e.matmul(b2[:, 0:128], Vtok[:, ip, :], P[:, 0:128], start=True, stop=False), [Vtok.b, P.b], [bb2], inc=False)
                OP("pe", lambda e: e.matmul(b2[:, 0:128], Vtok[:, ic, :], P[:, 128:256], start=False, stop=True), [Vtok.b, P.b], [bb2], inc=False)
                OP("pe", lambda e: e.matmul(b2[:, 128:256], onesB.ap, P[:, 0:128], start=True, stop=False), [onesB.b, P.b], [bb2], inc=False)
                OP("pe", lambda e: e.matmul(b2[:, 128:256], onesB.ap, P[:, 128:256], start=False, stop=True), [onesB.b, P.b], [bb2])
                src = b2[:, 0:256].rearrange("p (o q) -> p o q", o=2)
                if first:
                    OP("dve", lambda e: e.tensor_copy(accv, src), [bb2], [accOD.b])
                else:
                    OP("dve", lambda e: e.tensor_tensor(accv, accv, src, ALU.add), [bb2, accOD.b], [accOD.b])
            return (s1, s2)
        for i in range(8):
            tbl = tb["tb0f"] if i == 0 else tb["tb0"]
            units.append(mk_band(0, tbl, i, i + 1, slice(OWN0 + 128 * i, OWN0 + 128 * (i + 1)),
                                 accOD[:, :, 128 * i:128 * (i + 1)], True))
        for r in range(4):
            for nl in (2, 3):
                tbl = tb["tb1f"] if nl == 2 else tb["tb1"]
                ip = 9 + r * 3 + (nl - 2)
                units.append(mk_band(1, tbl, ip, ip + 1, _sl(OWN0 + 512 * (nl - 2) + r, 128, 4),
                                     accOD[:, :, _sl(512 * (nl - 2) + r, 128, 4)], False))

        def mk_g2(r0):
            def s1():
                bS, bbS = pf()
                for rr in range(4):
                    r = r0 + rr
                    OP("pe", lambda e: e.matmul(bS[:, rr * 64:(rr + 1) * 64], kvT[:, h, sels[21 + r]], qT[:, 2, _sl(OWN0 + r, 64, 16)],
                                                start=True, stop=True), [kvT.b, qT.b], [bbS], inc=(rr == 3))
                stt = str_.next()
                P = Pr.next()
                OP("dve", lambda e: e.scalar_tensor_tensor(stt.ap, bS[:, 0:256], SCALE, tb["tb2"][:, h, :], ALU.mult, ALU.add),
                   [bbS, tb["tb2"].b], [stt.b])
                act(P.ap, stt.ap, AF.Exp, [stt.b], [P.b])
                return P, 256

            def s2(P):
                b2, bb2 = pf()
                for rr in range(4):
                    OP("pe", lambda e: e.matmul(b2[:, rr * 64:(rr + 1) * 64], Vtok[:, 21 + r0 + rr, :], P[:, rr * 64:(rr + 1) * 64],
                                                start=True, stop=True), [Vtok.b, P.b], [bb2], inc=False)
                OP("pe", lambda e: e.matmul(b2[:, 256:512], onesB.ap, P.ap, start=True, stop=True), [onesB.b, P.b], [bb2])
                for o in range(2):
                    av = accOD[:, o, :].rearrange("p (i r) -> p r i", r=16)[:, r0:r0 + 4, :]
                    OP("dve", lambda e: e.tensor_tensor(av, av, b2[:, o * 256:(o + 1) * 256].rearrange("p (r i) -> p r i", r=4), ALU.add),
                       [bb2, accOD.b], [accOD.b])
            return (s1, s2)
        for r0 in range(0, 16, 4):
            units.append(mk_g2(r0))
        pend = []
        for (s1, s2) in units:
            P, _w = s1()
            pend.append((s2, P))
            if len(pend) > 2:
                f_, P_ = pend.pop(0)
                f_(P_)
        while pend:
            f_, P_ = pend.pop(0)
            f_(P_)
        act(accOD[:, 1, :], accOD[:, 1, :], AF.Ln, [accOD.b], [accOD.b])
        act(accOD[:, 1, :], accOD[:, 1, :], AF.Exp, [accOD.b], [accOD.b], scale=-1.0)
        OP("dve", lambda e: e.tensor_tensor(branchT[:, 8 + h, OWN0:OWN0 + HALF], accOD[:, 0, :], accOD[:, 1, :], ALU.mult),
           [accOD.b], [b_dil])
    M.free(qT, kf, Vtok, accOD, *stg.tns, *srow.tns, *str_.tns, *Pr.tns)
    for nm in ("tb0", "tb0f", "tb1", "tb1f", "tb2"):
        M.free(tb[nm])
    dump("attnT", branchT, ap=branchT[:, 8:12, :], dt=BF16)

    def sample_attend(q_tm, qcols, ngrp, key_rows, val_rows, tbl, out_chunk0, extra):
        ncol = ngrp * 4
        bO, bbO = pf_fixed(4)
        bD, bbD = pf_fixed(5)
        qb = M.alloc("qb", [128, qcols], F32)
        Kr = Ring([M.alloc("Kr%d" % i, [128, 512], F32) for i in range(2)])
        Vr = Ring([M.alloc("Vr%d" % i, [128, 512], F32) for i in range(2)])
        Vb = M.alloc("Vb", [128, ngrp, 512], BF16)
        prod = M.alloc("prod", [128, 512], F32)
        scs = M.alloc("scs", [128, ncol], F32)
        Ps = M.alloc("Ps", [128, ncol], BF16)
        for s in range(4):
            for g in range(qcols // 512):
                bank, bb = pf()
                OP("pe", lambda e: e.matmul(bank[:, 0:512], sel[0:4, s, :], q_tm[0:4, g * 512:(g + 1) * 512], start=True, stop=True),
                   [sel.b, q_tm.b], [bb])
                act(qb[:, g * 512:(g + 1) * 512], bank[:, 0:512], AF.Copy, [bb], [qb.b])
            yield
            for g in range(ngrp):
                K_ = Kr.next()
                S.dma("sp", K_.ap, key_rows(s, g), K_.b, writes=[K_.b])
                V_ = Vr.next()
                S.dma("sp", V_.ap, val_rows(s, g), V_.b, writes=[V_.b])
                qoff = (g * 512) % qcols
                OP("dve", lambda e: e.tensor_tensor(prod.ap, K_.ap, qb[:, qoff:qoff + 512], ALU.mult), [K_.b, qb.b], [prod.b])
                OP("dve", lambda e: e.reduce_sum(scs[:, g * 4:(g + 1) * 4], prod.ap.rearrange("p (h d) -> p h d", h=4), AX.X),
                   [prod.b], [scs.b])
                act(Vb[:, g, :], V_.ap, AF.Copy, [V_.b], [Vb.b])
                yield
            if tbl is not None:
                OP("dve", lambda e: e.scalar_tensor_tensor(scs.ap, scs.ap, SCALE, tbl.ap, ALU.mult, ALU.add), [scs.b, tbl.b], [scs.b])
                act(Ps.ap, scs.ap, AF.Exp, [scs.b], [Ps.b])
            else:
                act(Ps.ap, scs.ap, AF.Exp, [scs.b], [Ps.b], scale=SCALE)
            for h in range(4):
                for g in range(ngrp):
                    OP("pe", lambda e: e.matmul(bO[:, h * 4 + s:h * 4 + s + 1], Vb[:, g, h * 128:(h + 1) * 128], Ps[:, g * 4 + h:g * 4 + h + 1],
                                                start=(g == 0), stop=(g == ngrp - 1)), [Vb.b, Ps.b], [bbO], inc=(g == ngrp - 1))
            OP("pe", lambda e: e.matmul(bD[:, s * ncol:(s + 1) * ncol], onesB.ap, Ps.ap, start=True, stop=True), [onesB.b, Ps.b], [bbD])
            yield
        dsb = M.alloc("dsb", [128, 4 * ncol], F32)
        num = M.alloc("num", [128, 4, 4], F32)
        den = M.alloc("den", [128, 4, 4], F32)
        act(dsb.ap, bD[:, 0:4 * ncol], AF.Copy, [bbD], [dsb.b])
        dv = dsb.ap.rearrange("p (s g h) -> p g h s", s=4, g=ngrp)
        OP("dve", lambda e: e.tensor_tensor(den.ap, dv[:, 0], dv[:, 1], ALU.add), [dsb.b], [den.b])
        for g in range(2, ngrp):
            OP("dve", lambda e: e.tensor_tensor(den.ap, den.ap, dv[:, g], ALU.add), [dsb.b, den.b], [den.b])
        pso = bO[:, 0:16].rearrange("p (h s) -> p h s", h=4)
        if extra is not None:
            e0s, e0b, vs_ = extra
            OP("dve", lambda e: e.tensor_tensor(den.ap, den.ap, e0s, ALU.add), [den.b, e0b], [den.b])
            OP("dve", lambda e: e.tensor_tensor(num.ap, e0s, vs_.ap, ALU.mult), [e0b, vs_.b], [num.b])
            OP("dve", lambda e: e.tensor_tensor(num.ap, num.ap, pso, ALU.add), [num.b, bbO], [num.b])
        else:
            OP("dve", lambda e: e.tensor_copy(num.ap, pso), [bbO], [num.b])
        OP("dve", lambda e: e.reciprocal(den.ap, den.ap), [den.b], [den.b])
        OP("dve", lambda e: e.tensor_tensor(branchT[:, out_chunk0:out_chunk0 + 4, SMP0:SMP0 + 4], num.ap, den.ap, ALU.mult),
           [num.b, den.b], [b_dil if out_chunk0 == 8 else b_xat])
        M.free(qb, Vb, prod, scs, Ps, dsb, num, den, *Kr.tns, *Vr.tns)

    prod0 = M.alloc("prod0", [128, 3, 4, 4], F32)
    E0 = M.alloc("E0", [128, 3, 4, 4], F32)
    E0s = M.alloc("E0s", [128, 4, 4], F32)
    for g in range(3):
        OP("dve", lambda e: e.tensor_tensor(prod0[:, g], qsF[:, g], ksF.ap, ALU.mult), [qsF.b, ksF.b], [prod0.b])
    bank, bb = pf()
    OP("pe", lambda e: e.matmul(bank[:, 0:48], onesF.ap, prod0.ap.rearrange("p g h s -> p (g h s)"), start=True, stop=True),
       [onesF.b, prod0.b], [bb])
    act(E0.ap.rearrange("p g h s -> p (g h s)"), bank[:, 0:48], AF.Exp, [bb], [E0.b], scale=SCALE)
    OP("dve", lambda e: e.tensor_tensor(E0s.ap, E0[:, 0], E0[:, 1], ALU.add), [E0.b], [E0s.b])
    OP("dve", lambda e: e.tensor_tensor(E0s.ap, E0s.ap, E0[:, 2], ALU.add), [E0.b, E0s.b], [E0s.b])

    def ck_rows(src):
        def f(s, g):
            d_ = DIL[g]
            return src[s, _sl(2048 - 128 * d_, 128, d_), :]
        return f
    M.free(kvT)
    pstate["rot"] = [0, 1, 2, 3]
    sgen = sample_attend(qtm, 1536, 3, ck_rows(ck), ck_rows(cv), tbs, 8, (E0s.ap, E0s.b, vsF))

    def tick(k=1):
        for _ in range(k):
            next(sgen, None)

    mkT = M.alloc("mkT", [128, 4, 256], BF16); mkT.b.dj = True
    mvtok = M.alloc("mvtok", [128, 2, 4, 128], BF16); mvtok.b.dj = True
    mkvF = Ring([M.alloc("mkvF%d" % i, [128, 256], F32) for i in range(2)])
    mstg = Ring([M.alloc("mstg%d" % i, [128, 2, 128], F32) for i in range(2)])
    for j in range(8):
        slab = ws.next("memkv")
        mf = mkvF.next()

        def evm(ci, c0, n, bank, bb):
            act(mf.ap, bank[:, 0:256], AF.Copy, [bb], [mf.b])
        proj(slab, 16, lambda k, c0, n: memT[:, k, 0:256], [memT.b], [(0, 256)], evm)
        if j < 4:
            act(mkT[:, j, :], mf.ap, AF.Copy, [mf.b], [mkT.b])
        bank, bb = pf()
        for mt in range(2):
            OP("pe", lambda e: e.transpose(bank[:, mt * 128:(mt + 1) * 128], mf[:, mt * 128:(mt + 1) * 128], identF.ap),
               [mf.b, identF.b], [bb], inc=(mt == 1))
        ms = mstg.next()
        act(ms.ap, bank[:, 0:256].rearrange("p (t d) -> p t d", t=2), AF.Copy, [bb], [ms.b])
        dst = memk if j < 4 else memv
        jj = j % 4
        store(dst[:, jj * 128:(jj + 1) * 128].rearrange("(t p) d -> p t d", p=128), ms.ap, ms.b)
        if j >= 4:
            OP("dve", lambda e: e.tensor_copy(mvtok[:, :, jj, :], ms.ap), [ms.b], [mvtok.b])
        tick(2)
    M.free(memT, *mkvF.tns)
    qxT = M.alloc("qxT", [128, 4, T_EXT], BF16); qxT.b.dj = True
    qxtm = M.alloc("qxtm", [4, 512], F32); qxtm.b.dj = True
    for h in range(4):
        slab = ws.next("qx")

        def evx(ci, c0, n, bank, bb, h=h):
            act(qxT[:, h, c0:c0 + n], bank[:, 0:n], AF.Copy, [bb], [qxT.b])
        proj(slab, 16, lambda k, c0, n: hT[:, k, c0:c0 + n], [hT.b], CH3, evx)
        bank, bb = pf()
        for k in range(16):
            OP("pe", lambda e: e.matmul(bank[0:4, 0:128], hT[:, k, SMP0:SMP0 + 4], slab[:, k, :], start=(k == 0), stop=(k == 15)),
               [hT.b, slab.b], [bb], inc=(k == 15))
        act(qxtm[0:4, h * 128:(h + 1) * 128], bank[0:4, 0:128], AF.Copy, [bb], [qxtm.b])
        tick(2)
    Pm = Ring([M.alloc("Pm%d" % i, [128, 2, 512], BF16) for i in range(2)])
    rD = Ring([M.alloc("rD%d" % i, [128, 512], F32) for i in range(2)])
    for h in range(4):
        for cc in range(2):
            cs = slice(OWN0 + 512 * cc, OWN0 + 512 * (cc + 1))
            P = Pm.next()
            for mt in range(2):
                bS, bbS = pf()
                OP("pe", lambda e: e.matmul(bS[:, 0:512], mkT[:, h, mt * 128:(mt + 1) * 128], qxT[:, h, cs], start=True, stop=True),
                   [mkT.b, qxT.b], [bbS])
                act(P[:, mt, :], bS[:, 0:512], AF.Exp, [bbS], [P.b], scale=SCALE)
            bO, bbO = pf()
            bD, bbD = pf()
            for mt in range(2):
                OP("pe", lambda e: e.matmul(bO[:, 0:512], mvtok[:, mt, h, :], P[:, mt, :], start=(mt == 0), stop=(mt == 1)),
                   [mvtok.b, P.b], [bbO], inc=(mt == 1))
            for mt in range(2):
                OP("pe", lambda e: e.matmul(bD[:, 0:512], onesB.ap, P[:, mt, :], start=(mt == 0), stop=(mt == 1)),
                   [onesB.b, P.b], [bbD], inc=(mt == 1))
            r_ = rD.next()
            act(r_.ap, bD[:, 0:512], AF.Ln, [bbD], [r_.b])
            act(r_.ap, r_.ap, AF.Exp, [r_.b], [r_.b], scale=-1.0)
            OP("dve", lambda e: e.tensor_tensor(branchT[:, 12 + h, cs], bO[:, 0:512], r_.ap, ALU.mult), [bbO, r_.b], [b_xat])
            tick(1)
    for _ in sgen:
        pass
    M.free(prod0, E0, E0s, qsF, ksF, vsF, qtm, tbs)
    M.free(*Pm.tns, *rD.tns, mkT, mvtok, *mstg.tns)
    xgen = sample_attend(qxtm, 512, 2, lambda s, g: cmk[s, g * 128:(g + 1) * 128, :], lambda s, g: cmv[s, g * 128:(g + 1) * 128, :],
                         None, 12, None)
    xstate = {"live": True}

    def xtick(k=1, drain=False):
        if not xstate["live"]:
            return
        for _ in range(10 ** 6 if drain else k):
            try:
                next(xgen)
            except StopIteration:
                xstate["live"] = False
                pstate["rot"] = [0, 1, 2, 3, 4, 5]
                M.free(qxT, qxtm)
                return
    dump("branchT", branchT, dt=BF16)

    mergedT = M.alloc("mergedT", [128, NCH, NT], BF16, top=True); mergedT.b.dj = True
    sgH = Ring([M.alloc("sgH%d" % i, [128, 343], F32) for i in range(6)])
    mH = Ring([M.alloc("mH%d" % i, [128, 343], F32) for i in range(6)])
    tH = Ring([M.alloc("tH%d" % i, [128, 343], F32) for i in range(3)])
    br_off = (0, 8, 12)
    br_nk = (8, 4, 4)
    br_buf = (b_conv, b_dil, b_xat)
    for c in range(16):
        mcur = [mH.next() for _ in range(3)]
        for br in range(3):
            slab_g = ws.next("gate")
            sgs = []

            def evg(ci, c0, n, bank, bb, br=br, c=c):
                sg = sgH.next()
                sgs.append(sg)
                act(sg[:, 0:n], bank[:, 0:n], AF.Sigmoid, [bb, bg.b], [sg.b], bias=bg[:, br * 16 + c:br * 16 + c + 1])
                xtick(2)
            proj(slab_g, 16, lambda k, c0, n: hT[:, k, c0:c0 + n], [hT.b], CH3, evg)
            if br == 2:
                xtick(drain=True)
            slab_w = ws.next("bw")

            def evw(ci, c0, n, bank, bb, br=br, c=c):
                sg = sgs[ci]
                m_ = mcur[ci]
                if br == 0:
                    OP("dve", lambda e: e.tensor_tensor(m_[:, 0:n], bank[:, 0:n], sg[:, 0:n], ALU.mult), [bb, sg.b], [m_.b])
                else:
                    t_ = tH.next()
                    OP("dve", lambda e: e.tensor_tensor(t_[:, 0:n], bank[:, 0:n], sg[:, 0:n], ALU.mult), [bb, sg.b], [t_.b])
                    if br == 1:
                        OP("dve", lambda e: e.tensor_tensor(m_[:, 0:n], m_[:, 0:n], t_[:, 0:n], ALU.add), [m_.b, t_.b], [m_.b])
                    else:
                        OP("dve", lambda e: e.tensor_tensor(mergedT[:, c, c0 - OWN0:c0 - OWN0 + n], m_[:, 0:n], t_[:, 0:n], ALU.add),
                           [m_.b, t_.b], [mergedT.b])
            o_ = br_off[br]
            proj(slab_w, br_nk[br], lambda k, c0, n: branchT[:, o_ + k, c0:c0 + n], [br_buf[br]], CH3, evw)
    M.free(hT, branchT, *sgH.tns, *mH.tns, *tH.tns)
    dump("mergedT", mergedT, dt=BF16)

    zT = M.alloc("zT", [128, NCH, NT], F32, top=True)
    z_bufs = [[zT.buf("z%d_%d" % (c, ci)) for ci in range(3)] for c in range(16)]
    all_z = [b for row in z_bufs for b in row]
    ssb_ap, ssb_buf = pf_fixed(5)

    def ss_matmuls(zsq):
        for t in range(9):
            n = 128 if t < 8 else 4
            for c in range(16):
                OP("pe", lambda e: e.matmul(ssb_ap[0:n, t:t + 1], zsq[:, c, t * 128:t * 128 + n], onesB[:, 0:1],
                                            start=(c == 0), stop=(c == 15)), [zsq.b, onesB.b], [ssb_buf], inc=(c == 15))

    zsq = M.alloc("zsq", [128, NCH, NT], BF16); zsq.b.dj = True
    pstate["rot"] = [0, 1, 2, 3, 4]
    for c in range(16):
        slab = ws.next("wout")

        def evz(ci, c0, n, bank, bb, c=c):
            act(zsq[:, c, c0 - OWN0:c0 - OWN0 + n], bank[:, 0:n], AF.Square, [bb], [zsq.b])
            act(zT[:, c, c0 - OWN0:c0 - OWN0 + n], bank[:, 0:n], AF.Copy, [bb, gpm.b], [z_bufs[c][ci]], scale=gpm[:, c:c + 1])
        proj(slab, 16, lambda k, c0, n: mergedT[:, k, c0 - OWN0:c0 - OWN0 + n], [mergedT.b], CH3, evz)
    zsq.b.dj = False
    ss_matmuls(zsq)
    M.free(mergedT, zsq)
    dump("zT1", zT)
    h2T = M.alloc("h2T", [128, NCH, NT], BF16, top=True); h2T.b.dj = True
    io_alloc(with_z=True)

    def post(res_src, res_is_dram_in, out_dst, g2_d, make_h2):
        if make_h2:
            load(io["gB"], io["gB"].ap, g2_d.partition_broadcast(128))
        rs1 = M.alloc("rs1", [128, 16], F32)
        for (pp, c0_, c1_) in ((128, 0, 8), (4, 8, 9)):
            OP("dve", lambda e: e.tensor_scalar(rs1[0:pp, c0_:c1_], ssb_ap[0:pp, c0_:c1_], 1.0 / D, EPS, ALU.mult, ALU.add),
               [ssb_buf], [rs1.b])
            OP("act", lambda e: e.sqrt(rs1[0:pp, c0_:c1_], rs1[0:pp, c0_:c1_]), [rs1.b], [rs1.b])
            OP("dve", lambda e: e.reciprocal(rs1[0:pp, c0_:c1_], rs1[0:pp, c0_:c1_]), [rs1.b], [rs1.b])
        st8 = {}

        def stL(t):
            n = 128 if t < 8 else 4
            xt = io["x"].next()
            S.dma("act", xt[0:n, :], res_src(t, n), xt.b, reads=([] if res_is_dram_in else [x1s_buf]), writes=[xt.b])
            st8[("x", t)] = xt

        def stP(t):
            n = 128 if t < 8 else 4
            xt = st8.pop(("x", t))
            zt = io["z"].next()
            for hlf in range(2):
                pbufs = [pf_bufs[2 * hlf], pf_bufs[2 * hlf + 1]]
                for c8 in range(8):
                    c = hlf * 8 + c8
                    OP("pe", lambda e: e.transpose(PSF[0:n, hlf * 1024 + c8 * 128:hlf * 1024 + (c8 + 1) * 128],
                                                   zT[:, c, t * 128:t * 128 + n], identF.ap),
                       all_z + [identF.b], [pbufs[c8 // 4]], inc=(c8 % 4 == 3))
                act(zt[0:n, hlf * 1024:(hlf + 1) * 1024], PSF[0:n, hlf * 1024:(hlf + 1) * 1024], AF.Copy, pbufs, [zt.b])
            st8[t] = (xt, zt, n)

        def stQ(t):
            xt, zt, n = st8.pop(t)
            OP("dve", lambda e: e.scalar_tensor_tensor(zt[0:n, :], zt[0:n, :], rs1[0:n, t:t + 1], xt[0:n, :], ALU.mult, ALU.add),
               [zt.b, rs1.b, xt.b], [zt.b])
            store(out_dst(t, n), zt[0:n, :], zt.b, final=not make_h2, dram_buf=(x1s_buf if make_h2 else None))
            if make_h2:
                st8[("n", t)] = norm_a(None, n, xt=zt)

        def stR(t):
            norm_r(st8[("n", t)])

        def stS1(t):
            norm_b1(st8[("n", t)], io["gB"])

        def stS2(t):
            n = 128 if t < 8 else 4
            norm_b2(st8.pop(("n", t)), lambda hlf: h2T[:, 8 * hlf:8 * hlf + 8, t * 128:t * 128 + n], h2T.b)
        pstate["rot"] = [4]
        if make_h2:
            sw_pipeline(9, [stL, stP, stQ, stR, stS1, stS2], order=[0, 3, 2, 1, 4, 5])
        else:
            sw_pipeline(9, [stL, stP, stQ], order=[0, 2, 1])
        pstate["rot"] = [0, 1, 2, 3, 4, 5]
        M.free(rs1)

    def res1(t, n):
        return x_own[t * 128:(t + 1) * 128, :] if t < 8 else x_smp

    def x1_ap(t, n):
        return x1s[t * 128:t * 128 + n, :]
    post(res1, True, x1_ap, g_pre_ffn, True)
    io_free()
    dump("h2T", h2T, dt=BF16)

    x1_bufs = []
    actT = M.alloc("actT", [128, 11, NT], BF16)
    sgF = Ring([M.alloc("sgF%d" % i, [128, 343], F32) for i in range(3)])
    tF = Ring([M.alloc("tF%d" % i, [128, 343], F32) for i in range(3)])
    zsq2 = M.alloc("zsq2", [128, NCH, NT], BF16); zsq2.b.dj = True
    pstate["rot"] = [0, 1, 2, 3, 4]
    for p in range(4):
        actT.b.dj = True
        for f in range(11):
            slab_g = ws.next("fg")
            slab_u = ws.next("fu")
            ts_ = []

            def evfg(ci, c0, n, bank, bb):
                sg = sgF.next()
                t_ = tF.next()
                ts_.append(t_)
                act(sg[:, 0:n], bank[:, 0:n], AF.Sigmoid, [bb], [sg.b])
                OP("dve", lambda e: e.tensor_tensor(t_[:, 0:n], bank[:, 0:n], sg[:, 0:n], ALU.mult), [bb, sg.b], [t_.b])
            proj(slab_g, 16, lambda k, c0, n: h2T[:, k, c0 - OWN0:c0 - OWN0 + n], [h2T.b], CH3, evfg)

            def evfu(ci, c0, n, bank, bb, f=f):
                t_ = ts_[ci]
                OP("dve", lambda e: e.tensor_tensor(actT[:, f, c0 - OWN0:c0 - OWN0 + n], bank[:, 0:n], t_[:, 0:n], ALU.mult),
                   [bb, t_.b], [actT.b])
            proj(slab_u, 16, lambda k, c0, n: h2T[:, k, c0 - OWN0:c0 - OWN0 + n], [h2T.b], CH3, evfu)
        actT.b.dj = False
        for c in range(16):
            slab = ws.next("fd")

            def evd(ci, c0, n, bank, bb, c=c, p=p):
                zb_ = z_bufs[c][ci]
                zv = zT[:, c, c0 - OWN0:c0 - OWN0 + n]
                if p == 0:
                    act(zv, bank[:, 0:n], AF.Copy, [bb], [zb_])
                else:
                    OP("dve", lambda e: e.tensor_tensor(zv, zv, bank[:, 0:n], ALU.add), [bb, zb_], [zb_])
                if p == 3:
                    act(zsq2[:, c, c0 - OWN0:c0 - OWN0 + n], zv, AF.Square, [zb_], [zsq2.b])
                    act(zv, zv, AF.Copy, [zb_, gpf.b], [zb_], scale=gpf[:, c:c + 1])
            proj(slab, 11, lambda k, c0, n: actT[:, k, c0 - OWN0:c0 - OWN0 + n], [actT.b], CH3, evd)
    zsq2.b.dj = False
    ss_matmuls(zsq2)
    M.free(actT, h2T, zsq2, *sgF.tns, *tF.tns)
    io_alloc(with_z=True)
    dump("zT2", zT)

    def y_ap(t, n):
        return y_own[t * 128:(t + 1) * 128, :] if t < 8 else y_smp
    post(lambda t, n: x1s[t * 128:t * 128 + n, :], False, y_ap, None, False)

    S.finish(out_bufs, "sp")
    assert ws.consumed == len(ws.plan), (ws.consumed, len(ws.plan))
    info = dict(n_wait=S.n_wait, n_ins=dict(S.n_ins), nsem=S.nsem, peak=M.peak - SB_BASE)
    return nc, info


def _slopes():
    i = np.arange(1, 13, dtype=np.float32)
    return np.exp2(-8.0 * i / 12.0).astype(np.float32).reshape(3, 4)


def _tables(hf):
    sl = _slopes()
    kp = np.arange(128, dtype=np.float32)[:, None]
    qi = np.arange(128, dtype=np.float32)[None, :]

    def band(g, first):
        d_ = float(DIL[g])
        out = np.empty((128, 4, 256), np.float32)
        for h in range(4):
            dist_p = qi + 128.0 - kp
            prev = np.where(dist_p <= 128.0, -sl[g, h] * dist_p * d_, NEGB)
            if first and hf == 0:
                prev = np.full_like(prev, NEGB)
            dist_c = qi - kp
            cur = np.where(dist_c >= 0.0, -sl[g, h] * dist_c * d_, NEGB)
            out[:, h, 0:128] = prev
            out[:, h, 128:256] = cur
        return np.ascontiguousarray(out.reshape(128, 1024))
    t2 = np.empty((128, 4, 4, 64), np.float32)
    qm = 64.0 + np.arange(64, dtype=np.float32)[None, :]
    dist = qm - kp
    valid = dist >= 0.0
    if hf == 0:
        valid = valid & (kp >= 64.0)
    for h in range(4):
        t2[:, h, :, :] = np.where(valid, -sl[2, h] * 16.0 * dist, NEGB)[:, None, :]
    ts = np.empty((128, 12), np.float32)
    m = np.arange(128, dtype=np.float32)
    for g in range(3):
        for h in range(4):
            ts[:, g * 4 + h] = -sl[g, h] * (128.0 - m) * float(DIL[g])
    return dict(tb0=band(0, False), tb0f=band(0, True), tb1=band(1, False), tb1f=band(1, True),
                tb2=np.ascontiguousarray(t2.reshape(128, 1024)), tbs=ts)


_CACHE = {}


def _fm(v, nch):
    return np.ascontiguousarray(np.asarray(v, np.float32).reshape(nch, 128).T)


def kernel(x_prompt, x_sample, cache_win_k, cache_win_v, state_conv, cache_mem_k, cache_mem_v, mem_prompt,
           g_pre_mix, w_in, b_gate, conv_w, conv_b, conv_ln_g, conv_ln_b, w_conv_out, w_dil_o,
           g_mem, w_mem_kv, w_x_o, w_out, g_post_mix, g_pre_ffn, w_ffn_gate, w_ffn_up, w_ffn_down, g_post_ffn,
           _dbg=()):
    f32 = lambda a: np.ascontiguousarray(np.asarray(a, dtype=np.float32))
    x_prompt = f32(x_prompt); x_sample = f32(x_sample)
    key = tuple(_dbg)
    if key not in _CACHE:
        _CACHE[key] = build_program(dbg=_dbg)
    nc, info = _CACHE[key]
    common = {
        "w_in": f32(w_in[0]), "w_conv_out": f32(w_conv_out[0]), "w_dil_o": f32(w_dil_o[0]), "w_mem_kv": f32(w_mem_kv[0]),
        "w_x_o": f32(w_x_o[0]), "w_out": f32(w_out[0]), "w_gate": f32(w_ffn_gate[0]), "w_up": f32(w_ffn_up[0]),
        "w_down": f32(w_ffn_down[0]),
        "g_pre_mix": f32(g_pre_mix[0:1]), "g_mem": f32(g_mem[0:1]), "g_post_mix": f32(g_post_mix[0:1]),
        "g_pre_ffn": f32(g_pre_ffn[0:1]), "g_post_ffn": f32(g_post_ffn[0:1]),
        "bgT": _fm(b_gate[0], 48), "gpmT": _fm(g_post_mix[0], 16), "gpfT": _fm(g_post_ffn[0], 16),
        "cwT": np.ascontiguousarray(np.asarray(conv_w[0], np.float32).T.reshape(8, 128, 31).transpose(1, 0, 2).reshape(128, 248)),
        "cbT": _fm(conv_b[0], 8), "lngT": _fm(conv_ln_g[0], 8), "lnbT": _fm(conv_ln_b[0], 8),
        "ident": np.eye(128, dtype=np.float32),
        "sel": np.ascontiguousarray(np.repeat(np.eye(4, dtype=np.float32)[:, :, None], 128, axis=2).reshape(4, 512)),
    }
    ck_all = np.asarray(cache_win_k, np.float32)[0].reshape(32, 2048, 512)
    cv_all = np.asarray(cache_win_v, np.float32)[0].reshape(32, 2048, 512)
    cmk_all = np.asarray(cache_mem_k, np.float32)[0].reshape(32, 256, 512)
    cmv_all = np.asarray(cache_mem_v, np.float32)[0].reshape(32, 256, 512)
    sconv_all = np.asarray(state_conv, np.float32)[0]
    mem_all = np.asarray(mem_prompt, np.float32)
    zeros_ctx = np.zeros((HALF, D), np.float32)
    in_maps = []
    for i in range(8):
        b, hf = i // 2, i % 2
        m = dict(common)
        m["x_own"] = f32(x_prompt[b, hf * HALF:(hf + 1) * HALF])
        m["x_ctx"] = f32(x_prompt[b, 0:HALF]) if hf == 1 else zeros_ctx
        m["x_smp"] = f32(x_sample[4 * i:4 * i + 4, 0])
        m["ck"] = f32(ck_all[4 * i:4 * i + 4]); m["cv"] = f32(cv_all[4 * i:4 * i + 4])
        m["sconv"] = f32(sconv_all[4 * i:4 * i + 4])
        m["cmk"] = f32(cmk_all[4 * i:4 * i + 4]); m["cmv"] = f32(cmv_all[4 * i:4 * i + 4])
        m["mem"] = f32(mem_all[b])
        m.update(_tables(hf))
        in_maps.append(m)
    res = run_bass_kernel_spmd(nc, in_maps, core_ids=list(range(8)))
    R = res.results
    kernel.last_results = R
    y_prompt = np.empty((4, SEQ, D), np.float32)
    wk = np.empty((1, 4, SEQ, 4, 128), np.float32)
    wv = np.empty((1, 4, SEQ, 4, 128), np.float32)
    conv_pr = np.empty((1, 4, 30, C_CONV), np.float32)
    mk = np.empty((1, 4, 256, 4, 128), np.float32)
    mv = np.empty((1, 4, 256, 4, 128), np.float32)
    y_sample = np.empty((32, 1, D), np.float32)
    wks = np.empty((1, 32, 1, 4, 128), np.float32)
    wvs = np.empty((1, 32, 1, 4, 128), np.float32)
    conv_sm = np.empty((1, 32, 30, C_CONV), np.float32)
    for i in range(8):
        b, hf = i // 2, i % 2
        r = R[i]
        y_prompt[b, hf * HALF:(hf + 1) * HALF] = r["y_own"]
        wk[0, b, hf * HALF:(hf + 1) * HALF] = r["wk_own"].reshape(HALF, 4, 128)
        wv[0, b, hf * HALF:(hf + 1) * HALF] = r["wv_own"].reshape(HALF, 4, 128)
        if hf == 1:
            conv_pr[0, b] = r["conv_p"]
        else:
            mk[0, b] = r["memk"].reshape(256, 4, 128)
            mv[0, b] = r["memv"].reshape(256, 4, 128)
        y_sample[4 * i:4 * i + 4, 0] = r["y_smp"]
        wks[0, 4 * i:4 * i + 4, 0] = r["wk_s"].reshape(4, 4, 128)
        wvs[0, 4 * i:4 * i + 4, 0] = r["wv_s"].reshape(4, 4, 128)
        conv_sm[0, 4 * i:4 * i + 4] = r["conv_s"]
    return (y_prompt, y_sample, wk, wv, conv_pr, mk, mv, wks, wvs, conv_sm)
```

```python
import os
import math
from contextlib import ExitStack
import numpy as np
import concourse.bass as bass
import concourse.mybir as mybir
from concourse.bass_utils import run_bass_kernel_spmd

F32 = mybir.dt.float32
BF16 = mybir.dt.bfloat16
AF = mybir.ActivationFunctionType
ALU = mybir.AluOpType
AX = mybir.AxisListType

D = 2048
NCH = 16
SEQ = 2048
HALF = 1024
C_CONV = 1024
N_IN = 11264
D_FF = 5632
NFF = 44
EPS = 1e-6
SCALE = 128 ** -0.5
NEGB = -30000.0
T_EXT = 1060
OWN0 = 32
SMP0 = 1056
NT = 1028
CH3 = [(32, 343), (375, 343), (718, 342)]
CH3U = [(0, 354), (354, 353), (707, 353)]
COL_A, COL_B, COL_Q, COL_K, COL_V, COL_QX, COL_G = 0, 1024, 2048, 3584, 4096, 4608, 5120
SB_BASE = 16512
SB_TOP = 229312
DIL = (1, 4, 16)
DEBUG = bool(os.environ.get("MK_DEBUG"))


class Buf:
    __slots__ = ("name", "w", "r", "lsem", "lcnt", "dj")

    def __init__(self, name="", dj=False):
        self.name = name
        self.w = {}
        self.r = []
        self.lsem = None
        self.lcnt = 0
        self.dj = dj


class Sched:
    ENG = ("pe", "dve", "act", "pool", "sp")

    def __init__(self, nc, stack):
        self.nc = nc
        self.stack = stack
        self.e = {"pe": nc.tensor, "dve": nc.vector, "act": nc.scalar, "pool": nc.gpsimd, "sp": nc.sync}
        self.sems = {}
        self.cnt = {}
        for k in self.ENG:
            self.sems[k] = stack.enter_context(nc.semaphore("s_" + k))
            self.cnt[k] = 0
        self.seen = {k: {} for k in self.ENG}
        self.nsem = 5
        self.dsem_pool = []
        self.n_wait = 0
        self.n_ins = {k: 0 for k in self.ENG}

    def _dsem(self, buf):
        if buf.lsem is None:
            key = "d%d" % self.nsem
            self.sems[key] = self.stack.enter_context(self.nc.semaphore(key))
            self.nsem += 1
            buf.lsem = key
        return buf.lsem

    def share_dsem(self, src, dst):
        dst.lsem = src.lsem
        dst.lcnt = src.lcnt

    def _wait(self, e, deps):
        best = {}
        for (k, v) in deps:
            if k == "pe" and e == "pe":
                continue
            if v > best.get(k, 0):
                best[k] = v
        for k, v in best.items():
            if self.seen[e].get(k, 0) >= v:
                continue
            self.e[e].wait_ge(self.sems[k], v)
            self.seen[e][k] = v
            self.n_wait += 1

    @staticmethod
    def _deps(reads, writes):
        deps = set()
        for b in reads:
            deps.update(b.w.items())
        for b in writes:
            if not b.dj:
                deps.update(b.w.items())
            deps.update(b.r)
        return deps

    @staticmethod
    def _mark(reads, writes, tick):
        for b in reads:
            if len(b.r) > 48:
                b.r = Sched._compress(b.r)
            b.r.append(tick)
        for b in writes:
            if b.dj:
                b.w[tick[0]] = max(b.w.get(tick[0], 0), tick[1])
            else:
                b.w = {tick[0]: tick[1]}
                b.r = []

    def op(self, e, fn, reads=(), writes=(), inc=True):
        self._wait(e, self._deps(reads, writes))
        ins = fn(self.e[e])
        self.n_ins[e] += 1
        if inc:
            self.cnt[e] += 1
            ins.then_inc(self.sems[e], 1)
            tick = (e, self.cnt[e])
        else:
            tick = (e, self.cnt[e] + 1)
        self._mark(reads, writes, tick)
        return ins

    @staticmethod
    def _compress(ticks):
        best = {}
        for (k, v) in ticks:
            if v > best.get(k, 0):
                best[k] = v
        return list(best.items())

    def dma(self, q, out, in_, sb, reads=(), writes=(), **kw):
        self._wait(q, self._deps(reads, writes))
        key = self._dsem(sb)
        ins = self.e[q].dma_start(out=out, in_=in_, **kw)
        self.n_ins[q] += 1
        sb.lcnt += 16
        ins.then_inc(self.sems[key], 16)
        tick = (key, sb.lcnt)
        self._mark(reads, writes, tick)
        return ins

    def finish(self, bufs, e="sp"):
        deps = set()
        for b in bufs:
            deps.update(b.w.items())
            deps.update(b.r)
        self._wait(e, deps)


class Tn:
    def __init__(self, ap, off, end, inherit, name):
        self.ap = ap
        self.off = off
        self.end = end
        self.inherit = inherit
        self.bufs = []
        self.name = name
        self.b = self.buf(name)

    def buf(self, name=None, dj=False):
        b = Buf(name or self.name, dj)
        b.r = list(self.inherit)
        self.bufs.append(b)
        return b

    def __getitem__(self, k):
        return self.ap[k]


class Mem:
    def __init__(self, arena, base, top):
        self.arena = arena
        self.base = base
        self.free_list = [(base, top)]
        self.retired = []
        self.peak = 0

    def view(self, off, shape, dt):
        n = int(np.prod(shape[1:]))
        sz = 4 if dt == F32 else 2
        e0 = (off - self.base) // 2
        ap = self.arena[0:shape[0], e0:e0 + n * sz // 2]
        if dt == F32:
            ap = ap.bitcast(F32)
        if len(shape) == 3:
            ap = ap.rearrange("p (a b) -> p a b", a=shape[1])
        elif len(shape) == 4:
            ap = ap.rearrange("p (a b c) -> p a b c", a=shape[1], b=shape[2])
        return ap

    def alloc(self, name, shape, dt, top=False):
        n = int(np.prod(shape[1:])) * (4 if dt == F32 else 2)
        n = (n + 63) // 64 * 64
        order = list(enumerate(self.free_list))
        if top:
            order = order[::-1]
        for i, (s, e) in order:
            if e - s >= n:
                if top:
                    off = e - n
                    if e - s == n:
                        self.free_list.pop(i)
                    else:
                        self.free_list[i] = (s, e - n)
                else:
                    off = s
                    if e - s == n:
                        self.free_list.pop(i)
                    else:
                        self.free_list[i] = (s + n, e)
                break
        else:
            raise RuntimeError("SBUF arena OOM for %s (%d bytes); free=%s" % (name, n, self.free_list))
        end = off + n
        self.peak = max(self.peak, end)
        ticks = set()
        keep = []
        for (rs, re, rt) in self.retired:
            if rs < end and re > off:
                ticks.update(rt)
                if rs >= off and re <= end:
                    continue
            keep.append((rs, re, rt))
        self.retired = keep
        return Tn(self.view(off, shape, dt), off, end, list(Sched._compress(ticks)), name)

    def free(self, *tns):
        for tn in tns:
            ticks = set(tn.inherit)
            for b in tn.bufs:
                ticks.update(b.w.items())
                ticks.update(b.r)
            self.retired.append((tn.off, tn.end, Sched._compress(ticks)))
            self.free_list.append((tn.off, tn.end))
            self.free_list.sort()
            merged = []
            for s, e in self.free_list:
                if merged and merged[-1][1] == s:
                    merged[-1] = (merged[-1][0], e)
                else:
                    merged.append((s, e))
            self.free_list = merged


class Ring:
    def __init__(self, tns):
        self.tns = tns
        self.i = 0

    def next(self):
        t = self.tns[self.i % len(self.tns)]
        self.i += 1
        return t


class WStream:
    def __init__(self, S, mem, ns):
        self.S = S
        self.slots = [mem.alloc("wslot%d" % i, [128, 16, 128], BF16) for i in range(ns)]
        self.plan = []
        self.issued = 0
        self.consumed = 0

    def add(self, tag, w, r0, nk, c0):
        self.plan.append((tag, w, r0, nk, c0))

    def _issue(self, i):
        tag, w, r0, nk, c0 = self.plan[i]
        slot = self.slots[i % len(self.slots)]
        src = w[r0:r0 + nk * 128, c0:c0 + 128].rearrange("(k p) n -> p k n", p=128)
        self.S.dma("pool", slot[:, 0:nk, :], src, slot.b, writes=[slot.b])

    def next(self, tag):
        i = self.consumed
        assert self.plan[i][0] == tag, (self.plan[i][0], tag)
        lim = min(len(self.plan), i + len(self.slots) - 1)
        while self.issued < lim:
            self._issue(self.issued)
            self.issued += 1
        self.consumed += 1
        return self.slots[i % len(self.slots)]


def _sl(start, count, step=1):
    return slice(start, start + step * (count - 1) + 1, step)


def build_program(dbg=()):
    nc = bass.Bass("TRN2", target_bir_lowering=False)

    def din(name, shape, dt=F32):
        return nc.dram_tensor(name, list(shape), dt, kind="ExternalInput").ap()

    def dout(name, shape, dt=F32):
        return nc.dram_tensor(name, list(shape), dt, kind="ExternalOutput").ap()

    x_own = din("x_own", [HALF, D]); x_ctx = din("x_ctx", [HALF, D]); x_smp = din("x_smp", [4, D])
    ck = din("ck", [4, 2048, 512]); cv = din("cv", [4, 2048, 512])
    sconv = din("sconv", [4, 30, C_CONV])
    cmk = din("cmk", [4, 256, 512]); cmv = din("cmv", [4, 256, 512])
    mem = din("mem", [256, D])
    w_in = din("w_in", [D, N_IN]); w_conv_out = din("w_conv_out", [C_CONV, D]); w_dil_o = din("w_dil_o", [512, D])
    w_mem_kv = din("w_mem_kv", [D, 1024]); w_x_o = din("w_x_o", [512, D]); w_out = din("w_out", [D, D])
    w_gate = din("w_gate", [D, D_FF]); w_up = din("w_up", [D, D_FF]); w_down = din("w_down", [D_FF, D])
    g_pre_mix = din("g_pre_mix", [1, D]); g_mem = din("g_mem", [1, D]); g_post_mix = din("g_post_mix", [1, D])
    g_pre_ffn = din("g_pre_ffn", [1, D]); g_post_ffn = din("g_post_ffn", [1, D])
    gpmT_d = din("gpmT", [128, 16]); gpfT_d = din("gpfT", [128, 16])
    bgT_d = din("bgT", [128, 48]); cwT_d = din("cwT", [128, 8 * 31]); cbT_d = din("cbT", [128, 8])
    lngT_d = din("lngT", [128, 8]); lnbT_d = din("lnbT", [128, 8])
    ident_d = din("ident", [128, 128]); sel_d = din("sel", [4, 512])
    tb0_d = din("tb0", [128, 1024]); tb0f_d = din("tb0f", [128, 1024])
    tb1_d = din("tb1", [128, 1024]); tb1f_d = din("tb1f", [128, 1024])
    tb2_d = din("tb2", [128, 1024]); tbs_d = din("tbs", [128, 12])

    y_own = dout("y_own", [HALF, D]); y_smp = dout("y_smp", [4, D])
    wk_own = dout("wk_own", [HALF, 512]); wv_own = dout("wv_own", [HALF, 512])
    conv_p = dout("conv_p", [30, C_CONV])
    memk = dout("memk", [256, 512]); memv = dout("memv", [256, 512])
    wk_s = dout("wk_s", [4, 512]); wv_s = dout("wv_s", [4, 512])
    conv_s = dout("conv_s", [4, 30, C_CONV])
    x1s = nc.dram_tensor("x1s", [HALF + 4, D], F32).ap()

    st = ExitStack()
    S = Sched(nc, st)
    arena = nc.alloc_sbuf_tensor_at("arena", [128, (SB_TOP - SB_BASE) // 2], BF16, offset=SB_BASE)
    M = Mem(arena, SB_BASE, SB_TOP)
    PSF = st.enter_context(nc.psum_tensor("PSF", [128, 3072], F32))
    PSB = st.enter_context(nc.psum_tensor("PSB", [128, 2048], BF16))
    pf_bufs = [Buf("pf%d" % i) for i in range(6)]
    pb_bufs = [Buf("pb%d" % i) for i in range(2)]
    pstate = {"f": 0, "b": 0, "rot": [0, 1, 2, 3, 4, 5], "pair": 0}
    out_bufs = []
    dbg_outs = {}

    def pf():
        rot = pstate["rot"]
        i = rot[pstate["f"] % len(rot)]
        pstate["f"] += 1
        return PSF[:, i * 512:(i + 1) * 512], pf_bufs[i]

    def pf_fixed(i):
        return PSF[:, i * 512:(i + 1) * 512], pf_bufs[i]

    def pb():
        i = pstate["b"] % 2
        pstate["b"] += 1
        return PSB[:, i * 1024:(i + 1) * 1024], pb_bufs[i]

    def OP(e, fn, R=(), W=(), inc=True):
        return S.op(e, fn, reads=R, writes=W, inc=inc)

    def act(out, in_, func, R, W, **kw):
        return OP("act", lambda e: e.activation(out, in_, func, **kw), R, W)

    def load(dst_tn, dst_ap, src, q="sp"):
        S.dma(q, dst_ap, src, dst_tn.b, writes=[dst_tn.b])

    x1s_buf = Buf("x1s", dj=True)

    def store(dst, src_ap, src_buf, final=True, dram_buf=None):
        S.dma("sp", dst, src_ap, src_buf, reads=[src_buf], writes=([dram_buf] if dram_buf is not None else []))
        if final and src_buf not in out_bufs:
            out_bufs.append(src_buf)

    def dump(name, tn, ap=None, dt=F32, shape=None):
        if name not in dbg:
            return
        ap = tn.ap if ap is None else ap
        shape = list(ap.shape) if shape is None else shape
        d = dout("dbg_" + name, shape, dt)
        S.dma("sp", d, ap, tn.b, reads=[tn.b])
        out_bufs.append(tn.b)

    ws = WStream(S, M, 5)
    for j in range(8):
        ws.add("ctxkv", w_in, 0, 16, COL_K + j * 128)
    for c in range(8):
        ws.add("a", w_in, 0, 16, COL_A + c * 128)
        ws.add("b", w_in, 0, 16, COL_B + c * 128)
    for h in range(4):
        ws.add("k", w_in, 0, 16, COL_K + h * 128)
        ws.add("v", w_in, 0, 16, COL_V + h * 128)
        for g in range(3):
            ws.add("q", w_in, 0, 16, COL_Q + (g * 4 + h) * 128)
    for j in range(8):
        ws.add("memkv", w_mem_kv, 0, 16, j * 128)
    for h in range(4):
        ws.add("qx", w_in, 0, 16, COL_QX + h * 128)
    for c in range(16):
        for br in range(3):
            ws.add("gate", w_in, 0, 16, COL_G + br * D + c * 128)
            ws.add("bw", (w_conv_out, w_dil_o, w_x_o)[br], 0, (8, 4, 4)[br], c * 128)
    for c in range(16):
        ws.add("wout", w_out, 0, 16, c * 128)
    for p in range(4):
        for f in range(11 * p, 11 * p + 11):
            ws.add("fg", w_gate, 0, 16, f * 128)
            ws.add("fu", w_up, 0, 16, f * 128)
        for c in range(16):
            ws.add("fd", w_down, p * 11 * 128, 11, c * 128)

    identF = M.alloc("identF", [128, 128], F32); load(identF, identF.ap, ident_d)
    identB = M.alloc("identB", [128, 128], BF16)
    OP("dve", lambda e: e.tensor_copy(identB.ap, identF.ap), [identF.b], [identB.b])
    onesB = M.alloc("onesB", [128, 128], BF16); OP("dve", lambda e: e.memset(onesB.ap, 1.0), [], [onesB.b])
    onesF = M.alloc("onesF", [128, 128], F32); OP("dve", lambda e: e.memset(onesF.ap, 1.0), [], [onesF.b])
    bg = M.alloc("bg", [128, 48], F32); load(bg, bg.ap, bgT_d)
    cw = M.alloc("cw", [128, 8, 31], F32); load(cw, cw.ap, cwT_d.rearrange("p (c j) -> p c j", c=8))
    cb = M.alloc("cb", [128, 8], F32); load(cb, cb.ap, cbT_d)
    lng = M.alloc("lng", [128, 8], F32); load(lng, lng.ap, lngT_d)
    lnb = M.alloc("lnb", [128, 8], F32); load(lnb, lnb.ap, lnbT_d)
    gpm = M.alloc("gpm", [128, 16], F32); load(gpm, gpm.ap, gpmT_d)
    gpf = M.alloc("gpf", [128, 16], F32); load(gpf, gpf.ap, gpfT_d)
    sel = M.alloc("sel", [4, 4, 128], F32); load(sel, sel.ap, sel_d.rearrange("p (s m) -> p s m", s=4))
    smalls = M.alloc("smalls", [128, 64], F32)
    sm_bufs = [smalls.buf("sm%d" % i) for i in range(8)]
    sm_state = {"i": 0}

    def small2():
        i = sm_state["i"] % 8
        sm_state["i"] += 1
        return smalls[:, 2 * i:2 * i + 1], smalls[:, 2 * i + 1:2 * i + 2], sm_bufs[i]

    io = {}

    def io_alloc(with_z=False):
        io["x"] = Ring([M.alloc("xr%d" % i, [128, D], F32) for i in range(3 if with_z else 5)])
        if with_z:
            io["z"] = Ring([M.alloc("zr%d" % i, [128, D], F32) for i in range(4)])
        io["hb"] = Ring([M.alloc("hb%d" % i, [128, D], BF16) for i in range(4)])
        if not with_z:
            io["gA"] = M.alloc("gA", [128, D], F32)
        io["gB"] = M.alloc("gB", [128, D], F32)

    def io_free():
        M.free(*io["x"].tns, *io["hb"].tns, io["gB"])
        if "gA" in io:
            M.free(io["gA"])
        if "z" in io:
            M.free(*io["z"].tns)
        io.clear()

    def rstd_chain(ssap, rsap, sb, n):
        OP("dve", lambda e: e.tensor_scalar(rsap[0:n], ssap[0:n], 1.0 / D, EPS, ALU.mult, ALU.add), [sb], [sb])
        OP("act", lambda e: e.sqrt(rsap[0:n], rsap[0:n]), [sb], [sb])
        OP("dve", lambda e: e.reciprocal(rsap[0:n], rsap[0:n]), [sb], [sb])

    def norm_a(src, n, xt=None):
        if xt is None:
            xt = io["x"].next()
            S.dma("sp", xt[0:n, :], src, xt.b, writes=[xt.b])
        hb = io["hb"].next()
        ss, rs, sb = small2()
        OP("dve", lambda e: e.memset(ss[0:n], 0.0), [], [sb])
        act(hb[0:n, :], xt[0:n, :], AF.Square, [xt.b], [hb.b, sb], accum_out=ss[0:n])
        return (xt, hb, ss, rs, sb, n)

    def norm_r(ctx):
        xt, hb, ss, rs, sb, n = ctx
        rstd_chain(ss, rs, sb, n)

    def norm_b1(ctx, gtn):
        xt, hb, ss, rs, sb, n = ctx
        OP("dve", lambda e: e.scalar_tensor_tensor(hb[0:n, :], xt[0:n, :], rs[0:n], gtn[0:n, :], ALU.mult, ALU.mult),
           [xt.b, sb, gtn.b], [hb.b])

    def norm_b2(ctx, dst_fn, dst_buf):
        xt, hb, ss, rs, sb, n = ctx
        for hlf in range(2):
            bank, bb = pb()
            for c in range(8):
                cc = hlf * 8 + c
                OP("pe", lambda e: e.transpose(bank[:, c * 128:c * 128 + n], hb[0:n, cc * 128:(cc + 1) * 128], identB[0:n, 0:n]),
                   [hb.b, identB.b], [bb], inc=(c == 7))
            bv = bank.rearrange("p (c t) -> p c t", c=8)
            if hlf == 0:
                act(dst_fn(hlf), bv[:, :, 0:n], AF.Copy, [bb], [dst_buf])
            else:
                OP("dve", lambda e: e.tensor_copy(dst_fn(hlf), bv[:, :, 0:n]), [bb], [dst_buf])

    def sw_pipeline(n, stages, order=None):
        if order is None:
            order = list(reversed(range(len(stages))))
        for step in range(n + len(stages) - 1):
            for si in order:
                t = step - si
                if 0 <= t < n:
                    stages[si](t)

    def proj(slab, nk, rhs_fn, rhs_bufs, chunks, evac):
        for ci, (c0, n) in enumerate(chunks):
            bank, bb = pf()
            for k in range(nk):
                OP("pe", lambda e: e.matmul(bank[:, 0:n], slab[:, k, :], rhs_fn(k, c0, n), start=(k == 0), stop=(k == nk - 1)),
                   [slab.b] + rhs_bufs, [bb], inc=(k == nk - 1))
            evac(ci, c0, n, bank, bb)

    def transpose_out(src_ap, src_buf, ntok_tiles, n_last, dst_fn):
        for t in range(ntok_tiles):
            n = 128 if t < ntok_tiles - 1 or n_last == 128 else n_last
            bank, bb = pf()
            OP("pe", lambda e: e.transpose(bank[0:n, 0:128], src_ap[:, t * 128:t * 128 + n], identF.ap), [src_buf, identF.b], [bb])
            dst_fn(t, n, bank, bb)

    io_alloc()
    load(io["gA"], io["gA"].ap, g_pre_mix.partition_broadcast(128))
    kvT = M.alloc("kvT", [128, 8, SEQ], BF16, top=True)
    hT = M.alloc("hT", [128, NCH, T_EXT], BF16, top=True)
    hTc = M.alloc("hTc", [128, NCH, HALF], BF16)
    memT = M.alloc("memT", [128, NCH, 256], BF16, top=True)
    load(io["gB"], io["gB"].ap, g_mem.partition_broadcast(128))
    hT.b.dj = True
    memT.b.dj = True
    hTc.b.dj = True
    kvT.b.dj = True
    tiles = []

    bgq = []

    def ctx_after():
        def tail():
            act(hT[:, :, 0:32], hTc[:, :, HALF - 32:HALF], AF.Copy, [hTc.b], [hT.b])
        bgq.append(tail)
        pend = {}
        for j in range(8):
            def mm(j=j):
                slab = ws.next("ctxkv")
                got = []

                def ev(ci, c0, n, bank, bb):
                    got.append((c0, n, bank, bb))
                proj(slab, 16, lambda k, c0, n: hTc[:, k, c0:c0 + n], [hTc.b], [(0, 512), (512, 512)], ev)
                pend[j] = got

            def evac(j=j):
                for (c0, n, bank, bb) in pend.pop(j):
                    act(kvT[:, j, c0:c0 + n], bank[:, 0:n], AF.Copy, [bb], [kvT.b])
            if j == 0:
                bgq.append(mm)
            else:
                bgq.append(lambda mm=mm, evp=prev_ev: (evp(), mm()))
            prev_ev = evac
        bgq.append(prev_ev)
    for t in range(8):
        tiles.append((x_ctx[t * 128:(t + 1) * 128, :], 128, "gA",
                      (lambda hlf, t=t: hTc[:, 8 * hlf:8 * hlf + 8, t * 128:(t + 1) * 128]), hTc.b,
                      ctx_after if t == 7 else None))
    for t in range(8):
        tiles.append((x_own[t * 128:(t + 1) * 128, :], 128, "gA",
                      (lambda hlf, t=t: hT[:, 8 * hlf:8 * hlf + 8, OWN0 + t * 128:OWN0 + (t + 1) * 128]), hT.b, None))
    tiles.append((x_smp, 4, "gA", (lambda hlf: hT[:, 8 * hlf:8 * hlf + 8, SMP0:SMP0 + 4]), hT.b, None))
    for t in range(2):
        tiles.append((mem[t * 128:(t + 1) * 128, :], 128, "gB",
                      (lambda hlf, t=t: memT[:, 8 * hlf:8 * hlf + 8, t * 128:(t + 1) * 128]), memT.b, None))
    nctx = {}

    def st_a(i):
        nctx[i] = norm_a(tiles[i][0], tiles[i][1])

    def st_r(i):
        norm_r(nctx[i])

    def st_b1(i):
        norm_b1(nctx[i], io[tiles[i][2]])

    def st_b2(i):
        src, n, gk, dst_fn, dst_buf, hook = tiles[i]
        norm_b2(nctx.pop(i), dst_fn, dst_buf)
        if hook is not None:
            hook()
        elif bgq:
            bgq.pop(0)()
    sw_pipeline(len(tiles), [st_a, st_r, st_b1, st_b2], order=[1, 0, 2, 3])
    while bgq:
        bgq.pop(0)()
    hT.b.dj = False
    memT.b.dj = False
    kvT.b.dj = False
    M.free(hTc)
    io_free()
    dump("hT", hT, dt=BF16)
    dump("kvT_ctx", kvT, dt=BF16)

    branchT = M.alloc("branchT", [128, NCH, T_EXT], BF16, top=True)
    b_conv = branchT.b
    b_conv.dj = True
    b_dil = branchT.buf("dil", dj=True)
    b_xat = branchT.buf("xat", dj=True)
    yconv = M.alloc("yconv", [128, 8, NT], F32)
    yconv.b.dj = True
    utail = M.alloc("utail", [128, 8, 32], F32); utail.b.dj = True
    usmp = M.alloc("usmp", [128, 8, 4], F32); usmp.b.dj = True
    fulls = M.alloc("fulls", [128, 8, 4, 31], F32); fulls.b.dj = True
    ucr = Ring([M.alloc("uc%d" % i, [128, T_EXT], F32) for i in range(2)])
    sgr = Ring([M.alloc("sgd%d" % i, [128, 354], F32) for i in range(3)])
    scr = Ring([M.alloc("sc%d" % i, [32, C_CONV], F32) for i in range(2)])

    for s in range(4):
        sc = scr.next()
        S.dma("sp", sc[0:30, :], sconv[s], sc.b, writes=[sc.b])
        bank, bb = pf()
        for c in range(8):
            OP("pe", lambda e: e.transpose(bank[:, c * 32:c * 32 + 30], sc[0:30, c * 128:(c + 1) * 128], identF[0:30, 0:30]),
               [sc.b, identF.b], [bb], inc=(c == 7))
        act(fulls[:, :, s, 0:30], bank[:, 0:256].rearrange("p (c t) -> p c t", c=8)[:, :, 0:30], AF.Copy, [bb], [fulls.b])
        store(conv_s[s, 0:29, :], sc[1:30, :], sc.b)

    M.free(*scr.tns)
    ubr = Ring([M.alloc("ub%d" % i, [128, T_EXT], BF16) for i in range(2)])
    dgr = Ring([M.alloc("dg%d" % i, [128, 31, 128], BF16) for i in range(2)])
    for c in range(8):
        slab_a = ws.next("a")
        slab_b = ws.next("b")
        uc = ucr.next()
        ub = ubr.next()
        dg = dgr.next()
        OP("dve", lambda e: e.tensor_tensor(dg.ap, identB.ap.unsqueeze(1).to_broadcast([128, 31, 128]),
                                            cw[:, c, :].unsqueeze(2).to_broadcast([128, 31, 128]), ALU.mult),
           [identB.b, cw.b], [dg.b])
        for ci, (c0, n) in enumerate(CH3U):
            bka, bba = pf()
            for k in range(16):
                OP("pe", lambda e: e.matmul(bka[:, 0:n], slab_a[:, k, :], hT[:, k, c0:c0 + n], start=(k == 0), stop=(k == 15)),
                   [slab_a.b, hT.b], [bba], inc=(k == 15))
            bkb, bbb = pf()
            for k in range(16):
                OP("pe", lambda e: e.matmul(bkb[:, 0:n], slab_b[:, k, :], hT[:, k, c0:c0 + n], start=(k == 0), stop=(k == 15)),
                   [slab_b.b, hT.b], [bbb], inc=(k == 15))
            sg = sgr.next()
            act(sg[:, 0:n], bkb[:, 0:n], AF.Sigmoid, [bbb], [sg.b])
            OP("dve", lambda e: e.tensor_tensor(uc[:, c0:c0 + n], bka[:, 0:n], sg[:, 0:n], ALU.mult), [bba, sg.b], [uc.b])
        act(ub.ap, uc.ap, AF.Copy, [uc.b], [ub.b])
        OP("dve", lambda e: e.tensor_copy(utail[:, c, 0:30], uc[:, 1026:1056]), [uc.b], [utail.b])
        OP("dve", lambda e: e.tensor_copy(usmp[:, c, :], uc[:, SMP0:SMP0 + 4]), [uc.b], [usmp.b])
        OP("dve", lambda e: e.tensor_copy(fulls[:, c, :, 30], uc[:, SMP0:SMP0 + 4]), [uc.b], [fulls.b])
        for hv in range(2):
            bank, bb = pf()
            for j in range(31):
                o_ = 2 + j + 512 * hv
                OP("pe", lambda e: e.matmul(bank[:, 0:512], dg[:, j, :], ub[:, o_:o_ + 512], start=(j == 0), stop=(j == 30)),
                   [dg.b, ub.b], [bb], inc=(j == 30))
            act(yconv[:, c, 512 * hv:512 * (hv + 1)], bank[:, 0:512], AF.Identity, [bb, cb.b], [yconv.b], bias=cb[:, c:c + 1])
    M.free(*ubr.tns, *dgr.tns)
    urow = M.alloc("urow", [32, C_CONV], F32)
    dump("yconv", yconv)
    for (src, n, dst) in ((utail, 30, conv_p), (usmp, 4, None)):
        for hlf in range(2):
            bank, bb = pf()
            for c4 in range(4):
                c = hlf * 4 + c4
                OP("pe", lambda e: e.transpose(bank[0:n, c4 * 128:(c4 + 1) * 128], src[:, c, 0:n], identF.ap),
                   [src.b, identF.b], [bb], inc=(c4 == 3))
            act(urow[0:n, hlf * 512:(hlf + 1) * 512], bank[0:n, 0:512], AF.Copy, [bb], [urow.b])
        if dst is not None:
            store(dst, urow[0:30, :], urow.b)
        else:
            store(conv_s[:, 29, :], urow[0:4, :], urow.b)
    prods = M.alloc("prods", [128, 8, 4, 31], F32)
    ysm = M.alloc("ysm", [128, 8, 4], F32)
    for s in range(4):
        OP("dve", lambda e: e.tensor_tensor(prods[:, :, s, :], fulls[:, :, s, :], cw.ap, ALU.mult), [fulls.b, cw.b], [prods.b])
    OP("dve", lambda e: e.reduce_sum(ysm.ap.rearrange("p c s -> p (c s)"), prods.ap.rearrange("p c s j -> p (c s) j"), AX.X),
       [prods.b], [ysm.b])
    for s in range(4):
        OP("dve", lambda e: e.tensor_tensor(yconv[:, :, HALF + s], ysm[:, :, s], cb.ap, ALU.add), [ysm.b, cb.b], [yconv.b])
    M.free(prods, ysm, urow, *ucr.tns, *sgr.tns)
    LNCH = [(0, 343), (343, 343), (686, 342)]
    lnt = Ring([M.alloc("lnt%d" % i, [128, 343], F32) for i in range(3)])
    mus = [M.alloc("mu%d" % i, [128, 343], F32) for i in range(3)]
    rsds = [M.alloc("rsd%d" % i, [128, 343], F32) for i in range(3)]
    for li, (c0, n) in enumerate(LNCH):
        mu, rsd = mus[li], rsds[li]
        b1, bb1 = pf()
        b2, bb2 = pf()
        for c in range(8):
            OP("pe", lambda e: e.matmul(b1[:, 0:n], onesF.ap, yconv[:, c, c0:c0 + n], start=(c == 0), stop=(c == 7)),
               [onesF.b, yconv.b], [bb1], inc=(c == 7))
        for c in range(8):
            ysq = lnt.next()
            act(ysq[:, 0:n], yconv[:, c, c0:c0 + n], AF.Square, [yconv.b], [ysq.b])
            OP("pe", lambda e: e.matmul(b2[:, 0:n], onesF.ap, ysq[:, 0:n], start=(c == 0), stop=(c == 7)),
               [onesF.b, ysq.b], [bb2], inc=True)
        act(mu[:, 0:n], b1[:, 0:n], AF.Copy, [bb1], [mu.b], scale=1.0 / C_CONV)
        OP("dve", lambda e: e.tensor_tensor(rsd[:, 0:n], mu[:, 0:n], mu[:, 0:n], ALU.mult), [mu.b], [rsd.b])
        OP("dve", lambda e: e.scalar_tensor_tensor(rsd[:, 0:n], b2[:, 0:n], 1.0 / C_CONV, rsd[:, 0:n], ALU.mult, ALU.subtract),
           [bb2, rsd.b], [rsd.b])
        OP("dve", lambda e: e.tensor_scalar(rsd[:, 0:n], rsd[:, 0:n], 1.0, EPS, ALU.mult, ALU.add), [rsd.b], [rsd.b])
        OP("act", lambda e: e.sqrt(rsd[:, 0:n], rsd[:, 0:n]), [rsd.b], [rsd.b])
        OP("dve", lambda e: e.reciprocal(rsd[:, 0:n], rsd[:, 0:n]), [rsd.b], [rsd.b])
    items = [(li, c) for li in range(3) for c in range(8)]
    ynr = Ring([M.alloc("yn%d" % i, [128, 343], F32) for i in range(3)])
    t1r = Ring([M.alloc("t1_%d" % i, [128, 343], F32) for i in range(3)])
    sgl = Ring([M.alloc("sgl%d" % i, [128, 343], F32) for i in range(3)])
    lnst = {}

    def ln1(i):
        li, c = items[i]
        c0, n = LNCH[li]
        t1 = t1r.next()
        OP("dve", lambda e: e.tensor_tensor(t1[:, 0:n], yconv[:, c, c0:c0 + n], mus[li][:, 0:n], ALU.subtract), [yconv.b, mus[li].b], [t1.b])
        OP("dve", lambda e: e.tensor_tensor(t1[:, 0:n], t1[:, 0:n], rsds[li][:, 0:n], ALU.mult), [t1.b, rsds[li].b], [t1.b])
        lnst[i] = t1

    def ln2(i):
        li, c = items[i]
        c0, n = LNCH[li]
        t1 = lnst[i]
        yn = ynr.next()
        sg = sgl.next()
        act(yn[:, 0:n], t1[:, 0:n], AF.Identity, [t1.b, lng.b, lnb.b], [yn.b], scale=lng[:, c:c + 1], bias=lnb[:, c:c + 1])
        act(sg[:, 0:n], t1[:, 0:n], AF.Sigmoid, [t1.b, lng.b, lnb.b], [sg.b], scale=lng[:, c:c + 1], bias=lnb[:, c:c + 1])
        lnst[i] = (yn, sg)

    def ln3(i):
        li, c = items[i]
        c0, n = LNCH[li]
        yn, sg = lnst.pop(i)
        OP("dve", lambda e: e.tensor_tensor(branchT[:, c, OWN0 + c0:OWN0 + c0 + n], yn[:, 0:n], sg[:, 0:n], ALU.mult),
           [yn.b, sg.b], [b_conv])
    sw_pipeline(len(items), [ln1, ln2, ln3], order=[2, 0, 1])
    M.free(*ynr.tns)
    mu, rsd = mus[0], rsds[0]
    M.free(yconv, utail, usmp, fulls, *mus, *rsds, *lnt.tns, *t1r.tns, *sgl.tns)
    dump("uconvT", branchT, ap=branchT[:, 0:8, :], dt=BF16)

    kvT.b.dj = True
    tb = {}
    for nm, d_ in (("tb0", tb0_d), ("tb0f", tb0f_d), ("tb1", tb1_d), ("tb1f", tb1f_d), ("tb2", tb2_d)):
        tb[nm] = M.alloc(nm, [128, 4, 256], F32)
        load(tb[nm], tb[nm].ap, d_.rearrange("p (h q) -> p h q", h=4))
    tbs = M.alloc("tbs", [128, 12], F32); load(tbs, tbs.ap, tbs_d)
    qsF = M.alloc("qsF", [128, 3, 4, 4], F32); qsF.b.dj = True
    ksF = M.alloc("ksF", [128, 4, 4], F32); ksF.b.dj = True
    vsF = M.alloc("vsF", [128, 4, 4], F32); vsF.b.dj = True
    qtm = M.alloc("qtm", [4, 1536], F32); qtm.b.dj = True
    qT = M.alloc("qT", [128, 3, T_EXT], BF16)
    kf = M.alloc("kf", [128, NT], F32)
    stg = Ring([M.alloc("stg%d" % i, [128, 8, 128], F32) for i in range(2)])
    srow = Ring([M.alloc("srow%d" % i, [4, 128], F32) for i in range(2)])
    Vtok = M.alloc("Vtok", [128, 37, 128], BF16)
    accOD = M.alloc("accOD", [128, 2, HALF], F32)
    str_ = Ring([M.alloc("st%d" % i, [128, 256], F32) for i in range(4)])
    Pr = Ring([M.alloc("P%d" % i, [128, 256], BF16) for i in range(4)])

    def kv_slab(tag, h, j, out_own, out_smp, smpF):
        slab = ws.next(tag)

        def ev(ci, c0, n, bank, bb):
            act(kf[:, c0 - OWN0:c0 - OWN0 + n], bank[:, 0:n], AF.Copy, [bb], [kf.b])
        kf.b.dj = True
        proj(slab, 16, lambda k, c0, n: hT[:, k, c0:c0 + n], [hT.b], CH3, ev)
        kf.b.dj = False
        act(kvT[:, j, HALF:SEQ], kf[:, 0:HALF], AF.Copy, [kf.b], [kvT.b])
        OP("dve", lambda e: e.tensor_copy(smpF[:, h, :], kf[:, HALF:HALF + 4]), [kf.b], [smpF.b])
        sg_ = stg.next()
        for hlf in range(2):
            bank, bb = pf()
            for t4 in range(4):
                t = hlf * 4 + t4
                OP("pe", lambda e: e.transpose(bank[:, t4 * 128:(t4 + 1) * 128], kf[:, t * 128:(t + 1) * 128], identF.ap),
                   [kf.b, identF.b], [bb], inc=(t4 == 3))
            act(sg_[:, hlf * 4:hlf * 4 + 4, :], bank[:, 0:512].rearrange("p (t d) -> p t d", t=4), AF.Copy, [bb], [sg_.b])
        store(out_own[:, h * 128:(h + 1) * 128].rearrange("(t p) d -> p t d", p=128), sg_.ap, sg_.b)
        bank, bb = pf()
        OP("pe", lambda e: e.transpose(bank[0:4, 0:128], kf[:, HALF:HALF + 4], identF.ap), [kf.b, identF.b], [bb])
        sr = srow.next()
        act(sr.ap, bank[0:4, 0:128], AF.Copy, [bb], [sr.b])
        store(out_smp[:, h * 128:(h + 1) * 128], sr.ap, sr.b)

    for h in range(4):
        kv_slab("k", h, h, wk_own, wk_s, ksF)
        kv_slab("v", h, 4 + h, wv_own, wv_s, vsF)
        for g in range(3):
            slab = ws.next("q")

            def evq(ci, c0, n, bank, bb, g=g, h=h):
                act(qT[:, g, c0:c0 + n], bank[:, 0:n], AF.Copy, [bb], [qT.b])
                if ci == 2:
                    OP("dve", lambda e: e.tensor_copy(qsF[:, g, h, :], bank[:, SMP0 - c0:SMP0 - c0 + 4]), [bb], [qsF.b])
            qT.b.dj = True
            proj(slab, 16, lambda k, c0, n: hT[:, k, c0:c0 + n], [hT.b], CH3, evq)
            qT.b.dj = False
            bank, bb = pf()
            for k in range(16):
                OP("pe", lambda e: e.matmul(bank[0:4, 0:128], hT[:, k, SMP0:SMP0 + 4], slab[:, k, :], start=(k == 0), stop=(k == 15)),
                   [hT.b, slab.b], [bb], inc=(k == 15))
            act(qtm[0:4, (g * 4 + h) * 128:(g * 4 + h + 1) * 128], bank[0:4, 0:128], AF.Copy, [bb], [qtm.b])
        sels = [slice(128 * (7 + i), 128 * (8 + i)) for i in range(9)]
        for r in range(4):
            for nl in (1, 2, 3):
                sels.append(_sl(512 * nl + r, 128, 4))
        for r in range(16):
            sels.append(_sl(r, 128, 16))
        Vtok.b.dj = True
        for i0 in range(0, 37, 8):
            bank, bb = pb()
            cnt = min(8, 37 - i0)
            for ii in range(cnt):
                OP("pe", lambda e: e.transpose(bank[:, ii * 128:(ii + 1) * 128], kvT[:, 4 + h, sels[i0 + ii]], identB.ap),
                   [kvT.b, identB.b], [bb], inc=(ii == cnt - 1))
            act(Vtok[:, i0:i0 + cnt, :], bank[:, 0:cnt * 128].rearrange("p (i d) -> p i d", i=cnt), AF.Copy, [bb], [Vtok.b])
        Vtok.b.dj = False

        def unit(ksel, vidx, qsel, tbl_ap, tbl_buf, nq):
            nk_ = len(ksel)
            bS, bbS = pf()
            for i_, ks_ in enumerate(ksel):
                OP("pe", lambda e: e.matmul(bS[:, i_ * nq:(i_ + 1) * nq], kvT[:, h, ks_], qT[:, qsel[0], qsel[1]], start=True, stop=True),
                   [kvT.b, qT.b], [bbS], inc=(i_ == nk_ - 1))
            stt = str_.next()
            P = Pr.next()
            w_ = nk_ * nq
            OP("dve", lambda e: e.scalar_tensor_tensor(stt[:, 0:w_], bS[:, 0:w_], SCALE, tbl_ap, ALU.mult, ALU.add),
               [bbS, tbl_buf], [stt.b])
            act(P[:, 0:w_], stt[:, 0:w_], AF.Exp, [stt.b], [P.b])
            return P, w_

        units = []

        def mk_band(g_, tbl, ip, ic, qs_, accv, first):
            def s1():
                return unit([sels[ip], sels[ic]], None, (g_, qs_), tbl[:, h, :], tbl.b, 128)

            def s2(P):
                b2, bb2 = pf()
                OP("pe", lambda e: e.matmul(b2[:, 0:128], Vtok[:, ip, :], P[:, 0:128], start=True, stop=False), [Vtok.b, P.b], [bb2], inc=False)
                OP("pe", lambda e: e.matmul(b2[:, 0:128], Vtok[:, ic, :], P[:, 128:256], start=False, stop=True), [Vtok.b, P.b], [bb2], inc=False)
                OP("pe", lambda e: e.matmul(b2[:, 128:256], onesB.ap, P[:, 0:128], start=True, stop=False), [onesB.b, P.b], [bb2], inc=False)
                OP("pe", lambda e: e.matmul(b2[:, 128:256], onesB.ap, P[:, 128:256], start=False, stop=True), [onesB.b, P.b], [bb2])
                src = b2[:, 0:256].rearrange("p (o q) -> p o q", o=2)
                if first:
                    OP("dve", lambda e: e.tensor_copy(accv, src), [bb2], [accOD.b])
                else:
                    OP("dve", lambda e: e.tensor_tensor(accv, accv, src, ALU.add), [bb2, accOD.b], [accOD.b])
            return (s1, s2)
        for i in range(8):
            tbl = tb["tb0f"] if i == 0 else tb["tb0"]
            units.append(mk_band(0, tbl, i, i + 1, slice(OWN0 + 128 * i, OWN0 + 128 * (i + 1)),
                                 accOD[:, :, 128 * i:128 * (i + 1)], True))
        for r in range(4):
            for nl in (2, 3):
                tbl = tb["tb1f"] if nl == 2 else tb["tb1"]
                ip = 9 + r * 3 + (nl - 2)
                units.append(mk_band(1, tbl, ip, ip + 1, _sl(OWN0 + 512 * (nl - 2) + r, 128, 4),
                                     accOD[:, :, _sl(512 * (nl - 2) + r, 128, 4)], False))

        def mk_g2(r0):
            def s1():
                bS, bbS = pf()
                for rr in range(4):
                    r = r0 + rr
                    OP("pe", lambda e: e.matmul(bS[:, rr * 64:(rr + 1) * 64], kvT[:, h, sels[21 + r]], qT[:, 2, _sl(OWN0 + r, 64, 16)],
                                                start=True, stop=True), [kvT.b, qT.b], [bbS], inc=(rr == 3))
                stt = str_.next()
                P = Pr.next()
                OP("dve", lambda e: e.scalar_tensor_tensor(stt.ap, bS[:, 0:256], SCALE, tb["tb2"][:, h, :], ALU.mult, ALU.add),
                   [bbS, tb["tb2"].b], [stt.b])
                act(P.ap, stt.ap, AF.Exp, [stt.b], [P.b])
                return P, 256

            def s2(P):
                b2, bb2 = pf()
                for rr in range(4):
                    OP("pe", lambda e: e.matmul(b2[:, rr * 64:(rr + 1) * 64], Vtok[:, 21 + r0 + rr, :], P[:, rr * 64:(rr + 1) * 64],
                                                start=True, stop=True), [Vtok.b, P.b], [bb2], inc=False)
                OP("pe", lambda e: e.matmul(b2[:, 256:512], onesB.ap, P.ap, start=True, stop=True), [onesB.b, P.b], [bb2])
                for o in range(2):
                    av = accOD[:, o, :].rearrange("p (i r) -> p r i", r=16)[:, r0:r0 + 4, :]
                    OP("dve", lambda e: e.tensor_tensor(av, av, b2[:, o * 256:(o + 1) * 256].rearrange("p (r i) -> p r i", r=4), ALU.add),
                       [bb2, accOD.b], [accOD.b])
            return (s1, s2)
        for r0 in range(0, 16, 4):
            units.append(mk_g2(r0))
        pend = []
        for (s1, s2) in units:
            P, _w = s1()
            pend.append((s2, P))
            if len(pend) > 3:
                f_, P_ = pend.pop(0)
                f_(P_)
        while pend:
            f_, P_ = pend.pop(0)
            f_(P_)
        act(accOD[:, 1, :], accOD[:, 1, :], AF.Ln, [accOD.b], [accOD.b])
        act(accOD[:, 1, :], accOD[:, 1, :], AF.Exp, [accOD.b], [accOD.b], scale=-1.0)
        OP("dve", lambda e: e.tensor_tensor(branchT[:, 8 + h, OWN0:OWN0 + HALF], accOD[:, 0, :], accOD[:, 1, :], ALU.mult),
           [accOD.b], [b_dil])
    M.free(qT, kf, Vtok, accOD, *stg.tns, *srow.tns, *str_.tns, *Pr.tns)
    for nm in ("tb0", "tb0f", "tb1", "tb1f", "tb2"):
        M.free(tb[nm])
    dump("attnT", branchT, ap=branchT[:, 8:12, :], dt=BF16)

    def sample_attend(q_tm, qcols, ngrp, key_rows, val_rows, tbl, out_chunk0, extra):
        ncol = ngrp * 4
        bO, bbO = pf_fixed(4)
        bD, bbD = pf_fixed(5)
        qb = M.alloc("qb", [128, qcols], F32)
        Kr = Ring([M.alloc("Kr%d" % i, [128, 512], F32) for i in range(2)])
        Vr = Ring([M.alloc("Vr%d" % i, [128, 512], F32) for i in range(2)])
        Vb = M.alloc("Vb", [128, ngrp, 512], BF16)
        prod = M.alloc("prod", [128, 512], F32)
        scs = M.alloc("scs", [128, ncol], F32)
        Ps = M.alloc("Ps", [128, ncol], BF16)
        for s in range(4):
            for g in range(qcols // 512):
                bank, bb = pf()
                OP("pe", lambda e: e.matmul(bank[:, 0:512], sel[0:4, s, :], q_tm[0:4, g * 512:(g + 1) * 512], start=True, stop=True),
                   [sel.b, q_tm.b], [bb])
                act(qb[:, g * 512:(g + 1) * 512], bank[:, 0:512], AF.Copy, [bb], [qb.b])
            yield
            for g in range(ngrp):
                K_ = Kr.next()
                S.dma("sp", K_.ap, key_rows(s, g), K_.b, writes=[K_.b])
                V_ = Vr.next()
                S.dma("sp", V_.ap, val_rows(s, g), V_.b, writes=[V_.b])
                qoff = (g * 512) % qcols
                OP("dve", lambda e: e.tensor_tensor(prod.ap, K_.ap, qb[:, qoff:qoff + 512], ALU.mult), [K_.b, qb.b], [prod.b])
                OP("dve", lambda e: e.reduce_sum(scs[:, g * 4:(g + 1) * 4], prod.ap.rearrange("p (h d) -> p h d", h=4), AX.X),
                   [prod.b], [scs.b])
                act(Vb[:, g, :], V_.ap, AF.Copy, [V_.b], [Vb.b])
                yield
            if tbl is not None:
                OP("dve", lambda e: e.scalar_tensor_tensor(scs.ap, scs.ap, SCALE, tbl.ap, ALU.mult, ALU.add), [scs.b, tbl.b], [scs.b])
                act(Ps.ap, scs.ap, AF.Exp, [scs.b], [Ps.b])
            else:
                act(Ps.ap, scs.ap, AF.Exp, [scs.b], [Ps.b], scale=SCALE)
            for h in range(4):
                for g in range(ngrp):
                    OP("pe", lambda e: e.matmul(bO[:, h * 4 + s:h * 4 + s + 1], Vb[:, g, h * 128:(h + 1) * 128], Ps[:, g * 4 + h:g * 4 + h + 1],
                                                start=(g == 0), stop=(g == ngrp - 1)), [Vb.b, Ps.b], [bbO], inc=(g == ngrp - 1))
            OP("pe", lambda e: e.matmul(bD[:, s * ncol:(s + 1) * ncol], onesB.ap, Ps.ap, start=True, stop=True), [onesB.b, Ps.b], [bbD])
            yield
        dsb = M.alloc("dsb", [128, 4 * ncol], F32)
        num = M.alloc("num", [128, 4, 4], F32)
        den = M.alloc("den", [128, 4, 4], F32)
        act(dsb.ap, bD[:, 0:4 * ncol], AF.Copy, [bbD], [dsb.b])
        dv = dsb.ap.rearrange("p (s g h) -> p g h s", s=4, g=ngrp)
        OP("dve", lambda e: e.tensor_tensor(den.ap, dv[:, 0], dv[:, 1], ALU.add), [dsb.b], [den.b])
        for g in range(2, ngrp):
            OP("dve", lambda e: e.tensor_tensor(den.ap, den.ap, dv[:, g], ALU.add), [dsb.b, den.b], [den.b])
        pso = bO[:, 0:16].rearrange("p (h s) -> p h s", h=4)
        if extra is not None:
            e0s, e0b, vs_ = extra
            OP("dve", lambda e: e.tensor_tensor(den.ap, den.ap, e0s, ALU.add), [den.b, e0b], [den.b])
            OP("dve", lambda e: e.tensor_tensor(num.ap, e0s, vs_.ap, ALU.mult), [e0b, vs_.b], [num.b])
            OP("dve", lambda e: e.tensor_tensor(num.ap, num.ap, pso, ALU.add), [num.b, bbO], [num.b])
        else:
            OP("dve", lambda e: e.tensor_copy(num.ap, pso), [bbO], [num.b])
        OP("dve", lambda e: e.reciprocal(den.ap, den.ap), [den.b], [den.b])
        OP("dve", lambda e: e.tensor_tensor(branchT[:, out_chunk0:out_chunk0 + 4, SMP0:SMP0 + 4], num.ap, den.ap, ALU.mult),
           [num.b, den.b], [b_dil if out_chunk0 == 8 else b_xat])
        M.free(qb, Vb, prod, scs, Ps, dsb, num, den, *Kr.tns, *Vr.tns)

    prod0 = M.alloc("prod0", [128, 3, 4, 4], F32)
    E0 = M.alloc("E0", [128, 3, 4, 4], F32)
    E0s = M.alloc("E0s", [128, 4, 4], F32)
    for g in range(3):
        OP("dve", lambda e: e.tensor_tensor(prod0[:, g], qsF[:, g], ksF.ap, ALU.mult), [qsF.b, ksF.b], [prod0.b])
    bank, bb = pf()
    OP("pe", lambda e: e.matmul(bank[:, 0:48], onesF.ap, prod0.ap.rearrange("p g h s -> p (g h s)"), start=True, stop=True),
       [onesF.b, prod0.b], [bb])
    act(E0.ap.rearrange("p g h s -> p (g h s)"), bank[:, 0:48], AF.Exp, [bb], [E0.b], scale=SCALE)
    OP("dve", lambda e: e.tensor_tensor(E0s.ap, E0[:, 0], E0[:, 1], ALU.add), [E0.b], [E0s.b])
    OP("dve", lambda e: e.tensor_tensor(E0s.ap, E0s.ap, E0[:, 2], ALU.add), [E0.b, E0s.b], [E0s.b])

    def ck_rows(src):
        def f(s, g):
            d_ = DIL[g]
            return src[s, _sl(2048 - 128 * d_, 128, d_), :]
        return f
    M.free(kvT)
    pstate["rot"] = [0, 1, 2, 3]
    sgen = sample_attend(qtm, 1536, 3, ck_rows(ck), ck_rows(cv), tbs, 8, (E0s.ap, E0s.b, vsF))

    def tick(k=1):
        for _ in range(k):
            next(sgen, None)

    mkT = M.alloc("mkT", [128, 4, 256], BF16); mkT.b.dj = True
    mvtok = M.alloc("mvtok", [128, 2, 4, 128], BF16); mvtok.b.dj = True
    mkvF = Ring([M.alloc("mkvF%d" % i, [128, 256], F32) for i in range(2)])
    mstg = Ring([M.alloc("mstg%d" % i, [128, 2, 128], F32) for i in range(2)])
    for j in range(8):
        slab = ws.next("memkv")
        mf = mkvF.next()

        def evm(ci, c0, n, bank, bb):
            act(mf.ap, bank[:, 0:256], AF.Copy, [bb], [mf.b])
        proj(slab, 16, lambda k, c0, n: memT[:, k, 0:256], [memT.b], [(0, 256)], evm)
        if j < 4:
            act(mkT[:, j, :], mf.ap, AF.Copy, [mf.b], [mkT.b])
        bank, bb = pf()
        for mt in range(2):
            OP("pe", lambda e: e.transpose(bank[:, mt * 128:(mt + 1) * 128], mf[:, mt * 128:(mt + 1) * 128], identF.ap),
               [mf.b, identF.b], [bb], inc=(mt == 1))
        ms = mstg.next()
        act(ms.ap, bank[:, 0:256].rearrange("p (t d) -> p t d", t=2), AF.Copy, [bb], [ms.b])
        dst = memk if j < 4 else memv
        jj = j % 4
        store(dst[:, jj * 128:(jj + 1) * 128].rearrange("(t p) d -> p t d", p=128), ms.ap, ms.b)
        if j >= 4:
            OP("dve", lambda e: e.tensor_copy(mvtok[:, :, jj, :], ms.ap), [ms.b], [mvtok.b])
        tick(2)
    M.free(memT, *mkvF.tns)
    qxT = M.alloc("qxT", [128, 4, T_EXT], BF16); qxT.b.dj = True
    qxtm = M.alloc("qxtm", [4, 512], F32); qxtm.b.dj = True
    for h in range(4):
        slab = ws.next("qx")

        def evx(ci, c0, n, bank, bb, h=h):
            act(qxT[:, h, c0:c0 + n], bank[:, 0:n], AF.Copy, [bb], [qxT.b])
        proj(slab, 16, lambda k, c0, n: hT[:, k, c0:c0 + n], [hT.b], CH3, evx)
        bank, bb = pf()
        for k in range(16):
            OP("pe", lambda e: e.matmul(bank[0:4, 0:128], hT[:, k, SMP0:SMP0 + 4], slab[:, k, :], start=(k == 0), stop=(k == 15)),
               [hT.b, slab.b], [bb], inc=(k == 15))
        act(qxtm[0:4, h * 128:(h + 1) * 128], bank[0:4, 0:128], AF.Copy, [bb], [qxtm.b])
        tick(2)
    Pm = Ring([M.alloc("Pm%d" % i, [128, 2, 512], BF16) for i in range(2)])
    rD = Ring([M.alloc("rD%d" % i, [128, 512], F32) for i in range(2)])
    for h in range(4):
        for cc in range(2):
            cs = slice(OWN0 + 512 * cc, OWN0 + 512 * (cc + 1))
            P = Pm.next()
            for mt in range(2):
                bS, bbS = pf()
                OP("pe", lambda e: e.matmul(bS[:, 0:512], mkT[:, h, mt * 128:(mt + 1) * 128], qxT[:, h, cs], start=True, stop=True),
                   [mkT.b, qxT.b], [bbS])
                act(P[:, mt, :], bS[:, 0:512], AF.Exp, [bbS], [P.b], scale=SCALE)
            bO, bbO = pf()
            bD, bbD = pf()
            for mt in range(2):
                OP("pe", lambda e: e.matmul(bO[:, 0:512], mvtok[:, mt, h, :], P[:, mt, :], start=(mt == 0), stop=(mt == 1)),
                   [mvtok.b, P.b], [bbO], inc=(mt == 1))
            for mt in range(2):
                OP("pe", lambda e: e.matmul(bD[:, 0:512], onesB.ap, P[:, mt, :], start=(mt == 0), stop=(mt == 1)),
                   [onesB.b, P.b], [bbD], inc=(mt == 1))
            r_ = rD.next()
            act(r_.ap, bD[:, 0:512], AF.Ln, [bbD], [r_.b])
            act(r_.ap, r_.ap, AF.Exp, [r_.b], [r_.b], scale=-1.0)
            OP("dve", lambda e: e.tensor_tensor(branchT[:, 12 + h, cs], bO[:, 0:512], r_.ap, ALU.mult), [bbO, r_.b], [b_xat])
            tick(1)
    for _ in sgen:
        pass
    M.free(prod0, E0, E0s, qsF, ksF, vsF, qtm, tbs)
    M.free(*Pm.tns, *rD.tns, mkT, mvtok, *mstg.tns)
    xgen = sample_attend(qxtm, 512, 2, lambda s, g: cmk[s, g * 128:(g + 1) * 128, :], lambda s, g: cmv[s, g * 128:(g + 1) * 128, :],
                         None, 12, None)
    xstate = {"live": True}

    def xtick(k=1, drain=False):
        if not xstate["live"]:
            return
        for _ in range(10 ** 6 if drain else k):
            try:
                next(xgen)
            except StopIteration:
                xstate["live"] = False
                pstate["rot"] = [0, 1, 2, 3, 4, 5]
                M.free(qxT, qxtm)
                return
    dump("branchT", branchT, dt=BF16)

    mergedT = M.alloc("mergedT", [128, NCH, NT], BF16, top=True); mergedT.b.dj = True
    sgH = Ring([M.alloc("sgH%d" % i, [128, 343], F32) for i in range(6)])
    mH = Ring([M.alloc("mH%d" % i, [128, 343], F32) for i in range(6)])
    tH = Ring([M.alloc("tH%d" % i, [128, 343], F32) for i in range(3)])
    br_off = (0, 8, 12)
    br_nk = (8, 4, 4)
    br_buf = (b_conv, b_dil, b_xat)
    for c in range(16):
        mcur = [mH.next() for _ in range(3)]
        for br in range(3):
            slab_g = ws.next("gate")
            sgs = []

            def evg(ci, c0, n, bank, bb, br=br, c=c):
                sg = sgH.next()
                sgs.append(sg)
                act(sg[:, 0:n], bank[:, 0:n], AF.Sigmoid, [bb, bg.b], [sg.b], bias=bg[:, br * 16 + c:br * 16 + c + 1])
                xtick(2)
            proj(slab_g, 16, lambda k, c0, n: hT[:, k, c0:c0 + n], [hT.b], CH3, evg)
            if br == 2:
                xtick(drain=True)
            slab_w = ws.next("bw")

            def evw(ci, c0, n, bank, bb, br=br, c=c):
                sg = sgs[ci]
                m_ = mcur[ci]
                if br == 0:
                    OP("dve", lambda e: e.tensor_tensor(m_[:, 0:n], bank[:, 0:n], sg[:, 0:n], ALU.mult), [bb, sg.b], [m_.b])
                else:
                    t_ = tH.next()
                    OP("dve", lambda e: e.tensor_tensor(t_[:, 0:n], bank[:, 0:n], sg[:, 0:n], ALU.mult), [bb, sg.b], [t_.b])
                    if br == 1:
                        OP("dve", lambda e: e.tensor_tensor(m_[:, 0:n], m_[:, 0:n], t_[:, 0:n], ALU.add), [m_.b, t_.b], [m_.b])
                    else:
                        OP("dve", lambda e: e.tensor_tensor(mergedT[:, c, c0 - OWN0:c0 - OWN0 + n], m_[:, 0:n], t_[:, 0:n], ALU.add),
                           [m_.b, t_.b], [mergedT.b])
            o_ = br_off[br]
            proj(slab_w, br_nk[br], lambda k, c0, n: branchT[:, o_ + k, c0:c0 + n], [br_buf[br]], CH3, evw)
    M.free(hT, branchT, *sgH.tns, *mH.tns, *tH.tns)
    dump("mergedT", mergedT, dt=BF16)

    zT = M.alloc("zT", [128, NCH, NT], F32, top=True)
    z_bufs = [[zT.buf("z%d_%d" % (c, ci)) for ci in range(3)] for c in range(16)]
    all_z = [b for row in z_bufs for b in row]
    ssb_ap, ssb_buf = pf_fixed(5)

    def ss_matmuls(zsq):
        for t in range(9):
            n = 128 if t < 8 else 4
            for c in range(16):
                OP("pe", lambda e: e.matmul(ssb_ap[0:n, t:t + 1], zsq[:, c, t * 128:t * 128 + n], onesB[:, 0:1],
                                            start=(c == 0), stop=(c == 15)), [zsq.b, onesB.b], [ssb_buf], inc=(c == 15))

    zsq = M.alloc("zsq", [128, NCH, NT], BF16); zsq.b.dj = True
    pstate["rot"] = [0, 1, 2, 3, 4]
    for c in range(16):
        slab = ws.next("wout")

        def evz(ci, c0, n, bank, bb, c=c):
            act(zsq[:, c, c0 - OWN0:c0 - OWN0 + n], bank[:, 0:n], AF.Square, [bb], [zsq.b])
            act(zT[:, c, c0 - OWN0:c0 - OWN0 + n], bank[:, 0:n], AF.Copy, [bb, gpm.b], [z_bufs[c][ci]], scale=gpm[:, c:c + 1])
        proj(slab, 16, lambda k, c0, n: mergedT[:, k, c0 - OWN0:c0 - OWN0 + n], [mergedT.b], CH3, evz)
    zsq.b.dj = False
    ss_matmuls(zsq)
    M.free(mergedT, zsq)
    dump("zT1", zT)
    h2T = M.alloc("h2T", [128, NCH, NT], BF16, top=True); h2T.b.dj = True
    io_alloc(with_z=True)

    def post(res_src, res_is_dram_in, out_dst, g2_d, make_h2):
        if make_h2:
            load(io["gB"], io["gB"].ap, g2_d.partition_broadcast(128))
        rs1 = M.alloc("rs1", [128, 16], F32)
        for (pp, c0_, c1_) in ((128, 0, 8), (4, 8, 9)):
            OP("dve", lambda e: e.tensor_scalar(rs1[0:pp, c0_:c1_], ssb_ap[0:pp, c0_:c1_], 1.0 / D, EPS, ALU.mult, ALU.add),
               [ssb_buf], [rs1.b])
            OP("act", lambda e: e.sqrt(rs1[0:pp, c0_:c1_], rs1[0:pp, c0_:c1_]), [rs1.b], [rs1.b])
            OP("dve", lambda e: e.reciprocal(rs1[0:pp, c0_:c1_], rs1[0:pp, c0_:c1_]), [rs1.b], [rs1.b])
        st8 = {}

        def stL(t):
            n = 128 if t < 8 else 4
            xt = io["x"].next()
            S.dma("act", xt[0:n, :], res_src(t, n), xt.b, reads=([] if res_is_dram_in else [x1s_buf]), writes=[xt.b])
            st8[("x", t)] = xt

        def stP(t):
            n = 128 if t < 8 else 4
            xt = st8.pop(("x", t))
            zt = io["z"].next()
            for hlf in range(2):
                pbufs = [pf_bufs[2 * hlf], pf_bufs[2 * hlf + 1]]
                for c8 in range(8):
                    c = hlf * 8 + c8
                    OP("pe", lambda e: e.transpose(PSF[0:n, hlf * 1024 + c8 * 128:hlf * 1024 + (c8 + 1) * 128],
                                                   zT[:, c, t * 128:t * 128 + n], identF.ap),
                       all_z + [identF.b], [pbufs[c8 // 4]], inc=(c8 % 4 == 3))
                act(zt[0:n, hlf * 1024:(hlf + 1) * 1024], PSF[0:n, hlf * 1024:(hlf + 1) * 1024], AF.Copy, pbufs, [zt.b])
            st8[t] = (xt, zt, n)

        def stQ(t):
            xt, zt, n = st8.pop(t)
            OP("dve", lambda e: e.scalar_tensor_tensor(zt[0:n, :], zt[0:n, :], rs1[0:n, t:t + 1], xt[0:n, :], ALU.mult, ALU.add),
               [zt.b, rs1.b, xt.b], [zt.b])
            store(out_dst(t, n), zt[0:n, :], zt.b, final=not make_h2, dram_buf=(x1s_buf if make_h2 else None))
            if make_h2:
                st8[("n", t)] = norm_a(None, n, xt=zt)

        def stR(t):
            norm_r(st8[("n", t)])

        def stS1(t):
            norm_b1(st8[("n", t)], io["gB"])

        def stS2(t):
            n = 128 if t < 8 else 4
            norm_b2(st8.pop(("n", t)), lambda hlf: h2T[:, 8 * hlf:8 * hlf + 8, t * 128:t * 128 + n], h2T.b)
        pstate["rot"] = [4]
        if make_h2:
            sw_pipeline(9, [stL, stP, stQ, stR, stS1, stS2], order=[0, 3, 2, 1, 4, 5])
        else:
            sw_pipeline(9, [stL, stP, stQ], order=[0, 2, 1])
        pstate["rot"] = [0, 1, 2, 3, 4, 5]
        M.free(rs1)

    def res1(t, n):
        return x_own[t * 128:(t + 1) * 128, :] if t < 8 else x_smp

    def x1_ap(t, n):
        return x1s[t * 128:t * 128 + n, :]
    post(res1, True, x1_ap, g_pre_ffn, True)
    io_free()
    dump("h2T", h2T, dt=BF16)

    x1_bufs = []
    actT = M.alloc("actT", [128, 11, NT], BF16)
    sgF = Ring([M.alloc("sgF%d" % i, [128, 343], F32) for i in range(3)])
    tF = Ring([M.alloc("tF%d" % i, [128, 343], F32) for i in range(3)])
    zsq2 = M.alloc("zsq2", [128, NCH, NT], BF16); zsq2.b.dj = True
    pstate["rot"] = [0, 1, 2, 3, 4]
    for p in range(4):
        actT.b.dj = True
        for f in range(11):
            slab_g = ws.next("fg")
            slab_u = ws.next("fu")
            ts_ = []

            def evfg(ci, c0, n, bank, bb):
                sg = sgF.next()
                t_ = tF.next()
                ts_.append(t_)
                act(sg[:, 0:n], bank[:, 0:n], AF.Sigmoid, [bb], [sg.b])
                OP("dve", lambda e: e.tensor_tensor(t_[:, 0:n], bank[:, 0:n], sg[:, 0:n], ALU.mult), [bb, sg.b], [t_.b])
            proj(slab_g, 16, lambda k, c0, n: h2T[:, k, c0 - OWN0:c0 - OWN0 + n], [h2T.b], CH3, evfg)

            def evfu(ci, c0, n, bank, bb, f=f):
                t_ = ts_[ci]
                OP("dve", lambda e: e.tensor_tensor(actT[:, f, c0 - OWN0:c0 - OWN0 + n], bank[:, 0:n], t_[:, 0:n], ALU.mult),
                   [bb, t_.b], [actT.b])
            proj(slab_u, 16, lambda k, c0, n: h2T[:, k, c0 - OWN0:c0 - OWN0 + n], [h2T.b], CH3, evfu)
        actT.b.dj = False
        for c in range(16):
            slab = ws.next("fd")

            def evd(ci, c0, n, bank, bb, c=c, p=p):
                zb_ = z_bufs[c][ci]
                zv = zT[:, c, c0 - OWN0:c0 - OWN0 + n]
                if p == 0:
                    act(zv, bank[:, 0:n], AF.Copy, [bb], [zb_])
                else:
                    OP("dve", lambda e: e.tensor_tensor(zv, zv, bank[:, 0:n], ALU.add), [bb, zb_], [zb_])
                if p == 3:
                    act(zsq2[:, c, c0 - OWN0:c0 - OWN0 + n], zv, AF.Square, [zb_], [zsq2.b])
                    act(zv, zv, AF.Copy, [zb_, gpf.b], [zb_], scale=gpf[:, c:c + 1])
            proj(slab, 11, lambda k, c0, n: actT[:, k, c0 - OWN0:c0 - OWN0 + n], [actT.b], CH3, evd)
    zsq2.b.dj = False
    ss_matmuls(zsq2)
    M.free(actT, h2T, zsq2, *sgF.tns, *tF.tns)
    io_alloc(with_z=True)
    dump("zT2", zT)

    def y_ap(t, n):
        return y_own[t * 128:(t + 1) * 128, :] if t < 8 else y_smp
    post(lambda t, n: x1s[t * 128:t * 128 + n, :], False, y_ap, None, False)

    S.finish(out_bufs, "sp")
    assert ws.consumed == len(ws.plan), (ws.consumed, len(ws.plan))
    info = dict(n_wait=S.n_wait, n_ins=dict(S.n_ins), nsem=S.nsem, peak=M.peak - SB_BASE)
    return nc, info


def _slopes():
    i = np.arange(1, 13, dtype=np.float32)
    return np.exp2(-8.0 * i / 12.0).astype(np.float32).reshape(3, 4)


def _tables(hf):
    sl = _slopes()
    kp = np.arange(128, dtype=np.float32)[:, None]
    qi = np.arange(128, dtype=np.float32)[None, :]

    def band(g, first):
        d_ = float(DIL[g])
        out = np.empty((128, 4, 256), np.float32)
        for h in range(4):
            dist_p = qi + 128.0 - kp
            prev = np.where(dist_p <= 128.0, -sl[g, h] * dist_p * d_, NEGB)
            if first and hf == 0:
                prev = np.full_like(prev, NEGB)
            dist_c = qi - kp
            cur = np.where(dist_c >= 0.0, -sl[g, h] * dist_c * d_, NEGB)
            out[:, h, 0:128] = prev
            out[:, h, 128:256] = cur
        return np.ascontiguousarray(out.reshape(128, 1024))
    t2 = np.empty((128, 4, 4, 64), np.float32)
    qm = 64.0 + np.arange(64, dtype=np.float32)[None, :]
    dist = qm - kp
    valid = dist >= 0.0
    if hf == 0:
        valid = valid & (kp >= 64.0)
    for h in range(4):
        t2[:, h, :, :] = np.where(valid, -sl[2, h] * 16.0 * dist, NEGB)[:, None, :]
    ts = np.empty((128, 12), np.float32)
    m = np.arange(128, dtype=np.float32)
    for g in range(3):
        for h in range(4):
            ts[:, g * 4 + h] = -sl[g, h] * (128.0 - m) * float(DIL[g])
    return dict(tb0=band(0, False), tb0f=band(0, True), tb1=band(1, False), tb1f=band(1, True),
                tb2=np.ascontiguousarray(t2.reshape(128, 1024)), tbs=ts)


_CACHE = {}


def _fm(v, nch):
    return np.ascontiguousarray(np.asarray(v, np.float32).reshape(nch, 128).T)


def kernel(x_prompt, x_sample, cache_win_k, cache_win_v, state_conv, cache_mem_k, cache_mem_v, mem_prompt,
           g_pre_mix, w_in, b_gate, conv_w, conv_b, conv_ln_g, conv_ln_b, w_conv_out, w_dil_o,
           g_mem, w_mem_kv, w_x_o, w_out, g_post_mix, g_pre_ffn, w_ffn_gate, w_ffn_up, w_ffn_down, g_post_ffn,
           _dbg=()):
    f32 = lambda a: np.ascontiguousarray(np.asarray(a, dtype=np.float32))
    x_prompt = f32(x_prompt); x_sample = f32(x_sample)
    key = tuple(_dbg)
    if key not in _CACHE:
        _CACHE[key] = build_program(dbg=_dbg)
    nc, info = _CACHE[key]
    common = {
        "w_in": f32(w_in[0]), "w_conv_out": f32(w_conv_out[0]), "w_dil_o": f32(w_dil_o[0]), "w_mem_kv": f32(w_mem_kv[0]),
        "w_x_o": f32(w_x_o[0]), "w_out": f32(w_out[0]), "w_gate": f32(w_ffn_gate[0]), "w_up": f32(w_ffn_up[0]),
        "w_down": f32(w_ffn_down[0]),
        "g_pre_mix": f32(g_pre_mix[0:1]), "g_mem": f32(g_mem[0:1]), "g_post_mix": f32(g_post_mix[0:1]),
        "g_pre_ffn": f32(g_pre_ffn[0:1]), "g_post_ffn": f32(g_post_ffn[0:1]),
        "bgT": _fm(b_gate[0], 48), "gpmT": _fm(g_post_mix[0], 16), "gpfT": _fm(g_post_ffn[0], 16),
        "cwT": np.ascontiguousarray(np.asarray(conv_w[0], np.float32).T.reshape(8, 128, 31).transpose(1, 0, 2).reshape(128, 248)),
        "cbT": _fm(conv_b[0], 8), "lngT": _fm(conv_ln_g[0], 8), "lnbT": _fm(conv_ln_b[0], 8),
        "ident": np.eye(128, dtype=np.float32),
        "sel": np.ascontiguousarray(np.repeat(np.eye(4, dtype=np.float32)[:, :, None], 128, axis=2).reshape(4, 512)),
    }
    ck_all = np.asarray(cache_win_k, np.float32)[0].reshape(32, 2048, 512)
    cv_all = np.asarray(cache_win_v, np.float32)[0].reshape(32, 2048, 512)
    cmk_all = np.asarray(cache_mem_k, np.float32)[0].reshape(32, 256, 512)
    cmv_all = np.asarray(cache_mem_v, np.float32)[0].reshape(32, 256, 512)
    sconv_all = np.asarray(state_conv, np.float32)[0]
    mem_all = np.asarray(mem_prompt, np.float32)
    zeros_ctx = np.zeros((HALF, D), np.float32)
    in_maps = []
    for i in range(8):
        b, hf = i // 2, i % 2
        m = dict(common)
        m["x_own"] = f32(x_prompt[b, hf * HALF:(hf + 1) * HALF])
        m["x_ctx"] = f32(x_prompt[b, 0:HALF]) if hf == 1 else zeros_ctx
        m["x_smp"] = f32(x_sample[4 * i:4 * i + 4, 0])
        m["ck"] = f32(ck_all[4 * i:4 * i + 4]); m["cv"] = f32(cv_all[4 * i:4 * i + 4])
        m["sconv"] = f32(sconv_all[4 * i:4 * i + 4])
        m["cmk"] = f32(cmk_all[4 * i:4 * i + 4]); m["cmv"] = f32(cmv_all[4 * i:4 * i + 4])
        m["mem"] = f32(mem_all[b])
        m.update(_tables(hf))
        in_maps.append(m)
    res = run_bass_kernel_spmd(nc, in_maps, core_ids=list(range(8)))
    R = res.results
    kernel.last_results = R
    y_prompt = np.empty((4, SEQ, D), np.float32)
    wk = np.empty((1, 4, SEQ, 4, 128), np.float32)
    wv = np.empty((1, 4, SEQ, 4, 128), np.float32)
    conv_pr = np.empty((1, 4, 30, C_CONV), np.float32)
    mk = np.empty((1, 4, 256, 4, 128), np.float32)
    mv = np.empty((1, 4, 256, 4, 128), np.float32)
    y_sample = np.empty((32, 1, D), np.float32)
    wks = np.empty((1, 32, 1, 4, 128), np.float32)
    wvs = np.empty((1, 32, 1, 4, 128), np.float32)
    conv_sm = np.empty((1, 32, 30, C_CONV), np.float32)
    for i in range(8):
        b, hf = i // 2, i % 2
        r = R[i]
        y_prompt[b, hf * HALF:(hf + 1) * HALF] = r["y_own"]
        wk[0, b, hf * HALF:(hf + 1) * HALF] = r["wk_own"].reshape(HALF, 4, 128)
        wv[0, b, hf * HALF:(hf + 1) * HALF] = r["wv_own"].reshape(HALF, 4, 128)
        if hf == 1:
            conv_pr[0, b] = r["conv_p"]
        else:
            mk[0, b] = r["memk"].reshape(256, 4, 128)
            mv[0, b] = r["memv"].reshape(256, 4, 128)
        y_sample[4 * i:4 * i + 4, 0] = r["y_smp"]
        wks[0, 4 * i:4 * i + 4, 0] = r["wk_s"].reshape(4, 4, 128)
        wvs[0, 4 * i:4 * i + 4, 0] = r["wv_s"].reshape(4, 4, 128)
        conv_sm[0, 4 * i:4 * i + 4] = r["conv_s"]
    return (y_prompt, y_sample, wk, wv, conv_pr, mk, mv, wks, wvs, conv_sm)
```
